# Optimizing a Trainium2 kernel written in Bass

```python
import math
import jax
import jax.numpy as jnp
from jax import lax
import numpy as np

D_MODEL = 1024
BATCH = 32
SEQ = 256
DEPTH = 4
DEC_BATCH = 4
DEC_SEQ = 1024
PAST_LEN = 256

GRID_W = 64
N_EVEN = (DEPTH + 1) // 2
N_ODD = DEPTH // 2
H_A = 8
DK_A = D_MODEL // 16
DV_A = D_MODEL // 16
CHUNK = 64
CONV_W = 4
H_B = 8
KV_B = 2
G_B = H_B // KV_B
HD_B = D_MODEL // 16
WINDOW = 128
QBLK = 128
ROPE_AXIS = HD_B // 2
ROPE_BASE = 10000.0
D_RNN = D_MODEL
LRU_BLOCKS = 16
LRU_BW = D_RNN // LRU_BLOCKS
RG_C = 8.0
D_FF = 4 * D_MODEL
CONV_CH = 2 * H_A * DK_A + H_A * DV_A
MIX_W = H_A * DV_A + H_B * HD_B
IN_AB = CONV_CH + H_A * DV_A + 4 * H_A + (H_B + 2 * KV_B) * HD_B
EPS = 1e-6
NEG_INF = -1e30

kernel_name = 'hybrid_diffusion_step_deltanet_swa_rglru'

f32 = jnp.float32


def rms_norm(x, g):
    xf = x.astype(f32)
    y = xf * lax.rsqrt(jnp.mean(xf * xf, axis=-1, keepdims=True) + EPS)
    return (y * g.astype(f32)).astype(x.dtype)


def l2_norm(x):
    xf = x.astype(f32)
    return xf * lax.rsqrt(jnp.sum(xf * xf, axis=-1, keepdims=True) + EPS)


def adaln(cond, w, b):
    mod = jax.nn.silu(cond) @ w + b
    return jnp.split(mod[..., None, :], 6, axis=-1)


def modulate(x, g, shift, scale):
    return (rms_norm(x, g) * (1.0 + scale) + shift).astype(x.dtype)


def centred_depthwise_conv(x, w, b):
    left = CONV_W // 2
    right = CONV_W - 1 - left
    y = lax.conv_general_dilated(x, w[:, None, :].astype(x.dtype), window_strides=(1,),
                                 padding=[(left, right)], dimension_numbers=('NWC', 'WIO', 'NWC'),
                                 feature_group_count=x.shape[-1])
    return y + b.astype(x.dtype)


def axial_angles(n_tokens):
    rows = n_tokens // GRID_W
    r, cc = jnp.meshgrid(jnp.arange(rows, dtype=f32), jnp.arange(GRID_W, dtype=f32), indexing='ij')
    inv = ROPE_BASE ** (-jnp.arange(0, ROPE_AXIS, 2, dtype=f32) / ROPE_AXIS)
    return r.reshape(-1)[:, None] * inv, cc.reshape(-1)[:, None] * inv


def _rotate(x, ang):
    x1, x2 = jnp.split(x, 2, axis=-1)
    cos = jnp.cos(ang)[None, :, None, :]
    sin = jnp.sin(ang)[None, :, None, :]
    return jnp.concatenate([x1 * cos - x2 * sin, x2 * cos + x1 * sin], axis=-1)


def apply_axial_rope(x, ang_r, ang_c):
    xf = x.astype(f32)
    y = jnp.concatenate([_rotate(xf[..., :ROPE_AXIS], ang_r), _rotate(xf[..., ROPE_AXIS:], ang_c)], axis=-1)
    return y.astype(x.dtype)


def chunk_gated_delta(q, k, v, g, beta, s0):
    B, T, H, dk = q.shape
    dv = v.shape[-1]
    n = T // CHUNK

    def to_chunks(x):
        x = x.astype(f32).reshape((B, n, CHUNK, H) + x.shape[3:])
        return jnp.moveaxis(x, 3, 1)

    q = to_chunks(q) * (dk ** -0.5)
    k = to_chunks(k)
    v = to_chunks(v)
    beta = to_chunks(beta)
    gc = jnp.cumsum(to_chunks(g), axis=-1)
    causal = jnp.tril(jnp.ones((CHUNK, CHUNK), dtype=bool))
    strict = causal & ~jnp.eye(CHUNK, dtype=bool)
    diff = gc[..., :, None] - gc[..., None, :]
    decay = jnp.where(causal, jnp.exp(jnp.where(causal, diff, 0.0)), 0.0)
    k_beta = k * beta[..., None]
    v_beta = v * beta[..., None]
    a_low = jnp.where(strict, jnp.einsum('bhncd,bhnsd->bhncs', k_beta, k) * decay, 0.0)
    eye = jnp.eye(CHUNK, dtype=f32)
    t_inv = lax.linalg.triangular_solve(a_low + eye, jnp.broadcast_to(eye, a_low.shape),
                                        left_side=True, lower=True)
    u = t_inv @ v_beta
    w = t_inv @ (k_beta * jnp.exp(gc)[..., None])
    intra = jnp.where(causal, jnp.einsum('bhncd,bhnsd->bhncs', q, k) * decay, 0.0)
    xs = tuple(jnp.moveaxis(t, 2, 0) for t in (q, k, u, w, gc, intra))

    def step(S, inp):
        q_i, k_i, u_i, w_i, g_i, a_i = inp
        v_new = u_i - w_i @ S
        o_i = (q_i * jnp.exp(g_i)[..., None]) @ S + a_i @ v_new
        g_last = g_i[..., -1:]
        S = S * jnp.exp(g_last)[..., None] + jnp.einsum(
            'bhcd,bhce->bhde', k_i * jnp.exp(g_last - g_i)[..., None], v_new)
        return S, o_i

    s_fin, o = lax.scan(step, s0.astype(f32), xs)
    o = jnp.transpose(o, (1, 0, 3, 2, 4)).reshape(B, T, H, dv)
    return o, s_fin


def bidir_delta(q, k, v, g2, beta2, s0):
    out = 0.0
    finals = []
    for d in range(2):
        args = (q, k, v, g2[:, :, d], beta2[:, :, d])
        if d == 1:
            args = tuple(jnp.flip(a, axis=1) for a in args)
        o, s = chunk_gated_delta(*args, s0[:, d])
        if d == 1:
            o = jnp.flip(o, axis=1)
        out = out + o
        finals.append(s)
    return out, jnp.stack(finals, axis=1)


def _sink_softmax(scores, sink):
    s = jnp.broadcast_to(sink.astype(f32).reshape(1, KV_B, G_B, 1, 1), scores.shape[:-1] + (1,))
    p = jax.nn.softmax(jnp.concatenate([scores, s], axis=-1), axis=-1)
    return p[..., :-1]


def context_attention(q, k, v, sink):
    B, S, H, hd = q.shape
    nb = S // QBLK
    qb = jnp.moveaxis(q.astype(f32).reshape(B, nb, QBLK, KV_B, G_B, hd), 1, 0)
    kf = k.astype(f32)
    vf = v.astype(f32)
    scale = hd ** -0.5

    def block(q_blk):
        sc = jnp.einsum('bqkgd,bskd->bkgqs', q_blk, kf) * scale
        p = _sink_softmax(sc, sink)
        return jnp.einsum('bkgqs,bskd->bqkgd', p, vf)

    o = lax.map(block, qb)
    return jnp.moveaxis(o, 0, 1).reshape(B, S, H * hd)


def latent_window_attention(q, k, v, k_ctx, v_ctx, sink):
    B, L, H, hd = q.shape
    nb = L // QBLK
    span = QBLK + 2 * WINDOW
    qg = q.astype(f32).reshape(B, L, KV_B, G_B, hd)
    pad = ((0, 0), (WINDOW, WINDOW), (0, 0), (0, 0))
    kp = jnp.pad(k.astype(f32), pad)
    vp = jnp.pad(v.astype(f32), pad)
    kc = k_ctx.astype(f32)
    vc = v_ctx.astype(f32)
    scale = hd ** -0.5

    def block(i):
        start = i * QBLK
        q_blk = lax.dynamic_slice_in_dim(qg, start, QBLK, axis=1)
        k_win = lax.dynamic_slice_in_dim(kp, start, span, axis=1)
        v_win = lax.dynamic_slice_in_dim(vp, start, span, axis=1)
        q_pos = start + jnp.arange(QBLK)
        k_pos = start - WINDOW + jnp.arange(span)
        valid = ((jnp.abs(q_pos[:, None] - k_pos[None, :]) <= WINDOW)
                 & (k_pos[None, :] >= 0) & (k_pos[None, :] < L))
        s_win = jnp.where(valid, jnp.einsum('bqkgd,bskd->bkgqs', q_blk, k_win) * scale, NEG_INF)
        s_ctx = jnp.einsum('bqkgd,bskd->bkgqs', q_blk, kc) * scale
        p = _sink_softmax(jnp.concatenate([s_win, s_ctx], axis=-1), sink)
        return (jnp.einsum('bkgqs,bskd->bqkgd', p[..., :span], v_win)
                + jnp.einsum('bkgqs,bskd->bqkgd', p[..., span:], vc))

    o = lax.map(block, jnp.arange(nb))
    return jnp.moveaxis(o, 0, 1).reshape(B, L, H * hd)


def mixer_ab(h, w_in, conv_w, conv_b, a_log, dt_bias, dn_norm_g, q_norm_g, k_norm_g, sink, w_out, ctx=None):
    B, T, _ = h.shape
    s1 = CONV_CH
    s2 = s1 + H_A * DV_A
    s3 = s2 + 2 * H_A
    s4 = s3 + 2 * H_A
    s5 = s4 + H_B * HD_B
    s6 = s5 + KV_B * HD_B
    qkv_a, gate_a, alpha, beta_l, q_b, k_b, v_b = jnp.split(h @ w_in, [s1, s2, s3, s4, s5, s6], axis=-1)
    qkv_a = jax.nn.silu(centred_depthwise_conv(qkv_a, conv_w, conv_b))
    q_a, k_a, v_a = jnp.split(qkv_a, [H_A * DK_A, 2 * H_A * DK_A], axis=-1)
    q_a = l2_norm(q_a.reshape(B, T, H_A, DK_A))
    k_a = l2_norm(k_a.reshape(B, T, H_A, DK_A))
    v_a = v_a.reshape(B, T, H_A, DV_A)
    alpha = alpha.astype(f32).reshape(B, T, 2, H_A)
    beta = jax.nn.sigmoid(beta_l.astype(f32).reshape(B, T, 2, H_A))
    g = -jnp.exp(a_log.astype(f32)) * jax.nn.softplus(alpha + dt_bias.astype(f32))
    s0 = jnp.zeros((B, 2, H_A, DK_A, DV_A), f32) if ctx is None else ctx[0]
    o_a, s_fin = bidir_delta(q_a, k_a, v_a, g, beta, s0)
    o_a = rms_norm(o_a, dn_norm_g) * jax.nn.silu(gate_a.astype(f32)).reshape(B, T, H_A, DV_A)
    o_a = o_a.reshape(B, T, H_A * DV_A)
    q_b = rms_norm(q_b.reshape(B, T, H_B, HD_B), q_norm_g)
    k_b = rms_norm(k_b.reshape(B, T, KV_B, HD_B), k_norm_g)
    v_b = v_b.reshape(B, T, KV_B, HD_B)
    if ctx is None:
        o_b = context_attention(q_b, k_b, v_b, sink)
        new = (s_fin, k_b, v_b)
    else:
        ang_r, ang_c = axial_angles(T)
        o_b = latent_window_attention(apply_axial_rope(q_b, ang_r, ang_c), apply_axial_rope(k_b, ang_r, ang_c),
                                      v_b, ctx[1], ctx[2], sink)
        new = None
    out = jnp.concatenate([o_a.astype(h.dtype), o_b.astype(h.dtype)], axis=-1) @ w_out
    return out, new


def linear_scan(a, b, h0):
    b = b.at[:, 0].add(a[:, 0] * h0)

    def comb(l, r):
        return (l[0] * r[0], r[0] * l[1] + r[1])

    _, hs = lax.associative_scan(comb, (a, b), axis=1)
    return hs, hs[:, -1]


def mixer_c(h, w_in, conv_w, conv_b, w_a, b_a, w_x, b_x, lam, w_out, h0=None):
    B, T, _ = h.shape
    x, gate = jnp.split(h @ w_in, 2, axis=-1)
    x = centred_depthwise_conv(x, conv_w, conv_b).astype(f32)
    xb = x.reshape(B, T, LRU_BLOCKS, LRU_BW)
    if h0 is None:
        h0 = jnp.zeros((B, 2, D_RNN), f32)
    acc = 0.0
    finals = []
    for d in range(2):
        r = jax.nn.sigmoid(jnp.einsum('btnc,ncd->btnd', xb, w_a[d].astype(f32)).reshape(B, T, D_RNN)
                           + b_a[d].astype(f32))
        i = jax.nn.sigmoid(jnp.einsum('btnc,ncd->btnd', xb, w_x[d].astype(f32)).reshape(B, T, D_RNN)
                           + b_x[d].astype(f32))
        log_a = -RG_C * r * jax.nn.softplus(-lam[d].astype(f32))
        a = jnp.exp(log_a)
        u = jnp.sqrt(-jnp.expm1(2.0 * log_a)) * (i * x)
        if d == 1:
            a, u = jnp.flip(a, axis=1), jnp.flip(u, axis=1)
        hs, h_last = linear_scan(a, u, h0[:, d].astype(f32))
        if d == 1:
            hs = jnp.flip(hs, axis=1)
        acc = acc + hs
        finals.append(h_last)
    y = acc * jax.nn.gelu(gate.astype(f32))
    return y.astype(h.dtype) @ w_out, jnp.stack(finals, axis=1)


def sq_relu_mlp(h, w1, w2):
    return jnp.square(jax.nn.relu(h @ w1)) @ w2


def setup_inputs(seed: int = 0) -> dict:
    key = jax.random.key(seed)
    ks = iter(jax.random.split(key, 48))

    def nrm(shape, scale):
        return jax.random.normal(next(ks), shape, f32) * scale

    def unif(shape, lo, hi):
        return jax.random.uniform(next(ks), shape, f32, lo, hi)

    dt = jnp.exp(unif((N_EVEN, 2, H_A), math.log(1e-3), math.log(1e-1)))
    a0 = unif((N_ODD, 2, D_RNN), 0.9, 0.999) ** (1.0 / RG_C)
    return {
        'x_prompt': nrm((BATCH, SEQ, D_MODEL), 1.0),
        'x_sample': nrm((DEC_BATCH, DEC_SEQ, D_MODEL), 1.0),
        'state_delta': nrm((DEC_BATCH, N_EVEN, 2, H_A, DK_A, DV_A), 0.5),
        'cache_k': nrm((DEC_BATCH, N_EVEN, PAST_LEN, KV_B, HD_B), 1.0),
        'cache_v': nrm((DEC_BATCH, N_EVEN, PAST_LEN, KV_B, HD_B), 1.0),
        'state_lru': nrm((DEC_BATCH, N_ODD, 2, D_RNN), 0.5),
        'c': nrm((DEC_BATCH, D_MODEL), 1.0),
        'c_ctx': nrm((D_MODEL,), 1.0),
        'ada_w': nrm((DEPTH, D_MODEL, 6 * D_MODEL), 0.5 * D_MODEL ** -0.5),
        'ada_b': nrm((DEPTH, 6 * D_MODEL), 0.01),
        'norm1_g': 1.0 + nrm((DEPTH, D_MODEL), 0.01),
        'norm2_g': 1.0 + nrm((DEPTH, D_MODEL), 0.01),
        'ff_w1': nrm((DEPTH, D_MODEL, D_FF), D_MODEL ** -0.5),
        'ff_w2': nrm((DEPTH, D_FF, D_MODEL), D_FF ** -0.5),
        'ab_w_in': nrm((N_EVEN, D_MODEL, IN_AB), D_MODEL ** -0.5),
        'ab_conv_w': nrm((N_EVEN, CONV_W, CONV_CH), CONV_W ** -0.5),
        'ab_conv_b': nrm((N_EVEN, CONV_CH), 0.01),
        'dn_a_log': jnp.log(unif((N_EVEN, 2, H_A), 1.0, 16.0)),
        'dn_dt_bias': dt + jnp.log(-jnp.expm1(-dt)),
        'dn_norm_g': 1.0 + nrm((N_EVEN, DV_A), 0.01),
        'attn_q_norm_g': 1.0 + nrm((N_EVEN, HD_B), 0.01),
        'attn_k_norm_g': 1.0 + nrm((N_EVEN, HD_B), 0.01),
        'attn_sink': nrm((N_EVEN, H_B), 0.5),
        'ab_w_out': nrm((N_EVEN, MIX_W, D_MODEL), MIX_W ** -0.5),
        'c_w_in': nrm((N_ODD, D_MODEL, 2 * D_RNN), D_MODEL ** -0.5),
        'c_conv_w': nrm((N_ODD, CONV_W, D_RNN), CONV_W ** -0.5),
        'c_conv_b': nrm((N_ODD, D_RNN), 0.01),
        'lru_w_a': nrm((N_ODD, 2, LRU_BLOCKS, LRU_BW, LRU_BW), LRU_BW ** -0.5),
        'lru_b_a': nrm((N_ODD, 2, D_RNN), 0.01),
        'lru_w_x': nrm((N_ODD, 2, LRU_BLOCKS, LRU_BW, LRU_BW), LRU_BW ** -0.5),
        'lru_b_x': nrm((N_ODD, 2, D_RNN), 0.01),
        'lru_lambda': jnp.log(a0) - jnp.log1p(-a0),
        'c_w_out': nrm((N_ODD, D_RNN, D_MODEL), D_RNN ** -0.5),
    }


def reference(x_prompt, x_sample, state_delta, cache_k, cache_v, state_lru, c, c_ctx,
              ada_w, ada_b, norm1_g, norm2_g, ff_w1, ff_w2,
              ab_w_in, ab_conv_w, ab_conv_b, dn_a_log, dn_dt_bias, dn_norm_g,
              attn_q_norm_g, attn_k_norm_g, attn_sink, ab_w_out,
              c_w_in, c_conv_w, c_conv_b, lru_w_a, lru_b_a, lru_w_x, lru_b_x, lru_lambda, c_w_out):
    yp = x_prompt
    ys = x_sample
    new_dn, new_k, new_v, new_lru = [], [], [], []
    for l in range(DEPTH):
        j = l // 2
        mp = adaln(c_ctx, ada_w[l], ada_b[l])
        ms = adaln(c, ada_w[l], ada_b[l])
        hp = modulate(yp, norm1_g[l], mp[0], mp[1])
        hs = modulate(ys, norm1_g[l], ms[0], ms[1])
        if l % 2 == 0:
            ab = (ab_w_in[j], ab_conv_w[j], ab_conv_b[j], dn_a_log[j], dn_dt_bias[j], dn_norm_g[j],
                  attn_q_norm_g[j], attn_k_norm_g[j], attn_sink[j], ab_w_out[j])
            op, (s_dn, k_ctx, v_ctx) = mixer_ab(hp, *ab)
            os_, _ = mixer_ab(hs, *ab, ctx=(state_delta[:, j], cache_k[:, j], cache_v[:, j]))
            new_dn.append(s_dn)
            new_k.append(k_ctx)
            new_v.append(v_ctx)
        else:
            cp = (c_w_in[j], c_conv_w[j], c_conv_b[j], lru_w_a[j], lru_b_a[j], lru_w_x[j], lru_b_x[j],
                  lru_lambda[j], c_w_out[j])
            op, s_lru = mixer_c(hp, *cp)
            os_, _ = mixer_c(hs, *cp, h0=state_lru[:, j])
            new_lru.append(s_lru)
        yp = yp + (mp[2] * op).astype(yp.dtype)
        ys = ys + (ms[2] * os_).astype(ys.dtype)
        yp = yp + (mp[5] * sq_relu_mlp(modulate(yp, norm2_g[l], mp[3], mp[4]), ff_w1[l], ff_w2[l])).astype(yp.dtype)
        ys = ys + (ms[5] * sq_relu_mlp(modulate(ys, norm2_g[l], ms[3], ms[4]), ff_w1[l], ff_w2[l])).astype(ys.dtype)
    return (yp, ys, jnp.stack(new_dn, axis=1), jnp.stack(new_k, axis=1), jnp.stack(new_v, axis=1), jnp.stack(new_lru, axis=1))
```

```python
import os
import math
import numpy as np
from contextlib import ExitStack
import concourse.bass as bass
import concourse.mybir as mybir
from concourse.bass_utils import run_bass_kernel_spmd

F32 = mybir.dt.float32
BF16 = mybir.dt.bfloat16
F32R = mybir.dt.float32r
USE_F32R = os.environ.get("K_F32R", "0") == "1"
AF = mybir.ActivationFunctionType
ALU = mybir.AluOpType

D = 1024
NT = 1024
TT = 512
DEPTH = int(os.environ.get("K_DEPTH", "4"))
DO_A = os.environ.get("K_A", "1") == "1"
DO_B = os.environ.get("K_B", "1") == "1"
DO_C = os.environ.get("K_C", "1") == "1"
DO_MLP = os.environ.get("K_MLP", "1") == "1"
GROUPS = [int(x) for x in os.environ.get("K_GROUPS", "01")]
EPS = 1e-6
DN_STAGE = int(os.environ.get("K_DN_STAGE", "9"))
DN_CUT = float(os.environ.get("K_DN_CUT", "9"))
NH = 4
WH = NH * 64
EPOCH = 2000


class Sched:
    ENG = ("pe", "act", "dve", "pool", "sp")

    def __init__(self, nc, es, n_dma_sems=4):
        self.nc = nc
        self.es = es
        self.ops = {e: [] for e in self.ENG}
        self.cnt = {e: 0 for e in self.ENG}
        self.sems = {}
        self.dma_sems = {}
        self.dma_cnt = {}
        self.dma_rr = {}
        self.n_dma_sems = n_dma_sems
        for q in ("sp", "pool"):
            for s in range(n_dma_sems):
                self.dma_sems[(q, s)] = es.enter_context(nc.semaphore("d_%s%d" % (q, s)))
                self.dma_cnt[(q, s)] = 0
            self.dma_rr[q] = 0
        self.waited = {e: {} for e in self.ENG}
        self.last_w = {}
        self.readers = {}

    def _deps(self, eng, reads, writes):
        need = {}
        def add(ev):
            sk, v = ev
            if need.get(sk, 0) < v:
                need[sk] = v
        for k in reads:
            if k in self.last_w:
                add(self.last_w[k])
        for k in writes:
            if k in self.last_w:
                add(self.last_w[k])
            for ev in self.readers.get(k, ()):
                add(ev)
        waits = []
        w = self.waited[eng]
        for sk, v in need.items():
            if w.get(sk, 0) < v:
                w[sk] = v
                waits.append((sk, v))
        return waits

    def _commit(self, ev, reads, writes):
        for k in reads:
            r = self.readers.setdefault(k, [])
            r.append(ev)
            if len(r) > 24:
                best = {}
                for sk, v in r:
                    if best.get(sk, 0) < v:
                        best[sk] = v
                self.readers[k] = list(best.items())
        for k in writes:
            self.last_w[k] = ev
            self.readers[k] = []

    def op(self, eng, fn, reads=(), writes=()):
        psr = [k for k in reads if isinstance(k, str) and k.startswith("ps")]
        if psr:
            writes = list(writes) + [k for k in psr if k not in writes]
        waits = self._deps(eng, reads, writes)
        self.cnt[eng] += 1
        ev = (eng, self.cnt[eng])
        self.ops[eng].append((waits, fn, ("c", eng, self.cnt[eng])))
        self._commit(ev, reads, writes)

    def dma(self, q, out, in_, reads=(), writes=()):
        s = self.dma_rr[q]
        self.dma_rr[q] = (s + 1) % self.n_dma_sems
        sk = ("dma", q, s)
        waits = self._deps(q, reads, writes)
        prev = self.dma_cnt[(q, s)]
        if prev > 0 and self.waited[q].get(sk, 0) < prev:
            self.waited[q][sk] = prev
            waits.append((sk, prev))
        self.dma_cnt[(q, s)] = prev + 16
        ev = (sk, prev + 16)
        self.ops[q].append((waits, (lambda e: e.dma_start(out=out, in_=in_)), ("d", (q, s))))
        self._commit(ev, reads, writes)

    def barrier(self):
        for e in self.ENG:
            waits = []
            lazy = (e == "pool")
            for o in self.ENG:
                if o != "sp" and self.cnt[o] > 0 and self.waited[e].get(o, 0) < self.cnt[o]:
                    if not lazy:
                        self.waited[e][o] = self.cnt[o]
                    waits.append((o, self.cnt[o]))
            for (q, s), c in self.dma_cnt.items():
                sk = ("dma", q, s)
                if c > 0 and self.waited[e].get(sk, 0) < c:
                    if not lazy:
                        self.waited[e][sk] = c
                    waits.append((sk, c))
            if waits:
                self.ops[e].append((waits, None, ("bar",) if lazy else None))

    def _get_sem(self, eng, ep):
        k = (eng, ep)
        if k not in self.sems:
            self.sems[k] = self.es.enter_context(self.nc.semaphore("s_%s%d" % (eng, ep)))
        return self.sems[k]

    def _wait(self, e, sk, v):
        if isinstance(sk, tuple):
            e.wait_ge(self.dma_sems[(sk[1], sk[2])], v)
        else:
            ep, r = divmod(v - 1, EPOCH)
            e.wait_ge(self._get_sem(sk, ep), r + 1)

    def emit(self, block):
        for eng in self.ENG:
            for ep in range((self.cnt[eng] + EPOCH - 1) // EPOCH + 1):
                if eng != "sp":
                    self._get_sem(eng, ep)
        pl = self.ops["pool"]
        keep = []
        for i, (waits, fn, kind) in enumerate(pl):
            if kind == ("bar",):
                has_c = False
                for (w2, f2, k2) in pl[i + 1:]:
                    if k2 == ("bar",):
                        break
                    if k2 is not None and k2[0] == "c":
                        has_c = True
                        break
                if has_c:
                    keep.append((waits, None, None))
            else:
                keep.append((waits, fn, kind))
        self.ops["pool"] = keep
        def mk(ename):
            def body(e):
                for waits, fn, kind in self.ops[ename]:
                    for sk, v in waits:
                        self._wait(e, sk, v)
                    if fn is None:
                        continue
                    ins = fn(e)
                    if kind[0] == "c":
                        ep = (kind[2] - 1) // EPOCH
                        ins.then_inc(self._get_sem(kind[1], ep), 1)
                    else:
                        ins.then_inc(self.dma_sems[kind[1]], 16)
            return body
        block.tensor(mk("pe"))
        block.scalar(mk("act"))
        block.vector(mk("dve"))
        block.gpsimd(mk("pool"))
        block.sync(mk("sp"))


def build_program():
    nc = bass.Bass("TRN2", target_bir_lowering=False)
    NE = (DEPTH + 1) // 2
    NO = max(1, DEPTH // 2)
    NL = DEPTH
    def din(name, shape):
        return nc.dram_tensor(name, list(shape), F32, kind="ExternalInput").ap()
    def dout(name, shape):
        return nc.dram_tensor(name, list(shape), F32, kind="ExternalOutput").ap()
    xT = din("xT", [2, D, NT])
    cond = din("cond", [2, 128, 8])
    sdelta = din("sdelta", [NE, 2, 2, 128, 128])
    kcT = din("kcT", [NE, 64, 2, 256])
    vc = din("vc", [NE, 128, 2, 128])
    slru = din("slru", [NO, 2, 128, 8])
    ada_w = din("ada_w", [NL, 12, 128, 8 * 512])
    ada_b = din("ada_b", [NL, 128, 48])
    ng = din("ng", [NL, 128, 16])
    w1 = din("w1", [NL, 8, 128, 8 * 512])
    w2 = din("w2", [NL, 8, 128, 32 * 128])
    abin = din("abin", [NE, 6, 128, 8 * 512])
    about = din("about", [NE, 2, 128, 8 * 512])
    abp64 = din("abp64", [NE, 64, 24 * 5 + 3 + 8])
    abp16 = din("abp16", [NE, 16, 2])
    abp128 = din("abp128", [NE, 128, 61])
    cd = din("cd", [128, 384])
    csel = din("csel", [128, 520])
    cin = din("cin", [NO, 4, 128, 8 * 512])
    cout = din("cout", [NO, 2, 128, 8 * 512])
    cbd = din("cbd", [NO, 4, 128, 8 * 128])
    cp = din("cp", [NO, 128, 8 * 11])
    c64 = din("c64", [64, 832])
    cossin = din("cossin", [64, 2048])
    c16 = din("c16", [16, 1040])
    cmask = din("cmask", [16, 2048])
    c128 = din("c128", [128, 256])
    yT = dout("yT", [2, D, NT])
    o_dn = dout("o_dn", [4, NE, 2, 2, 128, 128])
    o_k = dout("o_k", [NE, 64, 2, NT])
    o_v = dout("o_v", [NE, 128, 8, 128])
    o_lru = dout("o_lru", [NO, 2, 4, 128, 8])

    with ExitStack() as es:
        S = Sched(nc, es)
        uid = [0]
        def sb(shape, dt=F32, ctx=None):
            uid[0] += 1
            return (ctx or es).enter_context(nc.sbuf_tensor("t%d" % uid[0], list(shape), dt))
        def ACT(out, in_, func, reads, writes, **kw):
            S.op("act", lambda e: e.activation(out=out, in_=in_, func=func, **kw), reads, writes)
        def TT_(eng, out, in0, in1, op, reads, writes):
            S.op(eng, lambda e: e.tensor_tensor(out=out, in0=in0, in1=in1, op=op), reads, writes)
        def TS(eng, out, in0, s1, s2, op0, op1, reads, writes):
            S.op(eng, lambda e: e.tensor_scalar(out=out, in0=in0, scalar1=s1, scalar2=s2, op0=op0, op1=op1), reads, writes)
        def STT(out, in0, scalar, in1, op0, op1, reads, writes):
            S.op("dve", lambda e: e.scalar_tensor_tensor(out=out, in0=in0, scalar=scalar, in1=in1, op0=op0, op1=op1), reads, writes)
        def CP(eng, out, in_, reads, writes):
            if eng == "act":
                ACT(out, in_, AF.Copy, reads, writes)
            else:
                S.op(eng, lambda e: e.tensor_copy(out=out, in_=in_), reads, writes)
        def RECIP(out, in_, reads, writes):
            S.op("dve", lambda e: e.reciprocal(out=out, in_=in_), reads, writes)
        def MM(out, pairs, reads, writes):
            pairs = list(pairs)
            def fn(e):
                ins = None
                n = len(pairs)
                for i, (l, r) in enumerate(pairs):
                    ins = e.matmul(out, lhsT=l, rhs=r, start=(i == 0), stop=(i == n - 1))
                return ins
            S.op("pe", fn, reads, writes)
        def MMS(items, reads, writes):
            items = list(items)
            def fn(e):
                ins = None
                for (o, l, r) in items:
                    ins = e.matmul(o, lhsT=l, rhs=r, start=True, stop=True)
                return ins
            S.op("pe", fn, reads, writes)
        def MEMSET(eng, out, val, writes):
            S.op(eng, lambda e: e.memset(out, val), (), writes)

        PSB = [es.enter_context(nc.psum_tensor("ps%d" % i, [128, 512], F32)) for i in range(8)]
        prr = [0]
        def newps():
            i = prr[0]
            prr[0] = (i + 1) % 8
            return PSB[i], "ps%d" % i

        Y = sb([128, 8, NT])
        HT = sb([128, 8, NT], BF16)
        OBt = sb([128, 8, NT], BF16)
        OB128 = OBt[:]
        def OBH(h, sl):
            return OBt[(h % 2) * 64:(h % 2) * 64 + 64, h // 2, sl]
        NSLOT = 3
        WR = [sb([128, 4096], BF16) for _ in range(NSLOT)]
        wrr = [0]
        C64 = sb([64, 832])
        C16 = sb([16, 1040])
        C128 = sb([128, 256])
        MODALL = sb([128, 4, 2, 48])
        SC = sb([128, 2, 8], BF16)
        CONDT = sb([128, 2, 8])
        ONESB = sb([128, 128], BF16)
        IDB = sb([64, 64], BF16)
        NG = sb([128, 4, 16])
        ADAB = sb([128, 4, 48])
        LP = sb([128, 64])
        IDENT4 = C64[:, 0:256]
        NEG_LO = C64[:, 256:512]
        NEG_UP = C64[:, 512:768]
        PERMT = C64[:, 768:832]
        def BLK(d, hh):
            o = (d * 2 + hh) * 256
            return C16[:, o:o + 256]
        def SEL(d, hh):
            o = 1024 + (d * 2 + hh) * 4
            return C16[:, o:o + 4]
        MASK_LO = C128[:, 0:128]
        MASK_UP = C128[:, 128:256]

        S.dma("sp", C64[:], c64[:, :], writes=["c64"])
        S.dma("sp", C16[:], c16[:, :], writes=["c16"])
        S.dma("sp", C128[:], c128[:, :], writes=["c128"])
        for g in range(2):
            S.dma("sp", CONDT[:, g, :], cond[g], writes=["condt"])
        for l in range(NL):
            S.dma("sp", NG[:, l, :], ng[l], writes=["ng"])
            S.dma("sp", ADAB[:, l, :], ada_b[l], writes=["adab"])
        ACT(SC[:], CONDT[:], AF.Silu, ["condt"], ["sc"])
        MEMSET("dve", ONESB[:], 1.0, ["onesb"])
        CP("dve", IDB[:], C64[:, 0:64], ["c64"], ["idb"])
        ONES16 = sb([128, 64])
        NEGONES16 = sb([128, 64])
        CD = sb([128, 384])
        CS = sb([128, 520])
        S.dma("sp", CD[:], cd[:, :], writes=["cd"])
        S.dma("sp", CS[:], csel[:, :], writes=["cs"])
        IDENT2 = CD[:, 0:128]
        NEG_LO2 = CD[:, 128:256]
        NEG_UP2 = CD[:, 256:384]
        def BLK2(d, hh):
            o = (d * 2 + hh) * 128
            return CS[:, o:o + 128]
        def SEL2(d, hh):
            o = 512 + (d * 2 + hh) * 2
            return CS[:, o:o + 2]
        IDB2 = sb([128, 64], BF16)
        CP("dve", IDB2[:], CD[:, 0:64], ["cd"], ["idb2"])
        MEMSET("dve", ONES16[:], 1.0, ["ones16"])
        MEMSET("dve", NEGONES16[:], -1.0, ["ones16"])
        CK = ["c64", "c16", "c128", "onesb", "idb", "ones16"]

        def wload(src_ap, parts=128):
            i = wrr[0]
            wrr[0] = (i + 1) % NSLOT
            S.dma("pool", WR[i][0:parts, :], src_ap, writes=["w%d" % i])
            return WR[i], "w%d" % i

        def adaln(l):
            ps, pk = newps()
            for mb in range(12):
                W, wk = wload(ada_w[l, mb])
                Wv = W[:].rearrange("p (k m) -> p k m", k=8)
                items = []
                for mi in range(4):
                    m = mb * 4 + mi
                    for kc in range(8):
                        pass
                def fn(e, Wv=Wv, mb=mb):
                    ins = None
                    for mi in range(4):
                        m = mb * 4 + mi
                        for kc in range(8):
                            ins = e.matmul(ps[:, 2 * m:2 * m + 2], lhsT=Wv[:, kc, mi * 128:(mi + 1) * 128], rhs=SC[:, :, kc],
                                           start=(kc == 0), stop=(kc == 7))
                    return ins
                S.op("pe", fn, [wk, "sc"], [pk])
            for g in range(2):
                TT_("dve", MODALL[:, l, g, :], ps[:, 0:96].rearrange("p (m g) -> p g m", g=2)[:, g, :], ADAB[:, l, :], ALU.add,
                    [pk, "adab"], ["mod%d" % l])

        def mod(l, g, i):
            return MODALL[:, l, g, i * 8:(i + 1) * 8]

        def modulate(l, g, which, ctx):
            GS = LP[:, which * 8:(which + 1) * 8]
            STT(GS, mod(l, g, 1 + 3 * which), 1.0, NG[:, l, which * 8:(which + 1) * 8], ALU.add, ALU.mult,
                ["mod%d" % l, "ng"], ["lp%d" % which])
            SQ = [sb([128, 8, TT], BF16, ctx) for _ in range(2)]
            RS = [sb([128, TT], F32, ctx) for _ in range(2)]
            RSTD = [sb([128, TT], F32, ctx) for _ in range(2)]
            TMP = [[sb([128, TT], F32, ctx) for _ in range(2)] for _ in range(2)]
            sls = [slice(tt * TT, (tt + 1) * TT) for tt in range(2)]
            pss = []
            for tt in range(2):
                for hf in range(2):
                    ACT(SQ[tt][:, hf * 4:(hf + 1) * 4, :], Y[:, hf * 4:(hf + 1) * 4, sls[tt]], AF.Square, ["y"], ["msq%d%d" % (tt, hf)])
            for tt in range(2):
                ps, pk = newps()
                pss.append((ps, pk))
                MM(ps[:], [(ONESB[:], SQ[tt][:, c, :]) for c in range(8)], ["msq%d0" % tt, "msq%d1" % tt, "onesb"], [pk])
            for tt in range(2):
                ACT(RS[tt][:], pss[tt][0][:], AF.Ln, [pss[tt][1]], ["mrs%d" % tt], scale=1.0 / D, bias=EPS)
            for tt in range(2):
                ACT(RSTD[tt][:], RS[tt][:], AF.Exp, ["mrs%d" % tt], ["mrstd%d" % tt], scale=-0.5)
            for c in range(8):
                for tt in range(2):
                    T_ = TMP[tt][c % 2]
                    tk = "mtmp%d%d" % (tt, c % 2)
                    STT(T_[:], Y[:, c, sls[tt]], GS[:, c:c + 1], RSTD[tt][:], ALU.mult, ALU.mult, ["y", "lp%d" % which, "mrstd%d" % tt], [tk])
                    ACT(HT[:, c, sls[tt]], T_[:], AF.Identity, [tk, "mod%d" % l], ["ht"],
                        bias=mod(l, g, 3 * which)[:, c:c + 1], scale=1.0)

        def mlp(l, g, ctx):
            H1 = sb([128, 32, NT], BF16, ctx)
            SQ2 = [sb([128, TT], F32, ctx) for _ in range(2)]
            n = 0
            for mb in range(8):
                W, wk = wload(w1[l, mb])
                Wv = W[:].rearrange("p (k m) -> p k m", k=8)
                for mi in range(4):
                    m = mb * 4 + mi
                    for tt in range(2):
                        sl = slice(tt * TT, (tt + 1) * TT)
                        ps, pk = newps()
                        MM(ps[:], [(Wv[:, kc, mi * 128:(mi + 1) * 128], HT[:, kc, sl]) for kc in range(8)], [wk, "ht"], [pk])
                        q = SQ2[n % 2]; qk = "sq2%d" % (n % 2); n += 1
                        ACT(q[:], ps[:], AF.Square, [pk], [qk])
                        STT(H1[:, m, sl], ps[:], 0.0, q[:], ALU.is_gt, ALU.mult, [pk, qk], [("h1", m)])
            for m in range(8):
                W, wk = wload(w2[l, m])
                Wv = W[:].rearrange("p (k m) -> p k m", k=32)
                for tt in range(2):
                    sl = slice(tt * TT, (tt + 1) * TT)
                    ps, pk = newps()
                    MM(ps[:], [(Wv[:, kc, :], H1[:, kc, sl]) for kc in range(32)], [wk] + [("h1", kc) for kc in range(32)], [pk])
                    STT(Y[:, m, sl], ps[:], mod(l, g, 5)[:, m:m + 1], Y[:, m, sl], ALU.mult, ALU.add, [pk, "y", "mod%d" % l], ["y"])

        def mixer_c(l, g, ctx):
            j = l // 2
            seqs = [(s * 256, 256) for s in range(4)] if g == 0 else [(0, 1024)]
            CPt = sb([128, 8, 11], F32, ctx)
            S.dma("sp", CPt[:], cp[j].rearrange("p (c k) -> p c k", k=11), writes=["cp"])
            CL = sb([128, 2, 8], F32, ctx)
            T1 = sb([128, 2, 8], F32, ctx)
            ACT(T1[:], CPt[:, :, 9:11].rearrange("p c d -> p d c"), AF.Exp, ["cp"], ["ct1"], scale=-1.0)
            ACT(T1[:], T1[:], AF.Ln, ["ct1"], ["ct1"], bias=1.0, scale=1.0)
            TS("dve", CL[:], T1[:], -8.0, None, ALU.mult, ALU.bypass, ["ct1"], ["cl"])
            BD = sb([128, 4, 8, 128], F32, ctx)
            for i in range(4):
                S.dma("sp", BD[:, i], cbd[j, i].rearrange("p (c m) -> p c m", c=8), writes=["bd"])
            H0 = sb([128, 2, 8], F32, ctx)
            if g == 1:
                for d in range(2):
                    S.dma("sp", H0[:, d, :], slru[j, d], writes=["h0"])
            FIN = sb([128, 2, 4, 8], F32, ctx)
            NSL = 2
            CSL = []
            for s_ in range(NSL):
                CSL.append(dict(xr=sb([128, NT], F32, ctx), xc=sb([128, NT], F32, ctx), gt=sb([128, NT], F32, ctx), ga=sb([128, NT], F32, ctx),
                                aa=[sb([128, NT], F32, ctx) for _ in range(2)], ta=[sb([128, NT], F32, ctx) for _ in range(2)],
                                tb=[sb([128, NT], F32, ctx) for _ in range(2)]))
            ns = len(seqs); L = seqs[0][1]
            def chunk_gen(s_, c, ci, Wxv, Wgv, wxk, wgk):
                B_ = CSL[s_]
                xr, xc, gt, ga = B_["xr"], B_["xc"], B_["gt"], B_["ga"]
                K_ = lambda n_: "%s_%d" % (n_, s_)
                kx, kc_, kg, kga = K_("xr"), K_("xc"), K_("gt"), K_("ga")
                banks = [(PSB[4 * s_ + q], "ps%d" % (4 * s_ + q)) for q in range(4)]
                for tt in range(2):
                    sl = slice(tt * TT, (tt + 1) * TT)
                    ps, pk = banks[tt]
                    MM(ps[:], [(Wxv[:, kc, ci * 128:(ci + 1) * 128], HT[:, kc, sl]) for kc in range(8)], [wxk, "ht"], [pk])
                    ps, pk = banks[2 + tt]
                    MM(ps[:], [(Wgv[:, kc, ci * 128:(ci + 1) * 128], HT[:, kc, sl]) for kc in range(8)], [wgk, "ht"], [pk])
                yield
                for tt in range(2):
                    sl = slice(tt * TT, (tt + 1) * TT)
                    CP("act", xr[:, sl], banks[tt][0][:], [banks[tt][1]], [kx])
                    CP("dve", gt[:, sl], banks[2 + tt][0][:], [banks[2 + tt][1]], [kg])
                yield
                TS("dve", xc[:], xr[:], CPt[:, c, 2:3], CPt[:, c, 4:5], ALU.mult, ALU.add, [kx, "cp"], [kc_])
                ACT(ga[:], gt[:], AF.Square, [kg], [kga])
                yield
                xr3 = xr[:].rearrange("p (s t) -> p s t", s=ns)
                xc3 = xc[:].rearrange("p (s t) -> p s t", s=ns)
                for tap, o in ((0, -2), (1, -1), (3, 1)):
                    d0, d1 = max(0, -o), L - max(0, o)
                    STT(xc3[:, :, d0:d1], xr3[:, :, d0 + o:d1 + o], CPt[:, c, tap:tap + 1], xc3[:, :, d0:d1], ALU.mult, ALU.add,
                        [kx, kc_, "cp"], [kc_])
                    yield
                TS("dve", ga[:], ga[:], 0.044715, 1.0, ALU.mult, ALU.add, [kga], [kga])
                yield
                TT_("dve", ga[:], ga[:], gt[:], ALU.mult, [kga, kg], [kga])
                yield
                ACT(ga[:], ga[:], AF.Sigmoid, [kga], [kga], scale=1.5957691216)
                yield
                TT_("dve", ga[:], ga[:], gt[:], ALU.mult, [kga, kg], [kga])
                yield
                for d in range(2):
                    a_, ta, tb = B_["aa"][d], B_["ta"][d], B_["tb"][d]
                    ka, kta, ktb = K_("aa%d" % d), K_("ta%d" % d), K_("tb%d" % d)
                    for tt in range(2):
                        sl = slice(tt * TT, (tt + 1) * TT)
                        ps, pk = banks[tt]
                        MM(ps[:], [(BD[:, d, c, :], xc[:, sl])], ["bd", kc_], [pk])
                        ps, pk = banks[2 + tt]
                        MM(ps[:], [(BD[:, 2 + d, c, :], xc[:, sl])], ["bd", kc_], [pk])
                    yield
                    for tt in range(2):
                        sl = slice(tt * TT, (tt + 1) * TT)
                        ACT(ta[:, sl], banks[tt][0][:], AF.Sigmoid, [banks[tt][1], "cp"], [kta], bias=CPt[:, c, 5 + d:6 + d], scale=1.0)
                        ACT(tb[:, sl], banks[2 + tt][0][:], AF.Sigmoid, [banks[2 + tt][1], "cp"], [ktb], bias=CPt[:, c, 7 + d:8 + d], scale=1.0)
                    yield
                    ACT(a_[:], ta[:], AF.Exp, [kta, "cl"], [ka], scale=CL[:, d, c:c + 1])
                    TT_("dve", tb[:], tb[:], xc[:], ALU.mult, [ktb, kc_], [ktb])
                    yield
                    ACT(ta[:], a_[:], AF.Square, [ka], [kta])
                    yield
                    ACT(ta[:], ta[:], AF.Sqrt, [kta], [kta], scale=-1.0, bias=1.0)
                    yield
                    TT_("dve", tb[:], ta[:], tb[:], ALU.mult, [kta, ktb], [ktb])
                    yield
                    for si, (t0, L_) in enumerate(seqs):
                        init = H0[:, d, c:c + 1] if g == 1 else 0.0
                        if d == 0:
                            o_, a2, u2 = tb[:, t0:t0 + L_], a_[:, t0:t0 + L_], tb[:, t0:t0 + L_]
                        else:
                            o_, a2, u2 = tb[:, t0:t0 + L_][:, ::-1], a_[:, t0:t0 + L_][:, ::-1], tb[:, t0:t0 + L_][:, ::-1]
                        S.op("dve", (lambda e, o_=o_, a2=a2, u2=u2, init=init: e.tensor_tensor_scan(
                            out=o_, data0=a2, data1=u2, initial=init, op0=ALU.mult, op1=ALU.add)), [ka, ktb, "h0"], [ktb])
                        if g == 0:
                            col = t0 + L_ - 1 if d == 0 else t0
                            CP("act", FIN[:, d, si, c:c + 1], tb[:, col:col + 1], [ktb], ["fin"])
                        yield
                tb0, tb1 = B_["tb"][0], B_["tb"][1]
                TT_("dve", tb0[:], tb0[:], tb1[:], ALU.add, [K_("tb0"), K_("tb1")], [K_("tb0")])
                yield
                TT_("dve", OB128[:, c, :], tb0[:], ga[:], ALU.mult, [K_("tb0"), kga], ["ob"])
                yield
            def run_rr2(gens):
                gens = list(gens)
                while gens:
                    nxt = []
                    for g_ in gens:
                        try:
                            next(g_)
                            nxt.append(g_)
                        except StopIteration:
                            pass
                    gens = nxt
            for half in range(2):
                Wx, wxk = wload(cin[j, half])
                Wg, wgk = wload(cin[j, 2 + half])
                Wxv = Wx[:].rearrange("p (k m) -> p k m", k=8)
                Wgv = Wg[:].rearrange("p (k m) -> p k m", k=8)
                for c0_ in range(0, 4, NSL):
                    run_rr2([chunk_gen(s_, half * 4 + c0_ + s_, c0_ + s_, Wxv, Wgv, wxk, wgk) for s_ in range(NSL)])
            if g == 0:
                for d in range(2):
                    for si in range(4):
                        S.dma("sp", o_lru[j, d, si], FIN[:, d, si, :], reads=["fin"], writes=["o_lru"])
            for mb in range(2):
                W, wk = wload(cout[j, mb])
                Wv = W[:].rearrange("p (k m) -> p k m", k=8)
                for mi in range(4):
                    m = mb * 4 + mi
                    for tt in range(2):
                        sl = slice(tt * TT, (tt + 1) * TT)
                        ps, pk = newps()
                        MM(ps[:], [(Wv[:, kc, mi * 128:(mi + 1) * 128], OB128[:, kc, sl]) for kc in range(8)], [wk, "ob"], [pk])
                        STT(Y[:, m, sl], ps[:], mod(l, g, 2)[:, m:m + 1], Y[:, m, sl], ALU.mult, ALU.add, [pk, "y", "mod%d" % l], ["y"])

        DNP = {}
        ONESBD = sb([128, 128], BF16)
        MEMSET("dve", ONESBD[:], 0.0, ["onesbd"])
        MEMSET("dve", ONESBD[0:64, 0:64], 1.0, ["onesbd"])
        MEMSET("dve", ONESBD[64:128, 64:128], 1.0, ["onesbd"])
        def deltanet(l, g, ctx0):
            j = l // 2
            seqs = [(s * 256, 4) for s in range(4)] if g == 0 else [(0, 16)]
            P16 = sb([16, 2], F32, ctx0)
            S.dma("sp", P16[:], abp16[j], writes=["p16"])
            P128 = sb([128, 61], F32, ctx0)
            S.dma("sp", P128[:], abp128[j], writes=["p128"])
            DNP["CW2"] = P128[:, 0:48].rearrange("p (h k) -> p h k", k=4)
            DNP["CB2"] = P128[:, 48:60]
            DNP["DNG2"] = P128[:, 60:61]
            BETA = sb([128, NT], F32, ctx0)
            GCF = sb([128, NT], F32, ctx0)
            GCB = sb([128, NT], F32, ctx0)
            NEA = sb([16, 1], F32, ctx0)
            cs_ = ExitStack()
            G = sb([16, NT], F32, cs_)
            CM = sb([16, 2048], F32, cs_)
            S.dma("sp", CM[:], cmask[:, :], writes=["cm"])
            CMF = CM[:, 0:1024]
            CMB = CM[:, 1024:2048]
            MEMSET("dve", BETA[:], 0.0, ["beta"])
            MEMSET("dve", GCF[:], 0.0, ["gcf"])
            MEMSET("dve", GCB[:], 0.0, ["gcb"])
            ACT(NEA[:], P16[:, 0:1], AF.Exp, ["p16"], ["nea"])
            TS("dve", NEA[:], NEA[:], -1.0, None, ALU.mult, ALU.bypass, ["nea"], ["nea"])
            W5, w5k = wload(abin[j, 5])
            W5v = W5[:].rearrange("p (k m) -> p k m", k=8)
            for tt in range(2):
                sl = slice(tt * TT, (tt + 1) * TT)
                ps, pk = newps()
                MM(ps[0:16, :], [(W5v[:, kc, 0:16], HT[:, kc, sl]) for kc in range(8)], [w5k, "ht"], [pk])
                ACT(G[:, sl], ps[0:16, :], AF.Exp, [pk, "p16"], ["g"], bias=P16[:, 1:2], scale=1.0)
                ACT(G[:, sl], G[:, sl], AF.Ln, ["g"], ["g"], bias=1.0, scale=1.0)
                TS("dve", G[:, sl], G[:, sl], NEA[:, 0:1], None, ALU.mult, ALU.bypass, ["g", "nea"], ["g"])
                ps, pk = newps()
                MM(ps[0:16, :], [(W5v[:, kc, 16:32], HT[:, kc, sl]) for kc in range(8)], [w5k, "ht"], [pk])
                ACT(BETA[0:16, sl], ps[0:16, :], AF.Sigmoid, [pk, "beta"], ["beta"])
            S.op("dve", lambda e: e.tensor_tensor_scan(out=GCF[0:16, :], data0=CMF, data1=G[:], initial=0.0, op0=ALU.mult, op1=ALU.add),
                 ["g", "cm", "gcf"], ["gcf"])
            S.op("dve", lambda e: e.tensor_tensor_scan(out=GCB[0:16, :][:, ::-1], data0=CMB[:, ::-1], data1=G[:, ::-1], initial=0.0,
                                                       op0=ALU.mult, op1=ALU.add), ["g", "cm", "gcb"], ["gcb"])
            CP("act", BETA[64:80, :], BETA[0:16, :], ["beta"], ["beta"])
            CP("act", GCF[64:80, :], GCF[0:16, :], ["gcf"], ["gcf"])
            CP("act", GCB[64:80, :], GCB[0:16, :], ["gcb"], ["gcb"])
            S.barrier()
            cs_.close()
            if DN_STAGE < 2:
                return
            for hh in range(2):
                with ExitStack() as ctx:
                    deltanet_half(l, g, j, hh, seqs, ctx, BETA, GCF, GCB)
                S.barrier()

        def deltanet_half(l, g, j, hh, seqs, ctx, BETA, GCF, GCB):
            CW2, CB2, DNG2 = DNP["CW2"], DNP["CB2"], DNP["DNG2"]
            HW_ = 128
            QT = sb([128, 2, NT], BF16, ctx)
            KT = sb([128, 2, NT], BF16, ctx)
            VT = sb([128, 2, NT], BF16, ctx)
            GATE = sb([128, 2, NT], BF16, ctx)
            OACC = sb([128, 2, NT], F32, ctx)
            SQ = sb([128, TT], BF16, ctx)
            RS = sb([128, TT], F32, ctx)
            c2 = ExitStack()
            RAW = [sb([128, NT], F32, c2) for _ in range(2)]
            CV = [sb([128, NT], F32, c2) for _ in range(2)]
            nseq = 4 if g == 0 else 1
            L = NT // nseq
            n = 0
            for blk in range(4):
                W, wk = wload(abin[j, blk])
                Wv = W[:].rearrange("p (k m) -> p k m", k=8)
                for pr in range(2):
                    h0 = hh * NH + 2 * pr
                    b = n % 2; n += 1
                    raw, cv = RAW[b], CV[b]
                    kr, kv_ = "raw%d" % b, "cv%d" % b
                    for tt in range(2):
                        sl = slice(tt * TT, (tt + 1) * TT)
                        ps, pk = newps()
                        MM(ps[:], [(Wv[:, kc, h0 * 64:(h0 + 2) * 64], HT[:, kc, sl]) for kc in range(8)], [wk, "ht"], [pk])
                        if blk == 3:
                            ACT(GATE[:, pr, sl], ps[:], AF.Silu, [pk], ["gate"])
                        else:
                            CP("act", raw[:, sl], ps[:], [pk], [kr])
                    if blk == 3:
                        continue
                    pbi = blk * 4 + hh * 2 + pr
                    TS("dve", cv[:], raw[:], CW2[:, pbi, 2:3], CB2[:, pbi:pbi + 1], ALU.mult, ALU.add, [kr, "p128"], [kv_])
                    r3 = raw[:].rearrange("p (s t) -> p s t", s=nseq)
                    c3 = cv[:].rearrange("p (s t) -> p s t", s=nseq)
                    for tap, o in ((0, -2), (1, -1), (3, 1)):
                        d0, d1 = max(0, -o), L - max(0, o)
                        STT(c3[:, :, d0:d1], r3[:, :, d0 + o:d1 + o], CW2[:, pbi, tap:tap + 1], c3[:, :, d0:d1], ALU.mult, ALU.add,
                            [kr, kv_, "p128"], [kv_])
                    if blk == 2:
                        ACT(VT[:, pr, :], cv[:], AF.Silu, [kv_], ["vt"])
                        continue
                    ACT(cv[:], cv[:], AF.Silu, [kv_], [kv_])
                    dst = QT if blk == 0 else KT
                    dk_ = "qt" if blk == 0 else "kt"
                    for tt in range(2):
                        sl = slice(tt * TT, (tt + 1) * TT)
                        ACT(SQ[:], cv[:, sl], AF.Square, [kv_], ["dsq"])
                        ps, pk = newps()
                        MM(ps[:], [(ONESBD[:], SQ[:])], ["dsq", "onesbd"], [pk])
                        ACT(RS[:], ps[:], AF.Ln, [pk], ["drs"], bias=EPS, scale=1.0)
                        ACT(RS[:], RS[:], AF.Exp, ["drs"], ["drs"], scale=-0.5)
                        STT(dst[:, pr, sl], cv[:, sl], (0.125 if blk == 0 else 1.0), RS[:], ALU.mult, ALU.mult, [kv_, "drs"], [dk_])
            S.barrier()
            c2.close()
            if DN_STAGE < 3:
                return
            KS = 4
            def mk(shape, dt=F32):
                return sb(shape, dt, ctx)
            SL = []
            for k in range(KS):
                SL.append(dict(A=mk([128, HW_]), Bm=mk([128, HW_]), EGT=mk([128, HW_]),
                               KBc=mk([128, HW_], BF16), QGc=mk([128, HW_], BF16), INTR=mk([128, HW_], BF16), KDEC=mk([128, HW_], BF16),
                               VTOK=mk([128, HW_], BF16), KTOK=mk([128, HW_], BF16), QX=mk([128, 2 * HW_]), QT=mk([128, HW_]),
                               TKS=mk([128, 10]), SELGL=mk([128, 2]), RT=mk([128, HW_]), VNB=mk([128, HW_], BF16)))
            for k in range(KS):
                MEMSET("dve", SL[k]["SELGL"][:], 0.0, ["selgl_%d" % k])
            SS = {}
            for si in range(min(2, len(seqs))):
                for d in range(2):
                    SS[(si, d)] = (sb([128, HW_], F32, ctx), sb([128, HW_], BF16, ctx))
            def init_state(si):
                for d in range(2):
                    Sf, Sb_ = SS[(si % 2, d)]
                    sk = ("S", si % 2, d)
                    if g == 1:
                        S.dma("sp", Sf[:], sdelta[j, d, hh], writes=[sk])
                    else:
                        MEMSET("dve", Sf[:], 0.0, [sk])
                    CP("act", Sb_[:], Sf[:], [sk], [("Sb", si % 2, d)])
            oacc_written = set()
            def h2(ap):
                return ap.rearrange("p (i x) -> p i x", i=2)
            def bc(ap):
                return ap.unsqueeze(2).to_broadcast([128, 2, 64])
            GP = ((0, slice(0, 64), slice(0, 16)), (1, slice(64, 128), slice(64, 80)))
            def geom(si, d, ck):
                t0, nch = seqs[si]
                chunk = ck if d == 0 else nch - 1 - ck
                c0 = t0 + chunk * 64
                return chunk, c0, slice(c0, c0 + 64)
            def partA(k, si, d, ck):
                sl_ = SL[k]
                X, Xk = PSB[2 * k], "ps%d" % (2 * k)
                Yb, Yk = PSB[2 * k + 1], "ps%d" % (2 * k + 1)
                chunk, c0, cs = geom(si, d, ck)
                GC = GCF if d == 0 else GCB
                gck = "gcf" if d == 0 else "gcb"
                last = 63 if d == 0 else 0
                NEGM = NEG_LO2 if d == 0 else NEG_UP2
                NEGMT = NEG_UP2 if d == 0 else NEG_LO2
                K_ = lambda n_: "%s_%d" % (n_, k)
                A, Bm, EGT = sl_["A"], sl_["Bm"], sl_["EGT"]
                KBc, QGc, INTR, KDEC, VTOK, KTOK = sl_["KBc"], sl_["QGc"], sl_["INTR"], sl_["KDEC"], sl_["VTOK"], sl_["KTOK"]
                QX, QT_, tks, SELGL = sl_["QX"], sl_["QT"], sl_["TKS"], sl_["SELGL"]
                blk = BLK2(d, hh)
                sel = SEL2(d, hh)
                TT_("pool", h2(A[0:80, :]), GC[0:80, cs].unsqueeze(1).to_broadcast([80, 2, 64]), h2(blk[0:80, :]), ALU.mult, [gck, "cs"], [K_("A")])
                TT_("pool", h2(Bm[0:80, :]), BETA[0:80, cs].unsqueeze(1).to_broadcast([80, 2, 64]), h2(blk[0:80, :]), ALU.mult, ["beta", "cs"], [K_("Bm")])
                TS("pool", SELGL[0:80, :], sel[0:80, :], GC[0:80, c0 + last:c0 + last + 1], None, ALU.mult, ALU.bypass, [gck, "cs"], [K_("selgl")])
                MMS([(X[pr_, i * 64:(i + 1) * 64], VT[pr_, i, cs], IDB2[pr_, :]) for (gp, pr_, p16) in GP for i in range(2)], ["vt", "idb2"], [Xk])
                MMS([(Yb[pr_, i * 64:(i + 1) * 64], KT[pr_, i, cs], IDB2[pr_, :]) for (gp, pr_, p16) in GP for i in range(2)], ["kt", "idb2"], [Yk])
                yield
                CP("act", VTOK[:], X[:, 0:HW_], [Xk], [K_("vtok")])
                CP("dve", KTOK[:], Yb[:, 0:HW_], [Yk], [K_("ktok")])
                yield
                def fE(e):
                    ins = None
                    for (gp, pr_, p16) in GP:
                        e.matmul(X[pr_, 0:HW_], lhsT=GC[p16, cs], rhs=blk[p16, :], start=True, stop=False)
                        e.matmul(X[pr_, 0:HW_], lhsT=NEGONES16[p16, :], rhs=A[p16, :], start=False, stop=True)
                        e.matmul(X[pr_, HW_:HW_ + 2], lhsT=BETA[p16, cs], rhs=sel[p16, :], start=True, stop=True)
                        e.matmul(X[pr_, HW_ + 2:HW_ + 4], lhsT=GC[p16, cs], rhs=sel[p16, :], start=True, stop=True)
                        ins = e.matmul(X[pr_, HW_ + 4:HW_ + 6], lhsT=ONES16[p16, :], rhs=SELGL[p16, :], start=True, stop=True)
                    return ins
                S.op("pe", fE, [gck, "beta", "cs", "ones16", K_("A"), K_("selgl")], [Xk])
                MMS([(Yb[pr_, 0:HW_], ONES16[p16, :], Bm[p16, :]) for (gp, pr_, p16) in GP] +
                    [(Yb[pr_, HW_:2 * HW_], ONES16[p16, :], A[p16, :]) for (gp, pr_, p16) in GP], ["ones16", K_("A"), K_("Bm")], [Yk])
                yield
                ACT(EGT[:], Yb[:, HW_:2 * HW_], AF.Exp, [Yk], [K_("EGT")])
                TT_("dve", h2(KBc[:]), KT[:, :, cs], h2(Yb[:, 0:HW_]), ALU.mult, ["kt", Yk], [K_("kbc")])
                ACT(tks[:, 2:6], X[:, HW_ + 2:HW_ + 6], AF.Exp, [Xk], [K_("tks")])
                ACT(tks[:, 6:8], h2(X[:, 0:HW_])[:, :, last], AF.Exp, [Xk], [K_("tks")], scale=-1.0)
                CP("dve", tks[:, 0:2], X[:, HW_:HW_ + 2], [Xk], [K_("tks")])
                yield
                TT_("dve", A[:], X[:, 0:HW_], NEGM, ALU.add, [Xk, "cd"], [K_("A")])
                STT(Bm[:], X[:, 0:HW_], -1.0, NEGMT, ALU.mult, ALU.add, [Xk, "cd"], [K_("Bm")])
                yield
                ACT(A[:], A[:], AF.Exp, [K_("A")], [K_("A")])
                ACT(Bm[:], Bm[:], AF.Exp, [K_("Bm")], [K_("Bm")])
                TT_("pool", h2(QGc[:]), QT[:, :, cs], h2(EGT[:]), ALU.mult, ["qt", K_("EGT")], [K_("qgc")])
                TT_("pool", h2(KDEC[:]), h2(KTOK[:]), bc(tks[:, 6:8]), ALU.mult, [K_("ktok"), K_("tks")], [K_("kdec")])
                yield
                TT_("pool", tks[:, 8:10], tks[:, 0:2], tks[:, 2:4], ALU.mult, [K_("tks")], [K_("tks")])
                kb3 = h2(KBc[:])
                MMS([(Yb[pr_, i * 64:(i + 1) * 64], kb3[pr_, i, :], KT[pr_, i, cs]) for (gp, pr_, p16) in GP for i in range(2)] +
                    [(Yb[pr_, HW_ + i * 64:HW_ + (i + 1) * 64], KT[pr_, i, cs], kb3[pr_, i, :]) for (gp, pr_, p16) in GP for i in range(2)],
                    [K_("kbc"), "kt"], [Yk])
                MMS([(X[pr_, i * 64:(i + 1) * 64], KT[pr_, i, cs], QT[pr_, i, cs]) for (gp, pr_, p16) in GP for i in range(2)], ["kt", "qt"], [Xk])
                yield
                qx4 = QX[:].rearrange("p (i two x) -> p i two x", i=2, two=2)
                qx3 = QX[:].rearrange("p (i y) -> p i y", i=2)
                qt3 = h2(QT_[:])
                STT(qt3, h2(Yb[:, 0:HW_]), -1.0, h2(A[:]), ALU.mult, ALU.mult, [Yk, K_("A")], [K_("qtn")])
                STT(qx4[:, :, 0, :], h2(Yb[:, HW_:2 * HW_]), -1.0, h2(Bm[:]), ALU.mult, ALU.mult, [Yk, K_("Bm")], [K_("qx")])
                yield
                TT_("pool", Bm[:], Bm[:], IDENT2, ALU.add, [K_("Bm"), "cd"], [K_("Bm")])
                TT_("dve", INTR[:], X[:, 0:HW_], Bm[:], ALU.mult, [Xk, K_("Bm")], [K_("intr")])
                TT_("pool", h2(A[:]), h2(VTOK[:]), bc(tks[:, 0:2]), ALU.mult, [K_("vtok"), K_("tks"), K_("A")], [K_("A")])
                yield
                for lev in range(5):
                    if lev == 0:
                        MMS([(X[pr_, i * 64:(i + 1) * 64], qt3[pr_, i, :], qx4[pr_, i, 0, :]) for (gp, pr_, p16) in GP for i in range(2)],
                            [K_("qtn"), K_("qx")], [Xk])
                    elif lev == 4:
                        MMS([(X[pr_, i * 64:(i + 1) * 64], qt3[pr_, i, :], qx4[pr_, i, 1, :]) for (gp, pr_, p16) in GP for i in range(2)],
                            [K_("qtn"), K_("qx")], [Xk])
                    else:
                        MMS([(X[pr_, i * 128:(i + 1) * 128], qt3[pr_, i, :], qx3[pr_, i, :]) for (gp, pr_, p16) in GP for i in range(2)],
                            [K_("qtn"), K_("qx")], [Xk])
                    MMS([(Yb[pr_, i * 64:(i + 1) * 64], qx4[pr_, i, 0, :], qt3[pr_, i, :]) for (gp, pr_, p16) in GP for i in range(2)],
                        [K_("qtn"), K_("qx")], [Yk])
                    yield
                    x4 = X[:, 0:2 * HW_].rearrange("p (i two x) -> p i two x", i=2, two=2)
                    if lev == 0:
                        TT_("pool", qx4[:, :, 1, :], qx4[:, :, 0, :], h2(IDENT2), ALU.add, [K_("qx"), "cd"], [K_("qx")])
                        CP("act", qx4[:, :, 0, :], h2(X[:, 0:HW_]), [Xk], [K_("qx")])
                    elif lev == 4:
                        TT_("dve", qx4[:, :, 1, :], h2(X[:, 0:HW_]), qx4[:, :, 1, :], ALU.add, [Xk, K_("qx")], [K_("qx")])
                    else:
                        CP("act", qx4[:, :, 0, :], x4[:, :, 0, :], [Xk], [K_("qx")])
                        TT_("dve", qx4[:, :, 1, :], x4[:, :, 1, :], qx4[:, :, 1, :], ALU.add, [Xk, K_("qx")], [K_("qx")])
                    CP("act", QT_[:], Yb[:, 0:HW_], [Yk], [K_("qtn")])
                    yield
                MMS([(X[pr_, i * 64:(i + 1) * 64], qt3[pr_, i, :], qx4[pr_, i, 1, :]) for (gp, pr_, p16) in GP for i in range(2)],
                    [K_("qtn"), K_("qx")], [Xk])
                yield
                TT_("dve", h2(EGT[:]), h2(X[:, 0:HW_]), qx4[:, :, 1, :], ALU.add, [Xk, K_("qx"), K_("EGT")], [K_("EGT")])
                yield

            def partB(k, si, d, ck):
                sl_ = SL[k]
                X, Xk = PSB[2 * k], "ps%d" % (2 * k)
                Yb, Yk = PSB[2 * k + 1], "ps%d" % (2 * k + 1)
                chunk, c0, cs = geom(si, d, ck)
                K_ = lambda n_: "%s_%d" % (n_, k)
                BV, QGc, INTR, KDEC = sl_["A"], sl_["QGc"], sl_["INTR"], sl_["KDEC"]
                tks, RT, VNB, TTB, SD = sl_["TKS"], sl_["RT"], sl_["VNB"], sl_["EGT"], sl_["Bm"]
                Sf, Sb_ = SS[(si % 2, d)]
                sk, sbk = ("S", si % 2, d), ("Sb", si % 2, d)
                Sb3, Sf3 = h2(Sb_[:]), h2(Sf[:])
                MMS([(X[pr_, i * 64:(i + 1) * 64], KT[pr_, i, cs], Sb3[pr_, i, :]) for (gp, pr_, p16) in GP for i in range(2)], ["kt", sbk], [Xk])
                TT_("pool", h2(SD[:]), Sf3, bc(tks[:, 4:6]), ALU.mult, [sk, K_("tks"), K_("Bm")], [K_("Bm")])
                yield
                TT_("dve", h2(RT[:]), h2(X[:, 0:HW_]), bc(tks[:, 8:10]), ALU.mult, [Xk, K_("tks")], [K_("rt")])
                yield
                TT_("dve", RT[:], BV[:], RT[:], ALU.subtract, [K_("A"), K_("rt")], [K_("rt")])
                yield
                tt3, r3 = h2(TTB[:]), h2(RT[:])
                MMS([(Yb[pr_, i * 64:(i + 1) * 64], tt3[pr_, i, :], r3[pr_, i, :]) for (gp, pr_, p16) in GP for i in range(2)], [K_("EGT"), K_("rt")], [Yk])
                yield
                CP("act", VNB[:], Yb[:, 0:HW_], [Yk], [K_("vnb")])
                yield
                vn3, qg3, in3, kd3 = h2(VNB[:]), h2(QGc[:]), h2(INTR[:]), h2(KDEC[:])
                MMS([(Yb[pr_, i * 64:(i + 1) * 64], kd3[pr_, i, :], vn3[pr_, i, :]) for (gp, pr_, p16) in GP for i in range(2)], [K_("kdec"), K_("vnb")], [Yk])
                def fo(e):
                    ins = None
                    for (gp, pr_, p16) in GP:
                        for i in range(2):
                            e.matmul(X[pr_, i * 64:(i + 1) * 64], lhsT=Sb3[pr_, i, :], rhs=qg3[pr_, i, :], start=True, stop=False)
                            ins = e.matmul(X[pr_, i * 64:(i + 1) * 64], lhsT=vn3[pr_, i, :], rhs=in3[pr_, i, :], start=False, stop=True)
                    return ins
                S.op("pe", fo, [sbk, K_("qgc"), K_("vnb"), K_("intr")], [Xk])
                yield
                TT_("dve", Sb_[:], Yb[:, 0:HW_], SD[:], ALU.add, [Yk, K_("Bm")], [sbk])
                TT_("dve", Sf[:], Yb[:, 0:HW_], SD[:], ALU.add, [Yk, K_("Bm")], [sk])
                ok_ = ("oacc", si, chunk)
                if ok_ not in oacc_written:
                    oacc_written.add(ok_)
                    CP("act", OACC[:, :, cs], h2(X[:, 0:HW_]), [Xk], [ok_])
                else:
                    TT_("dve", OACC[:, :, cs], h2(X[:, 0:HW_]), OACC[:, :, cs], ALU.add, [Xk, ok_], [ok_])
                yield

            def run_rr(gens):
                gens = list(gens)
                while gens:
                    nxt = []
                    for g_ in gens:
                        try:
                            next(g_)
                            nxt.append(g_)
                        except StopIteration:
                            pass
                    gens = nxt

            for sp_ in range(0, len(seqs), 2):
                sis = [si for si in (sp_, sp_ + 1) if si < len(seqs)]
                for si in sis:
                    init_state(si)
                nck = seqs[sis[0]][1] if DN_STAGE > 3 else 1
                steps = [(si, d, ck) for ck in range(nck) for si in sis for d in range(2)]
                for w0 in range(0, len(steps), KS):
                    win = steps[w0:w0 + KS]
                    run_rr([partA(k, *st) for k, st in enumerate(win)])
                    pending = list(enumerate(win))
                    while pending:
                        seen, rnd, rest = set(), [], []
                        for k, st in pending:
                            ch = (st[0], st[1])
                            if ch in seen:
                                rest.append((k, st))
                            else:
                                seen.add(ch)
                                rnd.append((k, st))
                        run_rr([partB(k, *st) for k, st in rnd])
                        pending = rest
                if g == 0:
                    for si in sis:
                        for d in range(2):
                            S.dma("sp", o_dn[si, j, d, hh], SS[(si % 2, d)][0][:], reads=[("S", si % 2, d)], writes=["o_dn"])
            if DN_STAGE < 9:
                return
            allo = [("oacc", si, c) for si in range(len(seqs)) for c in range(seqs[si][1])]
            TO = sb([128, TT], F32, ctx)
            for i in range(2):
                for tt in range(2):
                    sl = slice(tt * TT, (tt + 1) * TT)
                    ACT(SQ[:], OACC[:, i, sl], AF.Square, allo, ["dsq"])
                    ps, pk = newps()
                    MM(ps[:], [(ONESBD[:], SQ[:])], ["dsq", "onesbd"], [pk])
                    ACT(RS[:], ps[:], AF.Ln, [pk], ["drs"], bias=EPS, scale=1.0 / 64)
                    ACT(RS[:], RS[:], AF.Exp, ["drs"], ["drs"], scale=-0.5)
                    STT(TO[:], OACC[:, i, sl], DNG2, RS[:], ALU.mult, ALU.mult, allo + ["drs", "p128"], ["to"])
                    TT_("dve", OBt[:, hh * 2 + i, sl], TO[:], GATE[:, i, sl], ALU.mult, ["to", "gate"], ["ob"])

        def attention(l, g, ctx):
            j = l // 2
            P64 = sb([64, 24 * 5 + 11], F32, ctx)
            S.dma("sp", P64[:], abp64[j], writes=["p64b"])
            QG, KG = P64[:, 121:122], P64[:, 122:123]
            ESINK = sb([64, 8], F32, ctx)
            ACT(ESINK[:], P64[:, 123:131], AF.Exp, ["p64b"], ["esink"])
            QB = sb([64, 8, NT], BF16, ctx)
            KB = sb([64, 2, NT], BF16, ctx)
            KN = sb([64, 2, NT], F32, ctx)
            VB = sb([128, 8, 128], BF16, ctx)
            VF = sb([128, 8, 128], F32, ctx)
            SQ = sb([128, TT], BF16, ctx)
            RS = sb([128, TT], F32, ctx)
            QN = [sb([128, TT], F32, ctx) for _ in range(2)]
            QNB = sb([128, TT], BF16, ctx)
            T1 = sb([128, TT], F32, ctx)
            T2 = sb([128, TT], F32, ctx)
            PERMB = sb([128, 128], BF16, ctx)
            G128 = sb([128, 2], F32, ctx)
            if g == 1:
                CSN = sb([128, 2048], F32, ctx)
                for hf in range(2):
                    S.dma("sp", CSN[hf * 64:(hf + 1) * 64, :], cossin[:, :], writes=["csn"])
                COS = CSN[:, 0:1024]
                SIN = CSN[:, 1024:2048]
            MEMSET("dve", PERMB[:], 0.0, ["permb"])
            CP("dve", PERMB[0:64, 0:64], PERMT, ["c64", "permb"], ["permb"])
            CP("dve", PERMB[64:128, 64:128], PERMT, ["c64", "permb"], ["permb"])
            for hf in range(2):
                S.dma("sp", G128[hf * 64:(hf + 1) * 64, :], abp64[j][:, 121:123], writes=["g128"])
            W4, w4k = wload(abin[j, 4])
            W5, w5k = wload(abin[j, 5])
            W4v = W4[:].rearrange("p (k m) -> p k m", k=8)
            W5v = W5[:].rearrange("p (k m) -> p k m", k=8)
            n = 0
            for pp in range(5):
                for tt in range(2):
                    sl = slice(tt * TT, (tt + 1) * TT)
                    ps, pk = newps()
                    if pp < 4:
                        MM(ps[:], [(W4v[:, kc, pp * 128:(pp + 1) * 128], HT[:, kc, sl]) for kc in range(8)], [w4k, "ht"], [pk])
                        gn = G128[:, 0:1]
                    else:
                        MM(ps[:], [(W5v[:, kc, 32:160], HT[:, kc, sl]) for kc in range(8)], [w5k, "ht"], [pk])
                        gn = G128[:, 1:2]
                    ACT(SQ[:], ps[:], AF.Square, [pk], ["asq"])
                    ps2, pk2 = newps()
                    MM(ps2[:], [(ONESBD[:], SQ[:])], ["asq", "onesbd"], [pk2])
                    ACT(RS[:], ps2[:], AF.Ln, [pk2], ["ars"], bias=EPS, scale=1.0 / 64)
                    ACT(RS[:], RS[:], AF.Exp, ["ars"], ["ars"], scale=-0.5)
                    qn = QN[n % 2]; qnk = "qn%d" % (n % 2); n += 1
                    if pp < 4:
                        dsts = [QB[:, 2 * pp, sl], QB[:, 2 * pp + 1, sl]]
                        dkey = "qb"
                    else:
                        dsts = [KB[:, 0, sl], KB[:, 1, sl]]
                        dkey = "kb"
                    STT(qn[:], ps[:], gn, RS[:], ALU.mult, ALU.mult, [pk, "ars", "g128"], [qnk])
                    if g == 0:
                        for hf in range(2):
                            CP("act", dsts[hf], qn[hf * 64:(hf + 1) * 64, :], [qnk], [dkey])
                            if pp == 4:
                                CP("act", KN[:, hf, sl], qn[hf * 64:(hf + 1) * 64, :], [qnk], ["kn"])
                    else:
                        CP("act", QNB[:], qn[:], [qnk], ["qnb"])
                        ps3, pk3 = newps()
                        MM(ps3[:], [(PERMB[:], QNB[:])], ["permb", "qnb"], [pk3])
                        TT_("dve", T1[:], qn[:], COS[:, sl], ALU.mult, [qnk, "csn"], ["at1"])
                        TT_("dve", T2[:], ps3[:], SIN[:, sl], ALU.mult, [pk3, "csn"], ["at2"])
                        for hf in range(2):
                            TT_("dve", dsts[hf], T1[hf * 64:(hf + 1) * 64, :], T2[hf * 64:(hf + 1) * 64, :], ALU.add, ["at1", "at2"], [dkey])
            if g == 0:
                S.dma("sp", o_k[j], KN[:], reads=["kn"], writes=["o_k"])
            for tb in range(8):
                ps, pk = newps()
                MM(ps[:, 0:128], [(HT[:, kc, tb * 128:(tb + 1) * 128], W5v[:, kc, 160:288]) for kc in range(8)], [w5k, "ht"], [pk])
                CP("act", VB[:, tb, :], ps[:, 0:128], [pk], ["vb"])
                if g == 0:
                    CP("dve", VF[:, tb, :], ps[:, 0:128], [pk], ["vf"])
            if g == 0:
                S.dma("sp", o_v[j], VF[:], reads=["vf"], writes=["o_v"])
            if g == 1:
                KCF = sb([64, 2, 256], F32, ctx)
                KC = sb([64, 2, 256], BF16, ctx)
                VCF = sb([128, 2, 128], F32, ctx)
                VC = sb([128, 2, 128], BF16, ctx)
                S.dma("sp", KCF[:], kcT[j], writes=["kcf"])
                S.dma("sp", VCF[:], vc[j], writes=["vcf"])
                CP("dve", KC[:], KCF[:], ["kcf"], ["kc"])
                CP("dve", VC[:], VCF[:], ["vcf"], ["vcb"])
            PT = [sb([128, 5, 512], BF16, ctx) for _ in range(2)]
            DEN = sb([64, 512], F32, ctx)
            n = 0
            for qb in range(8):
                qs = slice(qb * 128, (qb + 1) * 128)
                for kv in range(2):
                    if g == 0:
                        s = qb // 2
                        kbl = [("lat", 2 * s, None), ("lat", 2 * s + 1, None)]
                    else:
                        kbl = []
                        if qb > 0:
                            kbl.append(("lat", qb - 1, MASK_LO))
                        kbl.append(("lat", qb, None))
                        if qb < 7:
                            kbl.append(("lat", qb + 1, MASK_UP))
                        kbl += [("ctx", 0, None), ("ctx", 1, None)]
                    pt = PT[n % 2]; ptk = "pt%d" % (n % 2); n += 1
                    for bi, (kind, kb_, msk) in enumerate(kbl):
                        ps, pk = newps()
                        if kind == "lat":
                            lh, lk = KB[:, kv, kb_ * 128:(kb_ + 1) * 128], "kb"
                        else:
                            lh, lk = KC[:, kv, kb_ * 128:(kb_ + 1) * 128], "kc"
                        MM(ps[:].rearrange("p (h q) -> p h q", h=4), [(lh, QB[:, 4 * kv:4 * kv + 4, qs])], [lk, "qb"], [pk])
                        ACT(pt[:, bi, :], ps[:], AF.Exp, [pk], [(ptk, bi)], scale=0.125)
                        if msk is not None:
                            TT_("dve", pt[:, bi, :].rearrange("p (h q) -> p h q", h=4), pt[:, bi, :].rearrange("p (h q) -> p h q", h=4),
                                msk.unsqueeze(1).to_broadcast([128, 4, 128]), ALU.mult, [(ptk, bi), "c128"], [(ptk, bi)])
                    nb = len(kbl)
                    pv, pvk = newps()
                    prs = []
                    for bi, (kind, kb_, msk) in enumerate(kbl):
                        if kind == "lat":
                            prs.append((VB[:, kb_, kv * 64:(kv + 1) * 64], pt[:, bi, :]))
                        else:
                            prs.append((VC[:, kb_, kv * 64:(kv + 1) * 64], pt[:, bi, :]))
                    MM(pv[0:64, :], prs, ["vb", "vcb"] + [(ptk, bi) for bi in range(nb)], [pvk])
                    pd, pdk = newps()
                    MM(pd[0:64, :], [(ONESB[:, 0:64], pt[:, bi, :]) for bi in range(nb)], ["onesb"] + [(ptk, bi) for bi in range(nb)], [pdk])
                    TT_("dve", DEN[:].rearrange("p (h q) -> p h q", h=4), pd[0:64, :].rearrange("p (h q) -> p h q", h=4),
                        ESINK[:, 4 * kv:4 * kv + 4].unsqueeze(2).to_broadcast([64, 4, 128]), ALU.add, [pdk, "esink"], ["den"])
                    ACT(DEN[:], DEN[:], AF.Ln, ["den"], ["den"])
                    ACT(DEN[:], DEN[:], AF.Exp, ["den"], ["den"], scale=-1.0)
                    for hi in range(4):
                        TT_("dve", OBH(8 + 4 * kv + hi, qs), pv[0:64, hi * 128:(hi + 1) * 128], DEN[:, hi * 128:(hi + 1) * 128], ALU.mult,
                            [pvk, "den"], ["ob"])

        def out_proj_ab(l, g):
            j = l // 2
            for mb in range(2):
                W, wk = wload(about[j, mb])
                Wv = W[:].rearrange("p (k m) -> p k m", k=8)
                for mi in range(4):
                    m = mb * 4 + mi
                    for tt in range(2):
                        sl = slice(tt * TT, (tt + 1) * TT)
                        ps, pk = newps()
                        MM(ps[:], [(Wv[:, kc, mi * 128:(mi + 1) * 128], OB128[:, kc, sl]) for kc in range(8)], [wk, "ob"], [pk])
                        STT(Y[:, m, sl], ps[:], mod(l, g, 2)[:, m:m + 1], Y[:, m, sl], ALU.mult, ALU.add, [pk, "y", "mod%d" % l], ["y"])

        ada_done = set()
        for g in GROUPS:
            for c in range(8):
                S.dma("sp", Y[:, c, :], xT[g, c * 128:(c + 1) * 128, :], writes=["y"])
            for l in range(DEPTH):
                if l not in ada_done:
                    adaln(l)
                    ada_done.add(l)
                with ExitStack() as ctx:
                    modulate(l, g, 0, ctx)
                S.barrier()
                if l % 2 == 0:
                    if not (DO_A and DO_B):
                        MEMSET("dve", OBt[:], 0.0, ["ob"])
                    if DO_A:
                        with ExitStack() as ctx:
                            deltanet(l, g, ctx)
                        S.barrier()
                    if DO_B:
                        with ExitStack() as ctx:
                            attention(l, g, ctx)
                        S.barrier()
                    out_proj_ab(l, g)
                else:
                    if DO_C:
                        with ExitStack() as ctx:
                            mixer_c(l, g, ctx)
                        S.barrier()
                with ExitStack() as ctx:
                    modulate(l, g, 1, ctx)
                S.barrier()
                if DO_MLP:
                    with ExitStack() as ctx:
                        mlp(l, g, ctx)
                    S.barrier()
            for c in range(8):
                S.dma("sp", yT[g, c * 128:(c + 1) * 128, :], Y[:, c, :], reads=["y"], writes=["yT"])
        S.barrier()
        with nc.Block() as block:
            S.emit(block)
    return nc


_CACHE = {}


def _consts():
    c64 = np.zeros((64, 832), np.float32)
    eye = np.eye(64, dtype=np.float32)
    r = np.arange(64)[:, None]; x = np.arange(64)[None, :]
    lo = np.where(r > x, 0.0, -30000.0).astype(np.float32)
    up = np.where(r < x, 0.0, -30000.0).astype(np.float32)
    for i in range(4):
        c64[:, i * 64:(i + 1) * 64] = eye
        c64[:, 256 + i * 64:256 + (i + 1) * 64] = lo
        c64[:, 512 + i * 64:512 + (i + 1) * 64] = up
    P = np.zeros((64, 64), np.float32)
    for m in range(64):
        if m % 32 < 16:
            P[m, m + 16] = -1.0
        else:
            P[m, m - 16] = 1.0
    c64[:, 768:832] = P.T
    t = np.arange(1024, dtype=np.float32)
    inv = (10000.0 ** (-np.arange(0, 32, 2, dtype=np.float32) / 32)).astype(np.float32)
    ang_r = (np.floor(t / 64)[:, None] * inv).astype(np.float32)
    ang_c = ((t % 64)[:, None] * inv).astype(np.float32)
    ang = np.zeros((64, 1024), np.float32)
    for p in range(64):
        ang[p] = (ang_r if p < 32 else ang_c)[:, p % 16]
    cossin = np.concatenate([np.cos(ang), np.sin(ang)], axis=1).astype(np.float32)
    c16 = np.zeros((16, 1040), np.float32)
    cmask = np.zeros((16, 2048), np.float32)
    for d in range(2):
        for hh in range(2):
            for i in range(4):
                k = d * 8 + hh * 4 + i
                o = (d * 2 + hh) * 256
                c16[k, o + i * 64:o + (i + 1) * 64] = 1.0
                c16[k, 1024 + (d * 2 + hh) * 4 + i] = 1.0
    cm = np.ones(1024, np.float32); cm[0::64] = 0.0
    cmask[:, 0:1024] = cm
    cm = np.ones(1024, np.float32); cm[63::64] = 0.0
    cmask[:, 1024:2048] = cm
    c128 = np.zeros((128, 256), np.float32)
    k = np.arange(128)[:, None]; q = np.arange(128)[None, :]
    c128[:, 0:128] = (q <= k)
    c128[:, 128:256] = (k <= q)
    return c64, c16, c128, cossin, cmask


def _fm(v):
    return np.ascontiguousarray(np.swapaxes(v.reshape(v.shape[:-1] + (8, 128)), -1, -2))


def _wt(w, kc, mcols):
    K, M = w.shape
    a = w.reshape(kc, 128, M // mcols, mcols)
    return np.ascontiguousarray(a.transpose(2, 1, 0, 3)).reshape(M // mcols, 128, kc * mcols)


def kernel(x_prompt, x_sample, state_delta, cache_k, cache_v, state_lru, c, c_ctx,
           ada_w, ada_b, norm1_g, norm2_g, ff_w1, ff_w2,
           ab_w_in, ab_conv_w, ab_conv_b, dn_a_log, dn_dt_bias, dn_norm_g,
           attn_q_norm_g, attn_k_norm_g, attn_sink, ab_w_out,
           c_w_in, c_conv_w, c_conv_b, lru_w_a, lru_b_a, lru_w_x, lru_b_x, lru_lambda, c_w_out):
    f = lambda a: np.asarray(a, dtype=np.float32)
    x_prompt, x_sample = f(x_prompt), f(x_sample)
    NE = (DEPTH + 1) // 2
    NO = max(1, DEPTH // 2)
    NL = DEPTH
    c64, c16, c128, cossin, cmask = _consts()
    shared = {"c64": c64, "c16": c16, "c128": c128, "cossin": cossin, "cmask": cmask}
    shared["ada_w"] = np.stack([_wt(f(ada_w[l]), 8, 512) for l in range(NL)])
    shared["ada_b"] = np.stack([np.ascontiguousarray(f(ada_b[l]).reshape(48, 128).T) for l in range(NL)])
    shared["ng"] = np.stack([np.concatenate([_fm(f(norm1_g[l])), _fm(f(norm2_g[l]))], axis=1) for l in range(NL)])
    shared["w1"] = np.stack([_wt(f(ff_w1[l]), 8, 512) for l in range(NL)])
    shared["w2"] = np.stack([_wt(f(ff_w2[l]), 32, 128) for l in range(NL)])
    abin = []
    for j in range(NE):
        w = f(ab_w_in[j])
        pad = np.zeros((1024, 6 * 512), np.float32)
        pad[:, 0:1536] = w[:, 0:1536]
        pad[:, 1536:2048] = w[:, 1536:2048]
        pad[:, 2048:2560] = w[:, 2080:2592]
        pad[:, 2560:2592] = w[:, 2048:2080]
        pad[:, 2592:2720] = w[:, 2592:2720]
        pad[:, 2720:2848] = w[:, 2720:2848]
        abin.append(_wt(pad, 8, 512))
    shared["abin"] = np.stack(abin)
    shared["about"] = np.stack([_wt(f(ab_w_out[j]), 8, 512) for j in range(NE)])
    p64 = np.zeros((NE, 64, 24 * 5 + 11), np.float32)
    for j in range(NE):
        cw = f(ab_conv_w[j]).reshape(4, 24, 64)
        p64[j, :, 0:96] = cw.transpose(2, 1, 0).reshape(64, 96)
        p64[j, :, 96:120] = f(ab_conv_b[j]).reshape(24, 64).T
        p64[j, :, 120] = f(dn_norm_g[j])
        p64[j, :, 121] = f(attn_q_norm_g[j])
        p64[j, :, 122] = f(attn_k_norm_g[j])
        p64[j, :, 123:131] = f(attn_sink[j])[None, :]
    shared["abp64"] = p64
    p128 = np.zeros((NE, 128, 61), np.float32)
    for j in range(NE):
        p128[j, :, 0:48] = f(ab_conv_w[j]).reshape(4, 12, 128).transpose(2, 1, 0).reshape(128, 48)
        p128[j, :, 48:60] = f(ab_conv_b[j]).reshape(12, 128).T
        p128[j, :, 60] = np.tile(f(dn_norm_g[j]), 2)
    shared["abp128"] = p128
    cdm = np.zeros((128, 384), np.float32)
    r_ = (np.arange(128) % 64)[:, None]; x_ = np.arange(64)[None, :]
    for i in range(2):
        cdm[:, i * 64:(i + 1) * 64] = (r_ == x_)
        cdm[:, 128 + i * 64:128 + (i + 1) * 64] = np.where(r_ > x_, 0.0, -30000.0)
        cdm[:, 256 + i * 64:256 + (i + 1) * 64] = np.where(r_ < x_, 0.0, -30000.0)
    shared["cd"] = cdm
    csm = np.zeros((128, 520), np.float32)
    for d in range(2):
        for hh_ in range(2):
            for gp in range(2):
                for ii in range(2):
                    kk = d * 8 + hh_ * 4 + 2 * ii + gp
                    o = (d * 2 + hh_) * 128
                    csm[gp * 64 + kk, o + ii * 64:o + (ii + 1) * 64] = 1.0
                    csm[gp * 64 + kk, 512 + (d * 2 + hh_) * 2 + ii] = 1.0
    shared["csel"] = csm
    p16 = np.zeros((NE, 16, 2), np.float32)
    for j in range(NE):
        p16[j, :, 0] = f(dn_a_log[j]).reshape(16)
        p16[j, :, 1] = f(dn_dt_bias[j]).reshape(16)
    shared["abp16"] = p16
    shared["cin"] = np.stack([_wt(f(c_w_in[j]), 8, 512) for j in range(NO)])
    shared["cout"] = np.stack([_wt(f(c_w_out[j]), 8, 512) for j in range(NO)])
    cbd = np.zeros((NO, 4, 128, 8, 128), np.float32)
    for j in range(NO):
        for gi, wsrc in enumerate((lru_w_a, lru_w_x)):
            for d in range(2):
                w = f(wsrc[j][d])
                for cc in range(8):
                    for s_ in range(2):
                        cbd[j, gi * 2 + d, s_ * 64:(s_ + 1) * 64, cc, s_ * 64:(s_ + 1) * 64] = w[cc * 2 + s_]
    shared["cbd"] = cbd.reshape(NO, 4, 128, 8 * 128)
    cpp = np.zeros((NO, 128, 8, 11), np.float32)
    for j in range(NO):
        cpp[j, :, :, 0:4] = np.stack([_fm(f(c_conv_w[j][t])) for t in range(4)], axis=-1)
        cpp[j, :, :, 4] = _fm(f(c_conv_b[j]))
        for d in range(2):
            cpp[j, :, :, 5 + d] = _fm(f(lru_b_a[j][d]))
            cpp[j, :, :, 7 + d] = _fm(f(lru_b_x[j][d]))
            cpp[j, :, :, 9 + d] = _fm(f(lru_lambda[j][d]))
    shared["cp"] = cpp.reshape(NO, 128, 88)
    in_maps = []
    for i in range(8):
        b = i % 4
        m = dict(shared)
        xp = x_prompt[4 * i:4 * i + 4].reshape(1024, 1024)
        m["xT"] = np.ascontiguousarray(np.stack([xp.T, x_sample[b].T]))
        m["cond"] = np.stack([_fm(f(c_ctx)), _fm(f(c[b]))])
        sd = f(state_delta[b])[:NE]
        m["sdelta"] = np.ascontiguousarray(sd.reshape(NE, 2, 2, 2, 2, 64, 64).transpose(0, 1, 2, 4, 5, 3, 6)).reshape(NE, 2, 2, 128, 128)
        ck = f(cache_k[b])[:NE]
        m["kcT"] = np.ascontiguousarray(ck.transpose(0, 3, 2, 1))
        cv = f(cache_v[b])[:NE].reshape(NE, 2, 128, 128)
        m["vc"] = np.ascontiguousarray(cv.transpose(0, 2, 1, 3))
        m["slru"] = _fm(f(state_lru[b])[:NO])
        in_maps.append(m)
    if "nc" not in _CACHE:
        _CACHE["nc"] = build_program()
    res = run_bass_kernel_spmd(_CACHE["nc"], in_maps, core_ids=list(range(8)))
    R = res.results
    y_prompt = np.zeros((32, 256, 1024), np.float32)
    y_sample = np.zeros((4, 1024, 1024), np.float32)
    new_dn = np.zeros((32, NE, 2, 8, 64, 64), np.float32)
    new_k = np.zeros((32, NE, 256, 2, 64), np.float32)
    new_v = np.zeros((32, NE, 256, 2, 64), np.float32)
    new_lru = np.zeros((32, NO, 2, 1024), np.float32)
    for i in range(8):
        r = R[i]
        y_prompt[4 * i:4 * i + 4] = r["yT"][0].T.reshape(4, 256, 1024)
        if i < 4:
            y_sample[i] = r["yT"][1].T
        dn = r["o_dn"].reshape(4, NE, 2, 2, 2, 64, 2, 64)
        new_dn[4 * i:4 * i + 4] = dn.transpose(0, 1, 2, 3, 6, 4, 5, 7).reshape(4, NE, 2, 8, 64, 64)
        ok = r["o_k"].reshape(NE, 64, 2, 4, 256)
        new_k[4 * i:4 * i + 4] = ok.transpose(3, 0, 4, 2, 1)
        ov = r["o_v"].reshape(NE, 128, 4, 2, 2, 64)
        new_v[4 * i:4 * i + 4] = ov.transpose(2, 0, 3, 1, 4, 5).reshape(4, NE, 256, 2, 64)
        ol = r["o_lru"].reshape(NO, 2, 4, 128, 8)
        new_lru[4 * i:4 * i + 4] = ol.transpose(2, 0, 1, 4, 3).reshape(4, NO, 2, 1024)
    return (y_prompt, y_sample, new_dn, new_k, new_v, new_lru)
```

```python
import os
import math
import numpy as np
from contextlib import ExitStack
import concourse.bass as bass
import concourse.mybir as mybir
from concourse.bass_utils import run_bass_kernel_spmd

F32 = mybir.dt.float32
BF16 = mybir.dt.bfloat16
F32R = mybir.dt.float32r
USE_F32R = os.environ.get("K_F32R", "0") == "1"
AF = mybir.ActivationFunctionType
ALU = mybir.AluOpType

D = 1024
NT = 1024
TT = 512
DEPTH = int(os.environ.get("K_DEPTH", "4"))
DO_A = os.environ.get("K_A", "1") == "1"
DO_B = os.environ.get("K_B", "1") == "1"
DO_C = os.environ.get("K_C", "1") == "1"
DO_MLP = os.environ.get("K_MLP", "1") == "1"
GROUPS = [int(x) for x in os.environ.get("K_GROUPS", "01")]
EPS = 1e-6
DN_STAGE = int(os.environ.get("K_DN_STAGE", "9"))
DN_CUT = float(os.environ.get("K_DN_CUT", "9"))
NH = 4
WH = NH * 64
EPOCH = 2000


class Sched:
    ENG = ("pe", "act", "dve", "pool", "sp")

    def __init__(self, nc, es, n_dma_sems=4):
        self.nc = nc
        self.es = es
        self.ops = {e: [] for e in self.ENG}
        self.cnt = {e: 0 for e in self.ENG}
        self.sems = {}
        self.dma_sems = {}
        self.dma_cnt = {}
        self.dma_rr = {}
        self.n_dma_sems = n_dma_sems
        for q in ("sp", "pool"):
            for s in range(n_dma_sems):
                self.dma_sems[(q, s)] = es.enter_context(nc.semaphore("d_%s%d" % (q, s)))
                self.dma_cnt[(q, s)] = 0
            self.dma_rr[q] = 0
        self.waited = {e: {} for e in self.ENG}
        self.last_w = {}
        self.readers = {}

    def _deps(self, eng, reads, writes):
        need = {}
        def add(ev):
            sk, v = ev
            if need.get(sk, 0) < v:
                need[sk] = v
        for k in reads:
            if k in self.last_w:
                add(self.last_w[k])
        for k in writes:
            if k in self.last_w:
                add(self.last_w[k])
            for ev in self.readers.get(k, ()):
                add(ev)
        waits = []
        w = self.waited[eng]
        for sk, v in need.items():
            if w.get(sk, 0) < v:
                w[sk] = v
                waits.append((sk, v))
        return waits

    def _commit(self, ev, reads, writes):
        for k in reads:
            r = self.readers.setdefault(k, [])
            r.append(ev)
            if len(r) > 24:
                best = {}
                for sk, v in r:
                    if best.get(sk, 0) < v:
                        best[sk] = v
                self.readers[k] = list(best.items())
        for k in writes:
            self.last_w[k] = ev
            self.readers[k] = []

    def op(self, eng, fn, reads=(), writes=()):
        psr = [k for k in reads if isinstance(k, str) and k.startswith("ps")]
        if psr:
            writes = list(writes) + [k for k in psr if k not in writes]
        waits = self._deps(eng, reads, writes)
        self.cnt[eng] += 1
        ev = (eng, self.cnt[eng])
        self.ops[eng].append((waits, fn, ("c", eng, self.cnt[eng])))
        self._commit(ev, reads, writes)

    def dma(self, q, out, in_, reads=(), writes=()):
        s = self.dma_rr[q]
        self.dma_rr[q] = (s + 1) % self.n_dma_sems
        sk = ("dma", q, s)
        waits = self._deps(q, reads, writes)
        prev = self.dma_cnt[(q, s)]
        if prev > 0 and self.waited[q].get(sk, 0) < prev:
            self.waited[q][sk] = prev
            waits.append((sk, prev))
        self.dma_cnt[(q, s)] = prev + 16
        ev = (sk, prev + 16)
        self.ops[q].append((waits, (lambda e: e.dma_start(out=out, in_=in_)), ("d", (q, s))))
        self._commit(ev, reads, writes)

    def barrier(self):
        for e in self.ENG:
            waits = []
            lazy = (e == "pool")
            for o in self.ENG:
                if o != "sp" and self.cnt[o] > 0 and self.waited[e].get(o, 0) < self.cnt[o]:
                    if not lazy:
                        self.waited[e][o] = self.cnt[o]
                    waits.append((o, self.cnt[o]))
            for (q, s), c in self.dma_cnt.items():
                sk = ("dma", q, s)
                if c > 0 and self.waited[e].get(sk, 0) < c:
                    if not lazy:
                        self.waited[e][sk] = c
                    waits.append((sk, c))
            if waits:
                self.ops[e].append((waits, None, ("bar",) if lazy else None))

    def _get_sem(self, eng, ep):
        k = (eng, ep)
        if k not in self.sems:
            self.sems[k] = self.es.enter_context(self.nc.semaphore("s_%s%d" % (eng, ep)))
        return self.sems[k]

    def _wait(self, e, sk, v):
        if isinstance(sk, tuple):
            e.wait_ge(self.dma_sems[(sk[1], sk[2])], v)
        else:
            ep, r = divmod(v - 1, EPOCH)
            e.wait_ge(self._get_sem(sk, ep), r + 1)

    def emit(self, block):
        for eng in self.ENG:
            for ep in range((self.cnt[eng] + EPOCH - 1) // EPOCH + 1):
                if eng != "sp":
                    self._get_sem(eng, ep)
        pl = self.ops["pool"]
        keep = []
        for i, (waits, fn, kind) in enumerate(pl):
            if kind == ("bar",):
                has_c = False
                for (w2, f2, k2) in pl[i + 1:]:
                    if k2 == ("bar",):
                        break
                    if k2 is not None and k2[0] == "c":
                        has_c = True
                        break
                if has_c:
                    keep.append((waits, None, None))
            else:
                keep.append((waits, fn, kind))
        self.ops["pool"] = keep
        def mk(ename):
            def body(e):
                for waits, fn, kind in self.ops[ename]:
                    for sk, v in waits:
                        self._wait(e, sk, v)
                    if fn is None:
                        continue
                    ins = fn(e)
                    if kind[0] == "c":
                        ep = (kind[2] - 1) // EPOCH
                        ins.then_inc(self._get_sem(kind[1], ep), 1)
                    else:
                        ins.then_inc(self.dma_sems[kind[1]], 16)
            return body
        block.tensor(mk("pe"))
        block.scalar(mk("act"))
        block.vector(mk("dve"))
        block.gpsimd(mk("pool"))
        block.sync(mk("sp"))


def build_program():
    nc = bass.Bass("TRN2", target_bir_lowering=False)
    NE = (DEPTH + 1) // 2
    NO = max(1, DEPTH // 2)
    NL = DEPTH
    def din(name, shape):
        return nc.dram_tensor(name, list(shape), F32, kind="ExternalInput").ap()
    def dout(name, shape):
        return nc.dram_tensor(name, list(shape), F32, kind="ExternalOutput").ap()
    xT = din("xT", [2, D, NT])
    cond = din("cond", [2, 128, 8])
    sdelta = din("sdelta", [NE, 2, 2, 128, 128])
    kcT = din("kcT", [NE, 64, 2, 256])
    vc = din("vc", [NE, 128, 2, 128])
    slru = din("slru", [NO, 2, 128, 8])
    ada_w = din("ada_w", [NL, 12, 128, 8 * 512])
    ada_b = din("ada_b", [NL, 128, 48])
    ng = din("ng", [NL, 128, 16])
    w1 = din("w1", [NL, 8, 128, 8 * 512])
    w2 = din("w2", [NL, 8, 128, 32 * 128])
    abin = din("abin", [NE, 6, 128, 8 * 512])
    about = din("about", [NE, 2, 128, 8 * 512])
    abp64 = din("abp64", [NE, 64, 24 * 5 + 3 + 8])
    abp16 = din("abp16", [NE, 16, 2])
    abp128 = din("abp128", [NE, 128, 61])
    cd = din("cd", [128, 384])
    csel = din("csel", [128, 520])
    cin = din("cin", [NO, 4, 128, 8 * 512])
    cout = din("cout", [NO, 2, 128, 8 * 512])
    cbd = din("cbd", [NO, 4, 128, 8 * 128])
    cp = din("cp", [NO, 128, 8 * 11])
    c64 = din("c64", [64, 832])
    cossin = din("cossin", [64, 2048])
    c16 = din("c16", [16, 1040])
    cmask = din("cmask", [16, 2048])
    c128 = din("c128", [128, 256])
    yT = dout("yT", [2, D, NT])
    o_dn = dout("o_dn", [4, NE, 2, 2, 128, 128])
    o_k = dout("o_k", [NE, 64, 2, NT])
    o_v = dout("o_v", [NE, 128, 8, 128])
    o_lru = dout("o_lru", [NO, 2, 4, 128, 8])

    with ExitStack() as es:
        S = Sched(nc, es)
        uid = [0]
        def sb(shape, dt=F32, ctx=None):
            uid[0] += 1
            return (ctx or es).enter_context(nc.sbuf_tensor("t%d" % uid[0], list(shape), dt))
        def ACT(out, in_, func, reads, writes, **kw):
            S.op("act", lambda e: e.activation(out=out, in_=in_, func=func, **kw), reads, writes)
        def TT_(eng, out, in0, in1, op, reads, writes):
            S.op(eng, lambda e: e.tensor_tensor(out=out, in0=in0, in1=in1, op=op), reads, writes)
        def TS(eng, out, in0, s1, s2, op0, op1, reads, writes):
            S.op(eng, lambda e: e.tensor_scalar(out=out, in0=in0, scalar1=s1, scalar2=s2, op0=op0, op1=op1), reads, writes)
        def STT(out, in0, scalar, in1, op0, op1, reads, writes):
            S.op("dve", lambda e: e.scalar_tensor_tensor(out=out, in0=in0, scalar=scalar, in1=in1, op0=op0, op1=op1), reads, writes)
        def CP(eng, out, in_, reads, writes):
            if eng == "act":
                ACT(out, in_, AF.Copy, reads, writes)
            else:
                S.op(eng, lambda e: e.tensor_copy(out=out, in_=in_), reads, writes)
        def RECIP(out, in_, reads, writes):
            S.op("dve", lambda e: e.reciprocal(out=out, in_=in_), reads, writes)
        def MM(out, pairs, reads, writes):
            pairs = list(pairs)
            def fn(e):
                ins = None
                n = len(pairs)
                for i, (l, r) in enumerate(pairs):
                    ins = e.matmul(out, lhsT=l, rhs=r, start=(i == 0), stop=(i == n - 1))
                return ins
            S.op("pe", fn, reads, writes)
        def MMS(items, reads, writes):
            items = list(items)
            def fn(e):
                ins = None
                for (o, l, r) in items:
                    ins = e.matmul(o, lhsT=l, rhs=r, start=True, stop=True)
                return ins
            S.op("pe", fn, reads, writes)
        def MEMSET(eng, out, val, writes):
            S.op(eng, lambda e: e.memset(out, val), (), writes)

        PSB = [es.enter_context(nc.psum_tensor("ps%d" % i, [128, 512], F32)) for i in range(8)]
        prr = [0]
        def newps():
            i = prr[0]
            prr[0] = (i + 1) % 8
            return PSB[i], "ps%d" % i

        Y = sb([128, 8, NT])
        HT = sb([128, 8, NT], BF16)
        OBt = sb([128, 8, NT], BF16)
        OB128 = OBt[:]
        def OBH(h, sl):
            return OBt[(h % 2) * 64:(h % 2) * 64 + 64, h // 2, sl]
        NSLOT = 3
        WR = [sb([128, 4096], BF16) for _ in range(NSLOT)]
        wrr = [0]
        C64 = sb([64, 832])
        C16 = sb([16, 1040])
        C128 = sb([128, 256])
        MODALL = sb([128, 4, 2, 48])
        SC = sb([128, 2, 8], BF16)
        CONDT = sb([128, 2, 8])
        ONESB = sb([128, 128], BF16)
        IDB = sb([64, 64], BF16)
        NG = sb([128, 4, 16])
        ADAB = sb([128, 4, 48])
        LP = sb([128, 64])
        IDENT4 = C64[:, 0:256]
        NEG_LO = C64[:, 256:512]
        NEG_UP = C64[:, 512:768]
        PERMT = C64[:, 768:832]
        def BLK(d, hh):
            o = (d * 2 + hh) * 256
            return C16[:, o:o + 256]
        def SEL(d, hh):
            o = 1024 + (d * 2 + hh) * 4
            return C16[:, o:o + 4]
        MASK_LO = C128[:, 0:128]
        MASK_UP = C128[:, 128:256]

        S.dma("sp", C64[:], c64[:, :], writes=["c64"])
        S.dma("sp", C16[:], c16[:, :], writes=["c16"])
        S.dma("sp", C128[:], c128[:, :], writes=["c128"])
        for g in range(2):
            S.dma("sp", CONDT[:, g, :], cond[g], writes=["condt"])
        for l in range(NL):
            S.dma("sp", NG[:, l, :], ng[l], writes=["ng"])
            S.dma("sp", ADAB[:, l, :], ada_b[l], writes=["adab"])
        ACT(SC[:], CONDT[:], AF.Silu, ["condt"], ["sc"])
        MEMSET("dve", ONESB[:], 1.0, ["onesb"])
        CP("dve", IDB[:], C64[:, 0:64], ["c64"], ["idb"])
        ONES16 = sb([128, 64])
        NEGONES16 = sb([128, 64])
        CD = sb([128, 384])
        CS = sb([128, 520])
        S.dma("sp", CD[:], cd[:, :], writes=["cd"])
        S.dma("sp", CS[:], csel[:, :], writes=["cs"])
        IDENT2 = CD[:, 0:128]
        NEG_LO2 = CD[:, 128:256]
        NEG_UP2 = CD[:, 256:384]
        def BLK2(d, hh):
            o = (d * 2 + hh) * 128
            return CS[:, o:o + 128]
        def SEL2(d, hh):
            o = 512 + (d * 2 + hh) * 2
            return CS[:, o:o + 2]
        IDB2 = sb([128, 64], BF16)
        CP("dve", IDB2[:], CD[:, 0:64], ["cd"], ["idb2"])
        MEMSET("dve", ONES16[:], 1.0, ["ones16"])
        MEMSET("dve", NEGONES16[:], -1.0, ["ones16"])
        CK = ["c64", "c16", "c128", "onesb", "idb", "ones16"]

        def wload(src_ap, parts=128):
            i = wrr[0]
            wrr[0] = (i + 1) % NSLOT
            S.dma("pool", WR[i][0:parts, :], src_ap, writes=["w%d" % i])
            return WR[i], "w%d" % i

        def adaln(l):
            ps, pk = newps()
            for mb in range(12):
                W, wk = wload(ada_w[l, mb])
                Wv = W[:].rearrange("p (k m) -> p k m", k=8)
                items = []
                for mi in range(4):
                    m = mb * 4 + mi
                    for kc in range(8):
                        pass
                def fn(e, Wv=Wv, mb=mb):
                    ins = None
                    for mi in range(4):
                        m = mb * 4 + mi
                        for kc in range(8):
                            ins = e.matmul(ps[:, 2 * m:2 * m + 2], lhsT=Wv[:, kc, mi * 128:(mi + 1) * 128], rhs=SC[:, :, kc],
                                           start=(kc == 0), stop=(kc == 7))
                    return ins
                S.op("pe", fn, [wk, "sc"], [pk])
            for g in range(2):
                TT_("dve", MODALL[:, l, g, :], ps[:, 0:96].rearrange("p (m g) -> p g m", g=2)[:, g, :], ADAB[:, l, :], ALU.add,
                    [pk, "adab"], ["mod%d" % l])

        def mod(l, g, i):
            return MODALL[:, l, g, i * 8:(i + 1) * 8]

        def modulate(l, g, which, ctx):
            GS = LP[:, which * 8:(which + 1) * 8]
            STT(GS, mod(l, g, 1 + 3 * which), 1.0, NG[:, l, which * 8:(which + 1) * 8], ALU.add, ALU.mult,
                ["mod%d" % l, "ng"], ["lp%d" % which])
            SQ = [sb([128, 8, TT], BF16, ctx) for _ in range(2)]
            RS = [sb([128, TT], F32, ctx) for _ in range(2)]
            RSTD = [sb([128, TT], F32, ctx) for _ in range(2)]
            TMP = [[sb([128, TT], F32, ctx) for _ in range(2)] for _ in range(2)]
            sls = [slice(tt * TT, (tt + 1) * TT) for tt in range(2)]
            pss = []
            for tt in range(2):
                for hf in range(2):
                    ACT(SQ[tt][:, hf * 4:(hf + 1) * 4, :], Y[:, hf * 4:(hf + 1) * 4, sls[tt]], AF.Square, ["y"], ["msq%d%d" % (tt, hf)])
            for tt in range(2):
                ps, pk = newps()
                pss.append((ps, pk))
                MM(ps[:], [(ONESB[:], SQ[tt][:, c, :]) for c in range(8)], ["msq%d0" % tt, "msq%d1" % tt, "onesb"], [pk])
            for tt in range(2):
                ACT(RS[tt][:], pss[tt][0][:], AF.Ln, [pss[tt][1]], ["mrs%d" % tt], scale=1.0 / D, bias=EPS)
            for tt in range(2):
                ACT(RSTD[tt][:], RS[tt][:], AF.Exp, ["mrs%d" % tt], ["mrstd%d" % tt], scale=-0.5)
            for c in range(8):
                for tt in range(2):
                    T_ = TMP[tt][c % 2]
                    tk = "mtmp%d%d" % (tt, c % 2)
                    STT(T_[:], Y[:, c, sls[tt]], GS[:, c:c + 1], RSTD[tt][:], ALU.mult, ALU.mult, ["y", "lp%d" % which, "mrstd%d" % tt], [tk])
                    ACT(HT[:, c, sls[tt]], T_[:], AF.Identity, [tk, "mod%d" % l], ["ht"],
                        bias=mod(l, g, 3 * which)[:, c:c + 1], scale=1.0)

        def mlp(l, g, ctx):
            H1 = sb([128, 32, NT], BF16, ctx)
            SQ2 = [sb([128, TT], F32, ctx) for _ in range(2)]
            n = 0
            for mb in range(8):
                W, wk = wload(w1[l, mb])
                Wv = W[:].rearrange("p (k m) -> p k m", k=8)
                for mi in range(4):
                    m = mb * 4 + mi
                    for tt in range(2):
                        sl = slice(tt * TT, (tt + 1) * TT)
                        ps, pk = newps()
                        MM(ps[:], [(Wv[:, kc, mi * 128:(mi + 1) * 128], HT[:, kc, sl]) for kc in range(8)], [wk, "ht"], [pk])
                        q = SQ2[n % 2]; qk = "sq2%d" % (n % 2); n += 1
                        ACT(q[:], ps[:], AF.Square, [pk], [qk])
                        STT(H1[:, m, sl], ps[:], 0.0, q[:], ALU.is_gt, ALU.mult, [pk, qk], [("h1", m)])
            for m in range(8):
                W, wk = wload(w2[l, m])
                Wv = W[:].rearrange("p (k m) -> p k m", k=32)
                for tt in range(2):
                    sl = slice(tt * TT, (tt + 1) * TT)
                    ps, pk = newps()
                    MM(ps[:], [(Wv[:, kc, :], H1[:, kc, sl]) for kc in range(32)], [wk] + [("h1", kc) for kc in range(32)], [pk])
                    STT(Y[:, m, sl], ps[:], mod(l, g, 5)[:, m:m + 1], Y[:, m, sl], ALU.mult, ALU.add, [pk, "y", "mod%d" % l], ["y"])

        def mixer_c(l, g, ctx):
            j = l // 2
            seqs = [(s * 256, 256) for s in range(4)] if g == 0 else [(0, 1024)]
            CPt = sb([128, 8, 11], F32, ctx)
            S.dma("sp", CPt[:], cp[j].rearrange("p (c k) -> p c k", k=11), writes=["cp"])
            CL = sb([128, 2, 8], F32, ctx)
            T1 = sb([128, 2, 8], F32, ctx)
            ACT(T1[:], CPt[:, :, 9:11].rearrange("p c d -> p d c"), AF.Exp, ["cp"], ["ct1"], scale=-1.0)
            ACT(T1[:], T1[:], AF.Ln, ["ct1"], ["ct1"], bias=1.0, scale=1.0)
            TS("dve", CL[:], T1[:], -8.0, None, ALU.mult, ALU.bypass, ["ct1"], ["cl"])
            BD = sb([128, 4, 8, 128], F32, ctx)
            for i in range(4):
                S.dma("sp", BD[:, i], cbd[j, i].rearrange("p (c m) -> p c m", c=8), writes=["bd"])
            H0 = sb([128, 2, 8], F32, ctx)
            if g == 1:
                for d in range(2):
                    S.dma("sp", H0[:, d, :], slru[j, d], writes=["h0"])
            FIN = sb([128, 2, 4, 8], F32, ctx)
            NSL = 2
            CSL = []
            for s_ in range(NSL):
                CSL.append(dict(xr=sb([128, NT], F32, ctx), xc=sb([128, NT], F32, ctx), gt=sb([128, NT], F32, ctx), ga=sb([128, NT], F32, ctx),
                                aa=[sb([128, NT], F32, ctx) for _ in range(2)], ta=[sb([128, NT], F32, ctx) for _ in range(2)],
                                tb=[sb([128, NT], F32, ctx) for _ in range(2)]))
            ns = len(seqs); L = seqs[0][1]
            def chunk_gen(s_, c, ci, Wxv, Wgv, wxk, wgk):
                B_ = CSL[s_]
                xr, xc, gt, ga = B_["xr"], B_["xc"], B_["gt"], B_["ga"]
                K_ = lambda n_: "%s_%d" % (n_, s_)
                kx, kc_, kg, kga = K_("xr"), K_("xc"), K_("gt"), K_("ga")
                banks = [(PSB[4 * s_ + q], "ps%d" % (4 * s_ + q)) for q in range(4)]
                for tt in range(2):
                    sl = slice(tt * TT, (tt + 1) * TT)
                    ps, pk = banks[tt]
                    MM(ps[:], [(Wxv[:, kc, ci * 128:(ci + 1) * 128], HT[:, kc, sl]) for kc in range(8)], [wxk, "ht"], [pk])
                    ps, pk = banks[2 + tt]
                    MM(ps[:], [(Wgv[:, kc, ci * 128:(ci + 1) * 128], HT[:, kc, sl]) for kc in range(8)], [wgk, "ht"], [pk])
                yield
                for tt in range(2):
                    sl = slice(tt * TT, (tt + 1) * TT)
                    CP("act", xr[:, sl], banks[tt][0][:], [banks[tt][1]], [kx])
                    CP("dve", gt[:, sl], banks[2 + tt][0][:], [banks[2 + tt][1]], [kg])
                yield
                TS("dve", xc[:], xr[:], CPt[:, c, 2:3], CPt[:, c, 4:5], ALU.mult, ALU.add, [kx, "cp"], [kc_])
                ACT(ga[:], gt[:], AF.Square, [kg], [kga])
                yield
                xr3 = xr[:].rearrange("p (s t) -> p s t", s=ns)
                xc3 = xc[:].rearrange("p (s t) -> p s t", s=ns)
                for tap, o in ((0, -2), (1, -1), (3, 1)):
                    d0, d1 = max(0, -o), L - max(0, o)
                    STT(xc3[:, :, d0:d1], xr3[:, :, d0 + o:d1 + o], CPt[:, c, tap:tap + 1], xc3[:, :, d0:d1], ALU.mult, ALU.add,
                        [kx, kc_, "cp"], [kc_])
                    yield
                TS("dve", ga[:], ga[:], 0.044715, 1.0, ALU.mult, ALU.add, [kga], [kga])
                yield
                TT_("dve", ga[:], ga[:], gt[:], ALU.mult, [kga, kg], [kga])
                yield
                ACT(ga[:], ga[:], AF.Sigmoid, [kga], [kga], scale=1.5957691216)
                yield
                TT_("dve", ga[:], ga[:], gt[:], ALU.mult, [kga, kg], [kga])
                yield
                for d in range(2):
                    a_, ta, tb = B_["aa"][d], B_["ta"][d], B_["tb"][d]
                    ka, kta, ktb = K_("aa%d" % d), K_("ta%d" % d), K_("tb%d" % d)
                    for tt in range(2):
                        sl = slice(tt * TT, (tt + 1) * TT)
                        ps, pk = banks[tt]
                        MM(ps[:], [(BD[:, d, c, :], xc[:, sl])], ["bd", kc_], [pk])
                        ps, pk = banks[2 + tt]
                        MM(ps[:], [(BD[:, 2 + d, c, :], xc[:, sl])], ["bd", kc_], [pk])
                    yield
                    for tt in range(2):
                        sl = slice(tt * TT, (tt + 1) * TT)
                        ACT(ta[:, sl], banks[tt][0][:], AF.Sigmoid, [banks[tt][1], "cp"], [kta], bias=CPt[:, c, 5 + d:6 + d], scale=1.0)
                        ACT(tb[:, sl], banks[2 + tt][0][:], AF.Sigmoid, [banks[2 + tt][1], "cp"], [ktb], bias=CPt[:, c, 7 + d:8 + d], scale=1.0)
                    yield
                    ACT(a_[:], ta[:], AF.Exp, [kta, "cl"], [ka], scale=CL[:, d, c:c + 1])
                    TT_("dve", tb[:], tb[:], xc[:], ALU.mult, [ktb, kc_], [ktb])
                    yield
                    ACT(ta[:], a_[:], AF.Square, [ka], [kta])
                    yield
                    ACT(ta[:], ta[:], AF.Sqrt, [kta], [kta], scale=-1.0, bias=1.0)
                    yield
                    TT_("dve", tb[:], ta[:], tb[:], ALU.mult, [kta, ktb], [ktb])
                    yield
                    for si, (t0, L_) in enumerate(seqs):
                        init = H0[:, d, c:c + 1] if g == 1 else 0.0
                        if d == 0:
                            o_, a2, u2 = tb[:, t0:t0 + L_], a_[:, t0:t0 + L_], tb[:, t0:t0 + L_]
                        else:
                            o_, a2, u2 = tb[:, t0:t0 + L_][:, ::-1], a_[:, t0:t0 + L_][:, ::-1], tb[:, t0:t0 + L_][:, ::-1]
                        S.op("dve", (lambda e, o_=o_, a2=a2, u2=u2, init=init: e.tensor_tensor_scan(
                            out=o_, data0=a2, data1=u2, initial=init, op0=ALU.mult, op1=ALU.add)), [ka, ktb, "h0"], [ktb])
                        if g == 0:
                            col = t0 + L_ - 1 if d == 0 else t0
                            CP("act", FIN[:, d, si, c:c + 1], tb[:, col:col + 1], [ktb], ["fin"])
                        yield
                tb0, tb1 = B_["tb"][0], B_["tb"][1]
                TT_("dve", tb0[:], tb0[:], tb1[:], ALU.add, [K_("tb0"), K_("tb1")], [K_("tb0")])
                yield
                TT_("dve", OB128[:, c, :], tb0[:], ga[:], ALU.mult, [K_("tb0"), kga], ["ob"])
                yield
            def run_rr2(gens):
                gens = list(gens)
                while gens:
                    nxt = []
                    for g_ in gens:
                        try:
                            next(g_)
                            nxt.append(g_)
                        except StopIteration:
                            pass
                    gens = nxt
            for half in range(2):
                Wx, wxk = wload(cin[j, half])
                Wg, wgk = wload(cin[j, 2 + half])
                Wxv = Wx[:].rearrange("p (k m) -> p k m", k=8)
                Wgv = Wg[:].rearrange("p (k m) -> p k m", k=8)
                for c0_ in range(0, 4, NSL):
                    run_rr2([chunk_gen(s_, half * 4 + c0_ + s_, c0_ + s_, Wxv, Wgv, wxk, wgk) for s_ in range(NSL)])
            if g == 0:
                for d in range(2):
                    for si in range(4):
                        S.dma("sp", o_lru[j, d, si], FIN[:, d, si, :], reads=["fin"], writes=["o_lru"])
            for mb in range(2):
                W, wk = wload(cout[j, mb])
                Wv = W[:].rearrange("p (k m) -> p k m", k=8)
                for mi in range(4):
                    m = mb * 4 + mi
                    for tt in range(2):
                        sl = slice(tt * TT, (tt + 1) * TT)
                        ps, pk = newps()
                        MM(ps[:], [(Wv[:, kc, mi * 128:(mi + 1) * 128], OB128[:, kc, sl]) for kc in range(8)], [wk, "ob"], [pk])
                        STT(Y[:, m, sl], ps[:], mod(l, g, 2)[:, m:m + 1], Y[:, m, sl], ALU.mult, ALU.add, [pk, "y", "mod%d" % l], ["y"])

        DNP = {}
        ONESBD = sb([128, 128], BF16)
        MEMSET("dve", ONESBD[:], 0.0, ["onesbd"])
        MEMSET("dve", ONESBD[0:64, 0:64], 1.0, ["onesbd"])
        MEMSET("dve", ONESBD[64:128, 64:128], 1.0, ["onesbd"])
        def deltanet(l, g, ctx0):
            j = l // 2
            seqs = [(s * 256, 4) for s in range(4)] if g == 0 else [(0, 16)]
            P16 = sb([16, 2], F32, ctx0)
            S.dma("sp", P16[:], abp16[j], writes=["p16"])
            P128 = sb([128, 61], F32, ctx0)
            S.dma("sp", P128[:], abp128[j], writes=["p128"])
            DNP["CW2"] = P128[:, 0:48].rearrange("p (h k) -> p h k", k=4)
            DNP["CB2"] = P128[:, 48:60]
            DNP["DNG2"] = P128[:, 60:61]
            BETA = sb([128, NT], F32, ctx0)
            GCF = sb([128, NT], F32, ctx0)
            GCB = sb([128, NT], F32, ctx0)
            NEA = sb([16, 1], F32, ctx0)
            cs_ = ExitStack()
            G = sb([16, NT], F32, cs_)
            CM = sb([16, 2048], F32, cs_)
            S.dma("sp", CM[:], cmask[:, :], writes=["cm"])
            CMF = CM[:, 0:1024]
            CMB = CM[:, 1024:2048]
            MEMSET("dve", BETA[:], 0.0, ["beta"])
            MEMSET("dve", GCF[:], 0.0, ["gcf"])
            MEMSET("dve", GCB[:], 0.0, ["gcb"])
            ACT(NEA[:], P16[:, 0:1], AF.Exp, ["p16"], ["nea"])
            TS("dve", NEA[:], NEA[:], -1.0, None, ALU.mult, ALU.bypass, ["nea"], ["nea"])
            W5, w5k = wload(abin[j, 5])
            W5v = W5[:].rearrange("p (k m) -> p k m", k=8)
            for tt in range(2):
                sl = slice(tt * TT, (tt + 1) * TT)
                ps, pk = newps()
                MM(ps[0:16, :], [(W5v[:, kc, 0:16], HT[:, kc, sl]) for kc in range(8)], [w5k, "ht"], [pk])
                ACT(G[:, sl], ps[0:16, :], AF.Exp, [pk, "p16"], ["g"], bias=P16[:, 1:2], scale=1.0)
                ACT(G[:, sl], G[:, sl], AF.Ln, ["g"], ["g"], bias=1.0, scale=1.0)
                TS("dve", G[:, sl], G[:, sl], NEA[:, 0:1], None, ALU.mult, ALU.bypass, ["g", "nea"], ["g"])
                ps, pk = newps()
                MM(ps[0:16, :], [(W5v[:, kc, 16:32], HT[:, kc, sl]) for kc in range(8)], [w5k, "ht"], [pk])
                ACT(BETA[0:16, sl], ps[0:16, :], AF.Sigmoid, [pk, "beta"], ["beta"])
            S.op("dve", lambda e: e.tensor_tensor_scan(out=GCF[0:16, :], data0=CMF, data1=G[:], initial=0.0, op0=ALU.mult, op1=ALU.add),
                 ["g", "cm", "gcf"], ["gcf"])
            S.op("dve", lambda e: e.tensor_tensor_scan(out=GCB[0:16, :][:, ::-1], data0=CMB[:, ::-1], data1=G[:, ::-1], initial=0.0,
                                                       op0=ALU.mult, op1=ALU.add), ["g", "cm", "gcb"], ["gcb"])
            CP("act", BETA[64:80, :], BETA[0:16, :], ["beta"], ["beta"])
            CP("act", GCF[64:80, :], GCF[0:16, :], ["gcf"], ["gcf"])
            CP("act", GCB[64:80, :], GCB[0:16, :], ["gcb"], ["gcb"])
            S.barrier()
            cs_.close()
            if DN_STAGE < 2:
                return
            for hh in range(2):
                with ExitStack() as ctx:
                    deltanet_half(l, g, j, hh, seqs, ctx, BETA, GCF, GCB)
                S.barrier()

        def deltanet_half(l, g, j, hh, seqs, ctx, BETA, GCF, GCB):
            CW2, CB2, DNG2 = DNP["CW2"], DNP["CB2"], DNP["DNG2"]
            HW_ = 128
            QT = sb([128, 2, NT], BF16, ctx)
            KT = sb([128, 2, NT], BF16, ctx)
            VT = sb([128, 2, NT], BF16, ctx)
            GATE = sb([128, 2, NT], BF16, ctx)
            OACC = sb([128, 2, NT], F32, ctx)
            SQ = sb([128, TT], BF16, ctx)
            RS = sb([128, TT], F32, ctx)
            c2 = ExitStack()
            SQ2 = [SQ, sb([128, TT], BF16, c2)]
            RS2 = [RS, sb([128, TT], F32, c2)]
            RAW = [sb([128, NT], F32, c2) for _ in range(2)]
            CV = [sb([128, NT], F32, c2) for _ in range(2)]
            nseq = 4 if g == 0 else 1
            L = NT // nseq
            n = 0
            for blk in range(4):
                W, wk = wload(abin[j, blk])
                Wv = W[:].rearrange("p (k m) -> p k m", k=8)
                for pr in range(2):
                    h0 = hh * NH + 2 * pr
                    b = n % 2; n += 1
                    raw, cv = RAW[b], CV[b]
                    kr, kv_ = "raw%d" % b, "cv%d" % b
                    for tt in range(2):
                        sl = slice(tt * TT, (tt + 1) * TT)
                        ps, pk = newps()
                        MM(ps[:], [(Wv[:, kc, h0 * 64:(h0 + 2) * 64], HT[:, kc, sl]) for kc in range(8)], [wk, "ht"], [pk])
                        if blk == 3:
                            ACT(GATE[:, pr, sl], ps[:], AF.Silu, [pk], ["gate"])
                        else:
                            CP("act", raw[:, sl], ps[:], [pk], [kr])
                    if blk == 3:
                        continue
                    pbi = blk * 4 + hh * 2 + pr
                    TS("dve", cv[:], raw[:], CW2[:, pbi, 2:3], CB2[:, pbi:pbi + 1], ALU.mult, ALU.add, [kr, "p128"], [kv_])
                    r3 = raw[:].rearrange("p (s t) -> p s t", s=nseq)
                    c3 = cv[:].rearrange("p (s t) -> p s t", s=nseq)
                    for tap, o in ((0, -2), (1, -1), (3, 1)):
                        d0, d1 = max(0, -o), L - max(0, o)
                        STT(c3[:, :, d0:d1], r3[:, :, d0 + o:d1 + o], CW2[:, pbi, tap:tap + 1], c3[:, :, d0:d1], ALU.mult, ALU.add,
                            [kr, kv_, "p128"], [kv_])
                    if blk == 2:
                        ACT(VT[:, pr, :], cv[:], AF.Silu, [kv_], ["vt"])
                        continue
                    ACT(cv[:], cv[:], AF.Silu, [kv_], [kv_])
                    dst = QT if blk == 0 else KT
                    dk_ = "qt" if blk == 0 else "kt"
                    sls = [slice(tt * TT, (tt + 1) * TT) for tt in range(2)]
                    pss = []
                    for tt in range(2):
                        ACT(SQ2[tt][:], cv[:, sls[tt]], AF.Square, [kv_], ["dsq%d" % tt])
                    for tt in range(2):
                        ps, pk = newps()
                        pss.append((ps, pk))
                        MM(ps[:], [(ONESBD[:], SQ2[tt][:])], ["dsq%d" % tt, "onesbd"], [pk])
                    for tt in range(2):
                        ACT(RS2[tt][:], pss[tt][0][:], AF.Ln, [pss[tt][1]], ["drs%d" % tt], bias=EPS, scale=1.0)
                    for tt in range(2):
                        ACT(RS2[tt][:], RS2[tt][:], AF.Exp, ["drs%d" % tt], ["drs%d" % tt], scale=-0.5)
                    for tt in range(2):
                        STT(dst[:, pr, sls[tt]], cv[:, sls[tt]], (0.125 if blk == 0 else 1.0), RS2[tt][:], ALU.mult, ALU.mult,
                            [kv_, "drs%d" % tt], [dk_])
            S.barrier()
            c2.close()
            if DN_STAGE < 3:
                return
            KS = 4
            def mk(shape, dt=F32):
                return sb(shape, dt, ctx)
            SL = []
            for k in range(KS):
                SL.append(dict(A=mk([128, HW_]), Bm=mk([128, HW_]), EGT=mk([128, HW_]),
                               KBc=mk([128, HW_], BF16), QGc=mk([128, HW_], BF16), INTR=mk([128, HW_], BF16), KDEC=mk([128, HW_], BF16),
                               VTOK=mk([128, HW_], BF16), KTOK=mk([128, HW_], BF16), QX=mk([128, 2 * HW_]), QT=mk([128, HW_]),
                               TKS=mk([128, 10]), SELGL=mk([128, 2]), RT=mk([128, HW_]), VNB=mk([128, HW_], BF16)))
            for k in range(KS):
                MEMSET("dve", SL[k]["SELGL"][:], 0.0, ["selgl_%d" % k])
            SS = {}
            for si in range(min(2, len(seqs))):
                for d in range(2):
                    SS[(si, d)] = (sb([128, HW_], F32, ctx), sb([128, HW_], BF16, ctx))
            def init_state(si):
                for d in range(2):
                    Sf, Sb_ = SS[(si % 2, d)]
                    sk = ("S", si % 2, d)
                    if g == 1:
                        S.dma("sp", Sf[:], sdelta[j, d, hh], writes=[sk])
                    else:
                        MEMSET("dve", Sf[:], 0.0, [sk])
                    CP("act", Sb_[:], Sf[:], [sk], [("Sb", si % 2, d)])
            oacc_written = set()
            def h2(ap):
                return ap.rearrange("p (i x) -> p i x", i=2)
            def bc(ap):
                return ap.unsqueeze(2).to_broadcast([128, 2, 64])
            GP = ((0, slice(0, 64), slice(0, 16)), (1, slice(64, 128), slice(64, 80)))
            def geom(si, d, ck):
                t0, nch = seqs[si]
                chunk = ck if d == 0 else nch - 1 - ck
                c0 = t0 + chunk * 64
                return chunk, c0, slice(c0, c0 + 64)
            def partA(k, si, d, ck):
                sl_ = SL[k]
                X, Xk = PSB[2 * k], "ps%d" % (2 * k)
                Yb, Yk = PSB[2 * k + 1], "ps%d" % (2 * k + 1)
                chunk, c0, cs = geom(si, d, ck)
                GC = GCF if d == 0 else GCB
                gck = "gcf" if d == 0 else "gcb"
                last = 63 if d == 0 else 0
                NEGM = NEG_LO2 if d == 0 else NEG_UP2
                NEGMT = NEG_UP2 if d == 0 else NEG_LO2
                K_ = lambda n_: "%s_%d" % (n_, k)
                A, Bm, EGT = sl_["A"], sl_["Bm"], sl_["EGT"]
                KBc, QGc, INTR, KDEC, VTOK, KTOK = sl_["KBc"], sl_["QGc"], sl_["INTR"], sl_["KDEC"], sl_["VTOK"], sl_["KTOK"]
                QX, QT_, tks, SELGL = sl_["QX"], sl_["QT"], sl_["TKS"], sl_["SELGL"]
                blk = BLK2(d, hh)
                sel = SEL2(d, hh)
                TT_("pool", h2(A[0:80, :]), GC[0:80, cs].unsqueeze(1).to_broadcast([80, 2, 64]), h2(blk[0:80, :]), ALU.mult, [gck, "cs"], [K_("A")])
                TT_("pool", h2(Bm[0:80, :]), BETA[0:80, cs].unsqueeze(1).to_broadcast([80, 2, 64]), h2(blk[0:80, :]), ALU.mult, ["beta", "cs"], [K_("Bm")])
                TS("pool", SELGL[0:80, :], sel[0:80, :], GC[0:80, c0 + last:c0 + last + 1], None, ALU.mult, ALU.bypass, [gck, "cs"], [K_("selgl")])
                MMS([(X[pr_, i * 64:(i + 1) * 64], VT[pr_, i, cs], IDB2[pr_, :]) for (gp, pr_, p16) in GP for i in range(2)], ["vt", "idb2"], [Xk])
                MMS([(Yb[pr_, i * 64:(i + 1) * 64], KT[pr_, i, cs], IDB2[pr_, :]) for (gp, pr_, p16) in GP for i in range(2)], ["kt", "idb2"], [Yk])
                yield
                CP("act", VTOK[:], X[:, 0:HW_], [Xk], [K_("vtok")])
                CP("dve", KTOK[:], Yb[:, 0:HW_], [Yk], [K_("ktok")])
                yield
                def fE(e):
                    ins = None
                    for (gp, pr_, p16) in GP:
                        e.matmul(X[pr_, 0:HW_], lhsT=GC[p16, cs], rhs=blk[p16, :], start=True, stop=False)
                        e.matmul(X[pr_, 0:HW_], lhsT=NEGONES16[p16, :], rhs=A[p16, :], start=False, stop=True)
                        e.matmul(X[pr_, HW_:HW_ + 2], lhsT=BETA[p16, cs], rhs=sel[p16, :], start=True, stop=True)
                        e.matmul(X[pr_, HW_ + 2:HW_ + 4], lhsT=GC[p16, cs], rhs=sel[p16, :], start=True, stop=True)
                        ins = e.matmul(X[pr_, HW_ + 4:HW_ + 6], lhsT=ONES16[p16, :], rhs=SELGL[p16, :], start=True, stop=True)
                    return ins
                S.op("pe", fE, [gck, "beta", "cs", "ones16", K_("A"), K_("selgl")], [Xk])
                MMS([(Yb[pr_, 0:HW_], ONES16[p16, :], Bm[p16, :]) for (gp, pr_, p16) in GP] +
                    [(Yb[pr_, HW_:2 * HW_], ONES16[p16, :], A[p16, :]) for (gp, pr_, p16) in GP], ["ones16", K_("A"), K_("Bm")], [Yk])
                yield
                ACT(EGT[:], Yb[:, HW_:2 * HW_], AF.Exp, [Yk], [K_("EGT")])
                TT_("dve", h2(KBc[:]), KT[:, :, cs], h2(Yb[:, 0:HW_]), ALU.mult, ["kt", Yk], [K_("kbc")])
                ACT(tks[:, 2:6], X[:, HW_ + 2:HW_ + 6], AF.Exp, [Xk], [K_("tks")])
                ACT(tks[:, 6:8], h2(X[:, 0:HW_])[:, :, last], AF.Exp, [Xk], [K_("tks")], scale=-1.0)
                CP("dve", tks[:, 0:2], X[:, HW_:HW_ + 2], [Xk], [K_("tks")])
                yield
                TT_("dve", A[:], X[:, 0:HW_], NEGM, ALU.add, [Xk, "cd"], [K_("A")])
                STT(Bm[:], X[:, 0:HW_], -1.0, NEGMT, ALU.mult, ALU.add, [Xk, "cd"], [K_("Bm")])
                yield
                ACT(A[:], A[:], AF.Exp, [K_("A")], [K_("A")])
                ACT(Bm[:], Bm[:], AF.Exp, [K_("Bm")], [K_("Bm")])
                TT_("pool", h2(QGc[:]), QT[:, :, cs], h2(EGT[:]), ALU.mult, ["qt", K_("EGT")], [K_("qgc")])
                TT_("pool", h2(KDEC[:]), h2(KTOK[:]), bc(tks[:, 6:8]), ALU.mult, [K_("ktok"), K_("tks")], [K_("kdec")])
                yield
                TT_("pool", tks[:, 8:10], tks[:, 0:2], tks[:, 2:4], ALU.mult, [K_("tks")], [K_("tks")])
                kb3 = h2(KBc[:])
                MMS([(Yb[pr_, i * 64:(i + 1) * 64], kb3[pr_, i, :], KT[pr_, i, cs]) for (gp, pr_, p16) in GP for i in range(2)] +
                    [(Yb[pr_, HW_ + i * 64:HW_ + (i + 1) * 64], KT[pr_, i, cs], kb3[pr_, i, :]) for (gp, pr_, p16) in GP for i in range(2)],
                    [K_("kbc"), "kt"], [Yk])
                MMS([(X[pr_, i * 64:(i + 1) * 64], KT[pr_, i, cs], QT[pr_, i, cs]) for (gp, pr_, p16) in GP for i in range(2)], ["kt", "qt"], [Xk])
                yield
                qx4 = QX[:].rearrange("p (i two x) -> p i two x", i=2, two=2)
                qx3 = QX[:].rearrange("p (i y) -> p i y", i=2)
                qt3 = h2(QT_[:])
                STT(qt3, h2(Yb[:, 0:HW_]), -1.0, h2(A[:]), ALU.mult, ALU.mult, [Yk, K_("A")], [K_("qtn")])
                STT(qx4[:, :, 0, :], h2(Yb[:, HW_:2 * HW_]), -1.0, h2(Bm[:]), ALU.mult, ALU.mult, [Yk, K_("Bm")], [K_("qx")])
                yield
                TT_("pool", Bm[:], Bm[:], IDENT2, ALU.add, [K_("Bm"), "cd"], [K_("Bm")])
                TT_("dve", INTR[:], X[:, 0:HW_], Bm[:], ALU.mult, [Xk, K_("Bm")], [K_("intr")])
                TT_("pool", h2(A[:]), h2(VTOK[:]), bc(tks[:, 0:2]), ALU.mult, [K_("vtok"), K_("tks"), K_("A")], [K_("A")])
                yield
                for lev in range(5):
                    if lev == 0:
                        MMS([(X[pr_, i * 64:(i + 1) * 64], qt3[pr_, i, :], qx4[pr_, i, 0, :]) for (gp, pr_, p16) in GP for i in range(2)],
                            [K_("qtn"), K_("qx")], [Xk])
                    elif lev == 4:
                        MMS([(X[pr_, i * 64:(i + 1) * 64], qt3[pr_, i, :], qx4[pr_, i, 1, :]) for (gp, pr_, p16) in GP for i in range(2)],
                            [K_("qtn"), K_("qx")], [Xk])
                    else:
                        MMS([(X[pr_, i * 128:(i + 1) * 128], qt3[pr_, i, :], qx3[pr_, i, :]) for (gp, pr_, p16) in GP for i in range(2)],
                            [K_("qtn"), K_("qx")], [Xk])
                    MMS([(Yb[pr_, i * 64:(i + 1) * 64], qx4[pr_, i, 0, :], qt3[pr_, i, :]) for (gp, pr_, p16) in GP for i in range(2)],
                        [K_("qtn"), K_("qx")], [Yk])
                    yield
                    x4 = X[:, 0:2 * HW_].rearrange("p (i two x) -> p i two x", i=2, two=2)
                    if lev == 0:
                        TT_("pool", qx4[:, :, 1, :], qx4[:, :, 0, :], h2(IDENT2), ALU.add, [K_("qx"), "cd"], [K_("qx")])
                        CP("act", qx4[:, :, 0, :], h2(X[:, 0:HW_]), [Xk], [K_("qx")])
                    elif lev == 4:
                        TT_("dve", qx4[:, :, 1, :], h2(X[:, 0:HW_]), qx4[:, :, 1, :], ALU.add, [Xk, K_("qx")], [K_("qx")])
                    else:
                        CP("act", qx4[:, :, 0, :], x4[:, :, 0, :], [Xk], [K_("qx")])
                        TT_("dve", qx4[:, :, 1, :], x4[:, :, 1, :], qx4[:, :, 1, :], ALU.add, [Xk, K_("qx")], [K_("qx")])
                    CP("act", QT_[:], Yb[:, 0:HW_], [Yk], [K_("qtn")])
                    yield
                MMS([(X[pr_, i * 64:(i + 1) * 64], qt3[pr_, i, :], qx4[pr_, i, 1, :]) for (gp, pr_, p16) in GP for i in range(2)],
                    [K_("qtn"), K_("qx")], [Xk])
                yield
                TT_("dve", h2(EGT[:]), h2(X[:, 0:HW_]), qx4[:, :, 1, :], ALU.add, [Xk, K_("qx"), K_("EGT")], [K_("EGT")])
                yield

            def partB(k, si, d, ck):
                sl_ = SL[k]
                X, Xk = PSB[2 * k], "ps%d" % (2 * k)
                Yb, Yk = PSB[2 * k + 1], "ps%d" % (2 * k + 1)
                chunk, c0, cs = geom(si, d, ck)
                K_ = lambda n_: "%s_%d" % (n_, k)
                BV, QGc, INTR, KDEC = sl_["A"], sl_["QGc"], sl_["INTR"], sl_["KDEC"]
                tks, RT, VNB, TTB, SD = sl_["TKS"], sl_["RT"], sl_["VNB"], sl_["EGT"], sl_["Bm"]
                Sf, Sb_ = SS[(si % 2, d)]
                sk, sbk = ("S", si % 2, d), ("Sb", si % 2, d)
                Sb3, Sf3 = h2(Sb_[:]), h2(Sf[:])
                MMS([(X[pr_, i * 64:(i + 1) * 64], KT[pr_, i, cs], Sb3[pr_, i, :]) for (gp, pr_, p16) in GP for i in range(2)], ["kt", sbk], [Xk])
                TT_("pool", h2(SD[:]), Sf3, bc(tks[:, 4:6]), ALU.mult, [sk, K_("tks"), K_("Bm")], [K_("Bm")])
                yield
                TT_("dve", h2(RT[:]), h2(X[:, 0:HW_]), bc(tks[:, 8:10]), ALU.mult, [Xk, K_("tks")], [K_("rt")])
                yield
                TT_("dve", RT[:], BV[:], RT[:], ALU.subtract, [K_("A"), K_("rt")], [K_("rt")])
                yield
                tt3, r3 = h2(TTB[:]), h2(RT[:])
                MMS([(Yb[pr_, i * 64:(i + 1) * 64], tt3[pr_, i, :], r3[pr_, i, :]) for (gp, pr_, p16) in GP for i in range(2)], [K_("EGT"), K_("rt")], [Yk])
                yield
                CP("act", VNB[:], Yb[:, 0:HW_], [Yk], [K_("vnb")])
                yield
                vn3, qg3, in3, kd3 = h2(VNB[:]), h2(QGc[:]), h2(INTR[:]), h2(KDEC[:])
                MMS([(Yb[pr_, i * 64:(i + 1) * 64], kd3[pr_, i, :], vn3[pr_, i, :]) for (gp, pr_, p16) in GP for i in range(2)], [K_("kdec"), K_("vnb")], [Yk])
                def fo(e):
                    ins = None
                    for (gp, pr_, p16) in GP:
                        for i in range(2):
                            e.matmul(X[pr_, i * 64:(i + 1) * 64], lhsT=Sb3[pr_, i, :], rhs=qg3[pr_, i, :], start=True, stop=False)
                            ins = e.matmul(X[pr_, i * 64:(i + 1) * 64], lhsT=vn3[pr_, i, :], rhs=in3[pr_, i, :], start=False, stop=True)
                    return ins
                S.op("pe", fo, [sbk, K_("qgc"), K_("vnb"), K_("intr")], [Xk])
                yield
                TT_("dve", Sb_[:], Yb[:, 0:HW_], SD[:], ALU.add, [Yk, K_("Bm")], [sbk])
                TT_("dve", Sf[:], Yb[:, 0:HW_], SD[:], ALU.add, [Yk, K_("Bm")], [sk])
                ok_ = ("oacc", si, chunk)
                if ok_ not in oacc_written:
                    oacc_written.add(ok_)
                    CP("act", OACC[:, :, cs], h2(X[:, 0:HW_]), [Xk], [ok_])
                else:
                    TT_("dve", OACC[:, :, cs], h2(X[:, 0:HW_]), OACC[:, :, cs], ALU.add, [Xk, ok_], [ok_])
                yield

            def run_rr(gens):
                gens = list(gens)
                while gens:
                    nxt = []
                    for g_ in gens:
                        try:
                            next(g_)
                            nxt.append(g_)
                        except StopIteration:
                            pass
                    gens = nxt

            for sp_ in range(0, len(seqs), 2):
                sis = [si for si in (sp_, sp_ + 1) if si < len(seqs)]
                for si in sis:
                    init_state(si)
                nck = seqs[sis[0]][1] if DN_STAGE > 3 else 1
                steps = [(si, d, ck) for ck in range(nck) for si in sis for d in range(2)]
                for w0 in range(0, len(steps), KS):
                    win = steps[w0:w0 + KS]
                    run_rr([partA(k, *st) for k, st in enumerate(win)])
                    pending = list(enumerate(win))
                    while pending:
                        seen, rnd, rest = set(), [], []
                        for k, st in pending:
                            ch = (st[0], st[1])
                            if ch in seen:
                                rest.append((k, st))
                            else:
                                seen.add(ch)
                                rnd.append((k, st))
                        run_rr([partB(k, *st) for k, st in rnd])
                        pending = rest
                if g == 0:
                    for si in sis:
                        for d in range(2):
                            S.dma("sp", o_dn[si, j, d, hh], SS[(si % 2, d)][0][:], reads=[("S", si % 2, d)], writes=["o_dn"])
            if DN_STAGE < 9:
                return
            allo = [("oacc", si, c) for si in range(len(seqs)) for c in range(seqs[si][1])]
            TO = sb([128, TT], F32, ctx)
            for i in range(2):
                for tt in range(2):
                    sl = slice(tt * TT, (tt + 1) * TT)
                    ACT(SQ[:], OACC[:, i, sl], AF.Square, allo, ["dsq0"])
                    ps, pk = newps()
                    MM(ps[:], [(ONESBD[:], SQ[:])], ["dsq0", "onesbd"], [pk])
                    ACT(RS[:], ps[:], AF.Ln, [pk], ["drs0"], bias=EPS, scale=1.0 / 64)
                    ACT(RS[:], RS[:], AF.Exp, ["drs0"], ["drs0"], scale=-0.5)
                    STT(TO[:], OACC[:, i, sl], DNG2, RS[:], ALU.mult, ALU.mult, allo + ["drs0", "p128"], ["to"])
                    TT_("dve", OBt[:, hh * 2 + i, sl], TO[:], GATE[:, i, sl], ALU.mult, ["to", "gate"], ["ob"])

        def attention(l, g, ctx):
            j = l // 2
            P64 = sb([64, 24 * 5 + 11], F32, ctx)
            S.dma("sp", P64[:], abp64[j], writes=["p64b"])
            QG, KG = P64[:, 121:122], P64[:, 122:123]
            ESINK = sb([64, 8], F32, ctx)
            ACT(ESINK[:], P64[:, 123:131], AF.Exp, ["p64b"], ["esink"])
            QB = sb([64, 8, NT], BF16, ctx)
            KB = sb([64, 2, NT], BF16, ctx)
            KN = sb([64, 2, NT], F32, ctx)
            VB = sb([128, 8, 128], BF16, ctx)
            VF = sb([128, 8, 128], F32, ctx)
            SQa = [sb([128, TT], BF16, ctx) for _ in range(2)]
            RSa = [sb([128, TT], F32, ctx) for _ in range(2)]
            QN = [sb([128, TT], F32, ctx) for _ in range(2)]
            QNBa = [sb([128, TT], BF16, ctx) for _ in range(2)]
            T1a = [sb([128, TT], F32, ctx) for _ in range(2)]
            T2a = [sb([128, TT], F32, ctx) for _ in range(2)]
            PERMB = sb([128, 128], BF16, ctx)
            G128 = sb([128, 2], F32, ctx)
            if g == 1:
                CSN = sb([128, 2048], F32, ctx)
                for hf in range(2):
                    S.dma("sp", CSN[hf * 64:(hf + 1) * 64, :], cossin[:, :], writes=["csn"])
                COS = CSN[:, 0:1024]
                SIN = CSN[:, 1024:2048]
            MEMSET("dve", PERMB[:], 0.0, ["permb"])
            CP("dve", PERMB[0:64, 0:64], PERMT, ["c64", "permb"], ["permb"])
            CP("dve", PERMB[64:128, 64:128], PERMT, ["c64", "permb"], ["permb"])
            for hf in range(2):
                S.dma("sp", G128[hf * 64:(hf + 1) * 64, :], abp64[j][:, 121:123], writes=["g128"])
            W4, w4k = wload(abin[j, 4])
            W5, w5k = wload(abin[j, 5])
            W4v = W4[:].rearrange("p (k m) -> p k m", k=8)
            W5v = W5[:].rearrange("p (k m) -> p k m", k=8)
            n = 0
            for pp in range(5):
                sls = [slice(tt * TT, (tt + 1) * TT) for tt in range(2)]
                if pp < 4:
                    gn = G128[:, 0:1]
                    dkey = "qb"
                    dsts = [[QB[:, 2 * pp, sls[tt]], QB[:, 2 * pp + 1, sls[tt]]] for tt in range(2)]
                else:
                    gn = G128[:, 1:2]
                    dkey = "kb"
                    dsts = [[KB[:, 0, sls[tt]], KB[:, 1, sls[tt]]] for tt in range(2)]
                pss, pss2, pss3 = [], [], []
                for tt in range(2):
                    ps, pk = newps()
                    pss.append((ps, pk))
                    if pp < 4:
                        MM(ps[:], [(W4v[:, kc, pp * 128:(pp + 1) * 128], HT[:, kc, sls[tt]]) for kc in range(8)], [w4k, "ht"], [pk])
                    else:
                        MM(ps[:], [(W5v[:, kc, 32:160], HT[:, kc, sls[tt]]) for kc in range(8)], [w5k, "ht"], [pk])
                for tt in range(2):
                    ACT(SQa[tt][:], pss[tt][0][:], AF.Square, [pss[tt][1]], ["asq%d" % tt])
                for tt in range(2):
                    ps2, pk2 = newps()
                    pss2.append((ps2, pk2))
                    MM(ps2[:], [(ONESBD[:], SQa[tt][:])], ["asq%d" % tt, "onesbd"], [pk2])
                for tt in range(2):
                    ACT(RSa[tt][:], pss2[tt][0][:], AF.Ln, [pss2[tt][1]], ["ars%d" % tt], bias=EPS, scale=1.0 / 64)
                for tt in range(2):
                    ACT(RSa[tt][:], RSa[tt][:], AF.Exp, ["ars%d" % tt], ["ars%d" % tt], scale=-0.5)
                for tt in range(2):
                    STT(QN[tt][:], pss[tt][0][:], gn, RSa[tt][:], ALU.mult, ALU.mult, [pss[tt][1], "ars%d" % tt, "g128"], ["qn%d" % tt])
                if g == 0:
                    for tt in range(2):
                        for hf in range(2):
                            CP("act", dsts[tt][hf], QN[tt][hf * 64:(hf + 1) * 64, :], ["qn%d" % tt], [dkey])
                            if pp == 4:
                                CP("act", KN[:, hf, sls[tt]], QN[tt][hf * 64:(hf + 1) * 64, :], ["qn%d" % tt], ["kn"])
                else:
                    for tt in range(2):
                        CP("act", QNBa[tt][:], QN[tt][:], ["qn%d" % tt], ["qnb%d" % tt])
                    for tt in range(2):
                        ps3, pk3 = newps()
                        pss3.append((ps3, pk3))
                        MM(ps3[:], [(PERMB[:], QNBa[tt][:])], ["permb", "qnb%d" % tt], [pk3])
                    for tt in range(2):
                        TT_("dve", T1a[tt][:], QN[tt][:], COS[:, sls[tt]], ALU.mult, ["qn%d" % tt, "csn"], ["at1%d" % tt])
                    for tt in range(2):
                        TT_("dve", T2a[tt][:], pss3[tt][0][:], SIN[:, sls[tt]], ALU.mult, [pss3[tt][1], "csn"], ["at2%d" % tt])
                    for tt in range(2):
                        for hf in range(2):
                            TT_("dve", dsts[tt][hf], T1a[tt][hf * 64:(hf + 1) * 64, :], T2a[tt][hf * 64:(hf + 1) * 64, :], ALU.add,
                                ["at1%d" % tt, "at2%d" % tt], [dkey])
            if g == 0:
                S.dma("sp", o_k[j], KN[:], reads=["kn"], writes=["o_k"])
            for tb in range(8):
                ps, pk = newps()
                MM(ps[:, 0:128], [(HT[:, kc, tb * 128:(tb + 1) * 128], W5v[:, kc, 160:288]) for kc in range(8)], [w5k, "ht"], [pk])
                CP("act", VB[:, tb, :], ps[:, 0:128], [pk], ["vb"])
                if g == 0:
                    CP("dve", VF[:, tb, :], ps[:, 0:128], [pk], ["vf"])
            if g == 0:
                S.dma("sp", o_v[j], VF[:], reads=["vf"], writes=["o_v"])
            if g == 1:
                KCF = sb([64, 2, 256], F32, ctx)
                KC = sb([64, 2, 256], BF16, ctx)
                VCF = sb([128, 2, 128], F32, ctx)
                VC = sb([128, 2, 128], BF16, ctx)
                S.dma("sp", KCF[:], kcT[j], writes=["kcf"])
                S.dma("sp", VCF[:], vc[j], writes=["vcf"])
                CP("dve", KC[:], KCF[:], ["kcf"], ["kc"])
                CP("dve", VC[:], VCF[:], ["vcf"], ["vcb"])
            PT = [sb([128, 5, 512], BF16, ctx) for _ in range(2)]
            DEN = sb([64, 512], F32, ctx)
            n = 0
            for qb in range(8):
                qs = slice(qb * 128, (qb + 1) * 128)
                for kv in range(2):
                    if g == 0:
                        s = qb // 2
                        kbl = [("lat", 2 * s, None), ("lat", 2 * s + 1, None)]
                    else:
                        kbl = []
                        if qb > 0:
                            kbl.append(("lat", qb - 1, MASK_LO))
                        kbl.append(("lat", qb, None))
                        if qb < 7:
                            kbl.append(("lat", qb + 1, MASK_UP))
                        kbl += [("ctx", 0, None), ("ctx", 1, None)]
                    pt = PT[n % 2]; ptk = "pt%d" % (n % 2); n += 1
                    for bi, (kind, kb_, msk) in enumerate(kbl):
                        ps, pk = newps()
                        if kind == "lat":
                            lh, lk = KB[:, kv, kb_ * 128:(kb_ + 1) * 128], "kb"
                        else:
                            lh, lk = KC[:, kv, kb_ * 128:(kb_ + 1) * 128], "kc"
                        MM(ps[:].rearrange("p (h q) -> p h q", h=4), [(lh, QB[:, 4 * kv:4 * kv + 4, qs])], [lk, "qb"], [pk])
                        ACT(pt[:, bi, :], ps[:], AF.Exp, [pk], [(ptk, bi)], scale=0.125)
                        if msk is not None:
                            TT_("dve", pt[:, bi, :].rearrange("p (h q) -> p h q", h=4), pt[:, bi, :].rearrange("p (h q) -> p h q", h=4),
                                msk.unsqueeze(1).to_broadcast([128, 4, 128]), ALU.mult, [(ptk, bi), "c128"], [(ptk, bi)])
                    nb = len(kbl)
                    pv, pvk = newps()
                    prs = []
                    for bi, (kind, kb_, msk) in enumerate(kbl):
                        if kind == "lat":
                            prs.append((VB[:, kb_, kv * 64:(kv + 1) * 64], pt[:, bi, :]))
                        else:
                            prs.append((VC[:, kb_, kv * 64:(kv + 1) * 64], pt[:, bi, :]))
                    MM(pv[0:64, :], prs, ["vb", "vcb"] + [(ptk, bi) for bi in range(nb)], [pvk])
                    pd, pdk = newps()
                    MM(pd[0:64, :], [(ONESB[:, 0:64], pt[:, bi, :]) for bi in range(nb)], ["onesb"] + [(ptk, bi) for bi in range(nb)], [pdk])
                    TT_("dve", DEN[:].rearrange("p (h q) -> p h q", h=4), pd[0:64, :].rearrange("p (h q) -> p h q", h=4),
                        ESINK[:, 4 * kv:4 * kv + 4].unsqueeze(2).to_broadcast([64, 4, 128]), ALU.add, [pdk, "esink"], ["den"])
                    ACT(DEN[:], DEN[:], AF.Ln, ["den"], ["den"])
                    ACT(DEN[:], DEN[:], AF.Exp, ["den"], ["den"], scale=-1.0)
                    for hi in range(4):
                        TT_("dve", OBH(8 + 4 * kv + hi, qs), pv[0:64, hi * 128:(hi + 1) * 128], DEN[:, hi * 128:(hi + 1) * 128], ALU.mult,
                            [pvk, "den"], ["ob"])

        def out_proj_ab(l, g):
            j = l // 2
            for mb in range(2):
                W, wk = wload(about[j, mb])
                Wv = W[:].rearrange("p (k m) -> p k m", k=8)
                for mi in range(4):
                    m = mb * 4 + mi
                    for tt in range(2):
                        sl = slice(tt * TT, (tt + 1) * TT)
                        ps, pk = newps()
                        MM(ps[:], [(Wv[:, kc, mi * 128:(mi + 1) * 128], OB128[:, kc, sl]) for kc in range(8)], [wk, "ob"], [pk])
                        STT(Y[:, m, sl], ps[:], mod(l, g, 2)[:, m:m + 1], Y[:, m, sl], ALU.mult, ALU.add, [pk, "y", "mod%d" % l], ["y"])

        ada_done = set()
        for g in GROUPS:
            for c in range(8):
                S.dma("sp", Y[:, c, :], xT[g, c * 128:(c + 1) * 128, :], writes=["y"])
            for l in range(DEPTH):
                if l not in ada_done:
                    adaln(l)
                    ada_done.add(l)
                with ExitStack() as ctx:
                    modulate(l, g, 0, ctx)
                S.barrier()
                if l % 2 == 0:
                    if not (DO_A and DO_B):
                        MEMSET("dve", OBt[:], 0.0, ["ob"])
                    if DO_A:
                        with ExitStack() as ctx:
                            deltanet(l, g, ctx)
                        S.barrier()
                    if DO_B:
                        with ExitStack() as ctx:
                            attention(l, g, ctx)
                        S.barrier()
                    out_proj_ab(l, g)
                else:
                    if DO_C:
                        with ExitStack() as ctx:
                            mixer_c(l, g, ctx)
                        S.barrier()
                with ExitStack() as ctx:
                    modulate(l, g, 1, ctx)
                S.barrier()
                if DO_MLP:
                    with ExitStack() as ctx:
                        mlp(l, g, ctx)
                    S.barrier()
            for c in range(8):
                S.dma("sp", yT[g, c * 128:(c + 1) * 128, :], Y[:, c, :], reads=["y"], writes=["yT"])
        S.barrier()
        with nc.Block() as block:
            S.emit(block)
    return nc


_CACHE = {}


def _consts():
    c64 = np.zeros((64, 832), np.float32)
    eye = np.eye(64, dtype=np.float32)
    r = np.arange(64)[:, None]; x = np.arange(64)[None, :]
    lo = np.where(r > x, 0.0, -30000.0).astype(np.float32)
    up = np.where(r < x, 0.0, -30000.0).astype(np.float32)
    for i in range(4):
        c64[:, i * 64:(i + 1) * 64] = eye
        c64[:, 256 + i * 64:256 + (i + 1) * 64] = lo
        c64[:, 512 + i * 64:512 + (i + 1) * 64] = up
    P = np.zeros((64, 64), np.float32)
    for m in range(64):
        if m % 32 < 16:
            P[m, m + 16] = -1.0
        else:
            P[m, m - 16] = 1.0
    c64[:, 768:832] = P.T
    t = np.arange(1024, dtype=np.float32)
    inv = (10000.0 ** (-np.arange(0, 32, 2, dtype=np.float32) / 32)).astype(np.float32)
    ang_r = (np.floor(t / 64)[:, None] * inv).astype(np.float32)
    ang_c = ((t % 64)[:, None] * inv).astype(np.float32)
    ang = np.zeros((64, 1024), np.float32)
    for p in range(64):
        ang[p] = (ang_r if p < 32 else ang_c)[:, p % 16]
    cossin = np.concatenate([np.cos(ang), np.sin(ang)], axis=1).astype(np.float32)
    c16 = np.zeros((16, 1040), np.float32)
    cmask = np.zeros((16, 2048), np.float32)
    for d in range(2):
        for hh in range(2):
            for i in range(4):
                k = d * 8 + hh * 4 + i
                o = (d * 2 + hh) * 256
                c16[k, o + i * 64:o + (i + 1) * 64] = 1.0
                c16[k, 1024 + (d * 2 + hh) * 4 + i] = 1.0
    cm = np.ones(1024, np.float32); cm[0::64] = 0.0
    cmask[:, 0:1024] = cm
    cm = np.ones(1024, np.float32); cm[63::64] = 0.0
    cmask[:, 1024:2048] = cm
    c128 = np.zeros((128, 256), np.float32)
    k = np.arange(128)[:, None]; q = np.arange(128)[None, :]
    c128[:, 0:128] = (q <= k)
    c128[:, 128:256] = (k <= q)
    return c64, c16, c128, cossin, cmask


def _fm(v):
    return np.ascontiguousarray(np.swapaxes(v.reshape(v.shape[:-1] + (8, 128)), -1, -2))


def _wt(w, kc, mcols):
    K, M = w.shape
    a = w.reshape(kc, 128, M // mcols, mcols)
    return np.ascontiguousarray(a.transpose(2, 1, 0, 3)).reshape(M // mcols, 128, kc * mcols)


def kernel(x_prompt, x_sample, state_delta, cache_k, cache_v, state_lru, c, c_ctx,
           ada_w, ada_b, norm1_g, norm2_g, ff_w1, ff_w2,
           ab_w_in, ab_conv_w, ab_conv_b, dn_a_log, dn_dt_bias, dn_norm_g,
           attn_q_norm_g, attn_k_norm_g, attn_sink, ab_w_out,
           c_w_in, c_conv_w, c_conv_b, lru_w_a, lru_b_a, lru_w_x, lru_b_x, lru_lambda, c_w_out):
    f = lambda a: np.asarray(a, dtype=np.float32)
    x_prompt, x_sample = f(x_prompt), f(x_sample)
    NE = (DEPTH + 1) // 2
    NO = max(1, DEPTH // 2)
    NL = DEPTH
    c64, c16, c128, cossin, cmask = _consts()
    shared = {"c64": c64, "c16": c16, "c128": c128, "cossin": cossin, "cmask": cmask}
    shared["ada_w"] = np.stack([_wt(f(ada_w[l]), 8, 512) for l in range(NL)])
    shared["ada_b"] = np.stack([np.ascontiguousarray(f(ada_b[l]).reshape(48, 128).T) for l in range(NL)])
    shared["ng"] = np.stack([np.concatenate([_fm(f(norm1_g[l])), _fm(f(norm2_g[l]))], axis=1) for l in range(NL)])
    shared["w1"] = np.stack([_wt(f(ff_w1[l]), 8, 512) for l in range(NL)])
    shared["w2"] = np.stack([_wt(f(ff_w2[l]), 32, 128) for l in range(NL)])
    abin = []
    for j in range(NE):
        w = f(ab_w_in[j])
        pad = np.zeros((1024, 6 * 512), np.float32)
        pad[:, 0:1536] = w[:, 0:1536]
        pad[:, 1536:2048] = w[:, 1536:2048]
        pad[:, 2048:2560] = w[:, 2080:2592]
        pad[:, 2560:2592] = w[:, 2048:2080]
        pad[:, 2592:2720] = w[:, 2592:2720]
        pad[:, 2720:2848] = w[:, 2720:2848]
        abin.append(_wt(pad, 8, 512))
    shared["abin"] = np.stack(abin)
    shared["about"] = np.stack([_wt(f(ab_w_out[j]), 8, 512) for j in range(NE)])
    p64 = np.zeros((NE, 64, 24 * 5 + 11), np.float32)
    for j in range(NE):
        cw = f(ab_conv_w[j]).reshape(4, 24, 64)
        p64[j, :, 0:96] = cw.transpose(2, 1, 0).reshape(64, 96)
        p64[j, :, 96:120] = f(ab_conv_b[j]).reshape(24, 64).T
        p64[j, :, 120] = f(dn_norm_g[j])
        p64[j, :, 121] = f(attn_q_norm_g[j])
        p64[j, :, 122] = f(attn_k_norm_g[j])
        p64[j, :, 123:131] = f(attn_sink[j])[None, :]
    shared["abp64"] = p64
    p128 = np.zeros((NE, 128, 61), np.float32)
    for j in range(NE):
        p128[j, :, 0:48] = f(ab_conv_w[j]).reshape(4, 12, 128).transpose(2, 1, 0).reshape(128, 48)
        p128[j, :, 48:60] = f(ab_conv_b[j]).reshape(12, 128).T
        p128[j, :, 60] = np.tile(f(dn_norm_g[j]), 2)
    shared["abp128"] = p128
    cdm = np.zeros((128, 384), np.float32)
    r_ = (np.arange(128) % 64)[:, None]; x_ = np.arange(64)[None, :]
    for i in range(2):
        cdm[:, i * 64:(i + 1) * 64] = (r_ == x_)
        cdm[:, 128 + i * 64:128 + (i + 1) * 64] = np.where(r_ > x_, 0.0, -30000.0)
        cdm[:, 256 + i * 64:256 + (i + 1) * 64] = np.where(r_ < x_, 0.0, -30000.0)
    shared["cd"] = cdm
    csm = np.zeros((128, 520), np.float32)
    for d in range(2):
        for hh_ in range(2):
            for gp in range(2):
                for ii in range(2):
                    kk = d * 8 + hh_ * 4 + 2 * ii + gp
                    o = (d * 2 + hh_) * 128
                    csm[gp * 64 + kk, o + ii * 64:o + (ii + 1) * 64] = 1.0
                    csm[gp * 64 + kk, 512 + (d * 2 + hh_) * 2 + ii] = 1.0
    shared["csel"] = csm
    p16 = np.zeros((NE, 16, 2), np.float32)
    for j in range(NE):
        p16[j, :, 0] = f(dn_a_log[j]).reshape(16)
        p16[j, :, 1] = f(dn_dt_bias[j]).reshape(16)
    shared["abp16"] = p16
    shared["cin"] = np.stack([_wt(f(c_w_in[j]), 8, 512) for j in range(NO)])
    shared["cout"] = np.stack([_wt(f(c_w_out[j]), 8, 512) for j in range(NO)])
    cbd = np.zeros((NO, 4, 128, 8, 128), np.float32)
    for j in range(NO):
        for gi, wsrc in enumerate((lru_w_a, lru_w_x)):
            for d in range(2):
                w = f(wsrc[j][d])
                for cc in range(8):
                    for s_ in range(2):
                        cbd[j, gi * 2 + d, s_ * 64:(s_ + 1) * 64, cc, s_ * 64:(s_ + 1) * 64] = w[cc * 2 + s_]
    shared["cbd"] = cbd.reshape(NO, 4, 128, 8 * 128)
    cpp = np.zeros((NO, 128, 8, 11), np.float32)
    for j in range(NO):
        cpp[j, :, :, 0:4] = np.stack([_fm(f(c_conv_w[j][t])) for t in range(4)], axis=-1)
        cpp[j, :, :, 4] = _fm(f(c_conv_b[j]))
        for d in range(2):
            cpp[j, :, :, 5 + d] = _fm(f(lru_b_a[j][d]))
            cpp[j, :, :, 7 + d] = _fm(f(lru_b_x[j][d]))
            cpp[j, :, :, 9 + d] = _fm(f(lru_lambda[j][d]))
    shared["cp"] = cpp.reshape(NO, 128, 88)
    in_maps = []
    for i in range(8):
        b = i % 4
        m = dict(shared)
        xp = x_prompt[4 * i:4 * i + 4].reshape(1024, 1024)
        m["xT"] = np.ascontiguousarray(np.stack([xp.T, x_sample[b].T]))
        m["cond"] = np.stack([_fm(f(c_ctx)), _fm(f(c[b]))])
        sd = f(state_delta[b])[:NE]
        m["sdelta"] = np.ascontiguousarray(sd.reshape(NE, 2, 2, 2, 2, 64, 64).transpose(0, 1, 2, 4, 5, 3, 6)).reshape(NE, 2, 2, 128, 128)
        ck = f(cache_k[b])[:NE]
        m["kcT"] = np.ascontiguousarray(ck.transpose(0, 3, 2, 1))
        cv = f(cache_v[b])[:NE].reshape(NE, 2, 128, 128)
        m["vc"] = np.ascontiguousarray(cv.transpose(0, 2, 1, 3))
        m["slru"] = _fm(f(state_lru[b])[:NO])
        in_maps.append(m)
    if "nc" not in _CACHE:
        _CACHE["nc"] = build_program()
    res = run_bass_kernel_spmd(_CACHE["nc"], in_maps, core_ids=list(range(8)))
    R = res.results
    y_prompt = np.zeros((32, 256, 1024), np.float32)
    y_sample = np.zeros((4, 1024, 1024), np.float32)
    new_dn = np.zeros((32, NE, 2, 8, 64, 64), np.float32)
    new_k = np.zeros((32, NE, 256, 2, 64), np.float32)
    new_v = np.zeros((32, NE, 256, 2, 64), np.float32)
    new_lru = np.zeros((32, NO, 2, 1024), np.float32)
    for i in range(8):
        r = R[i]
        y_prompt[4 * i:4 * i + 4] = r["yT"][0].T.reshape(4, 256, 1024)
        if i < 4:
            y_sample[i] = r["yT"][1].T
        dn = r["o_dn"].reshape(4, NE, 2, 2, 2, 64, 2, 64)
        new_dn[4 * i:4 * i + 4] = dn.transpose(0, 1, 2, 3, 6, 4, 5, 7).reshape(4, NE, 2, 8, 64, 64)
        ok = r["o_k"].reshape(NE, 64, 2, 4, 256)
        new_k[4 * i:4 * i + 4] = ok.transpose(3, 0, 4, 2, 1)
        ov = r["o_v"].reshape(NE, 128, 4, 2, 2, 64)
        new_v[4 * i:4 * i + 4] = ov.transpose(2, 0, 3, 1, 4, 5).reshape(4, NE, 256, 2, 64)
        ol = r["o_lru"].reshape(NO, 2, 4, 128, 8)
        new_lru[4 * i:4 * i + 4] = ol.transpose(2, 0, 1, 4, 3).reshape(4, NO, 2, 1024)
    return (y_prompt, y_sample, new_dn, new_k, new_v, new_lru)
```

```python
import os
import math
import numpy as np
from contextlib import ExitStack
import concourse.bass as bass
import concourse.mybir as mybir
from concourse.bass_utils import run_bass_kernel_spmd

F32 = mybir.dt.float32
BF16 = mybir.dt.bfloat16
F32R = mybir.dt.float32r
USE_F32R = os.environ.get("K_F32R", "0") == "1"
AF = mybir.ActivationFunctionType
ALU = mybir.AluOpType

D = 1024
NT = 1024
TT = 512
DEPTH = int(os.environ.get("K_DEPTH", "4"))
DO_A = os.environ.get("K_A", "1") == "1"
DO_B = os.environ.get("K_B", "1") == "1"
DO_C = os.environ.get("K_C", "1") == "1"
DO_MLP = os.environ.get("K_MLP", "1") == "1"
GROUPS = [int(x) for x in os.environ.get("K_GROUPS", "01")]
EPS = 1e-6
DN_STAGE = int(os.environ.get("K_DN_STAGE", "9"))
DN_CUT = float(os.environ.get("K_DN_CUT", "9"))
NH = 4
WH = NH * 64
EPOCH = 2000


class Sched:
    ENG = ("pe", "act", "dve", "pool", "sp")

    def __init__(self, nc, es, n_dma_sems=4):
        self.nc = nc
        self.es = es
        self.ops = {e: [] for e in self.ENG}
        self.cnt = {e: 0 for e in self.ENG}
        self.sems = {}
        self.dma_sems = {}
        self.dma_cnt = {}
        self.dma_rr = {}
        self.n_dma_sems = n_dma_sems
        for q in ("sp", "pool"):
            for s in range(n_dma_sems):
                self.dma_sems[(q, s)] = es.enter_context(nc.semaphore("d_%s%d" % (q, s)))
                self.dma_cnt[(q, s)] = 0
            self.dma_rr[q] = 0
        self.waited = {e: {} for e in self.ENG}
        self.last_w = {}
        self.readers = {}

    def _deps(self, eng, reads, writes):
        need = {}
        def add(ev):
            sk, v = ev
            if need.get(sk, 0) < v:
                need[sk] = v
        for k in reads:
            if k in self.last_w:
                add(self.last_w[k])
        for k in writes:
            if k in self.last_w:
                add(self.last_w[k])
            for ev in self.readers.get(k, ()):
                add(ev)
        waits = []
        w = self.waited[eng]
        for sk, v in need.items():
            if w.get(sk, 0) < v:
                w[sk] = v
                waits.append((sk, v))
        return waits

    def _commit(self, ev, reads, writes):
        for k in reads:
            r = self.readers.setdefault(k, [])
            r.append(ev)
            if len(r) > 24:
                best = {}
                for sk, v in r:
                    if best.get(sk, 0) < v:
                        best[sk] = v
                self.readers[k] = list(best.items())
        for k in writes:
            self.last_w[k] = ev
            self.readers[k] = []

    def op(self, eng, fn, reads=(), writes=()):
        psr = [k for k in reads if isinstance(k, str) and k.startswith("ps")]
        if psr:
            writes = list(writes) + [k for k in psr if k not in writes]
        waits = self._deps(eng, reads, writes)
        self.cnt[eng] += 1
        ev = (eng, self.cnt[eng])
        self.ops[eng].append((waits, fn, ("c", eng, self.cnt[eng])))
        self._commit(ev, reads, writes)

    def dma(self, q, out, in_, reads=(), writes=()):
        s = self.dma_rr[q]
        self.dma_rr[q] = (s + 1) % self.n_dma_sems
        sk = ("dma", q, s)
        waits = self._deps(q, reads, writes)
        prev = self.dma_cnt[(q, s)]
        if prev > 0 and self.waited[q].get(sk, 0) < prev:
            self.waited[q][sk] = prev
            waits.append((sk, prev))
        self.dma_cnt[(q, s)] = prev + 16
        ev = (sk, prev + 16)
        self.ops[q].append((waits, (lambda e: e.dma_start(out=out, in_=in_)), ("d", (q, s))))
        self._commit(ev, reads, writes)

    def barrier(self):
        for e in self.ENG:
            waits = []
            lazy = (e == "pool")
            for o in self.ENG:
                if o != "sp" and self.cnt[o] > 0 and self.waited[e].get(o, 0) < self.cnt[o]:
                    if not lazy:
                        self.waited[e][o] = self.cnt[o]
                    waits.append((o, self.cnt[o]))
            for (q, s), c in self.dma_cnt.items():
                sk = ("dma", q, s)
                if c > 0 and self.waited[e].get(sk, 0) < c:
                    if not lazy:
                        self.waited[e][sk] = c
                    waits.append((sk, c))
            if waits:
                self.ops[e].append((waits, None, ("bar",) if lazy else None))

    def _get_sem(self, eng, ep):
        k = (eng, ep)
        if k not in self.sems:
            self.sems[k] = self.es.enter_context(self.nc.semaphore("s_%s%d" % (eng, ep)))
        return self.sems[k]

    def _wait(self, e, sk, v):
        if isinstance(sk, tuple):
            e.wait_ge(self.dma_sems[(sk[1], sk[2])], v)
        else:
            ep, r = divmod(v - 1, EPOCH)
            e.wait_ge(self._get_sem(sk, ep), r + 1)

    def emit(self, block):
        for eng in self.ENG:
            for ep in range((self.cnt[eng] + EPOCH - 1) // EPOCH + 1):
                if eng != "sp":
                    self._get_sem(eng, ep)
        pl = self.ops["pool"]
        keep = []
        for i, (waits, fn, kind) in enumerate(pl):
            if kind == ("bar",):
                has_c = False
                for (w2, f2, k2) in pl[i + 1:]:
                    if k2 == ("bar",):
                        break
                    if k2 is not None and k2[0] == "c":
                        has_c = True
                        break
                if has_c:
                    keep.append((waits, None, None))
            else:
                keep.append((waits, fn, kind))
        self.ops["pool"] = keep
        def mk(ename):
            def body(e):
                for waits, fn, kind in self.ops[ename]:
                    for sk, v in waits:
                        self._wait(e, sk, v)
                    if fn is None:
                        continue
                    ins = fn(e)
                    if kind[0] == "c":
                        ep = (kind[2] - 1) // EPOCH
                        ins.then_inc(self._get_sem(kind[1], ep), 1)
                    else:
                        ins.then_inc(self.dma_sems[kind[1]], 16)
            return body
        block.tensor(mk("pe"))
        block.scalar(mk("act"))
        block.vector(mk("dve"))
        block.gpsimd(mk("pool"))
        block.sync(mk("sp"))


def build_program():
    nc = bass.Bass("TRN2", target_bir_lowering=False)
    NE = (DEPTH + 1) // 2
    NO = max(1, DEPTH // 2)
    NL = DEPTH
    def din(name, shape):
        return nc.dram_tensor(name, list(shape), F32, kind="ExternalInput").ap()
    def dout(name, shape):
        return nc.dram_tensor(name, list(shape), F32, kind="ExternalOutput").ap()
    xT = din("xT", [2, D, NT])
    cond = din("cond", [2, 128, 8])
    sdelta = din("sdelta", [NE, 2, 2, 128, 128])
    kcT = din("kcT", [NE, 64, 2, 256])
    vc = din("vc", [NE, 128, 2, 128])
    slru = din("slru", [NO, 2, 128, 8])
    ada_w = din("ada_w", [NL, 12, 128, 8 * 512])
    ada_b = din("ada_b", [NL, 128, 48])
    ng = din("ng", [NL, 128, 16])
    w1 = din("w1", [NL, 8, 128, 8 * 512])
    w2 = din("w2", [NL, 8, 128, 32 * 128])
    abin = din("abin", [NE, 6, 128, 8 * 512])
    about = din("about", [NE, 2, 128, 8 * 512])
    abp64 = din("abp64", [NE, 64, 24 * 5 + 3 + 8])
    abp16 = din("abp16", [NE, 16, 2])
    abp128 = din("abp128", [NE, 128, 61])
    cd = din("cd", [128, 384])
    csel = din("csel", [128, 520])
    cin = din("cin", [NO, 4, 128, 8 * 512])
    cout = din("cout", [NO, 2, 128, 8 * 512])
    cbd = din("cbd", [NO, 4, 128, 8 * 128])
    cp = din("cp", [NO, 128, 8 * 11])
    c64 = din("c64", [64, 832])
    cossin = din("cossin", [64, 2048])
    c16 = din("c16", [16, 1040])
    cmask = din("cmask", [16, 2048])
    c128 = din("c128", [128, 256])
    yT = dout("yT", [2, D, NT])
    o_dn = dout("o_dn", [4, NE, 2, 2, 128, 128])
    o_k = dout("o_k", [NE, 64, 2, NT])
    o_v = dout("o_v", [NE, 128, 8, 128])
    o_lru = dout("o_lru", [NO, 2, 4, 128, 8])

    with ExitStack() as es:
        S = Sched(nc, es)
        uid = [0]
        def sb(shape, dt=F32, ctx=None):
            uid[0] += 1
            return (ctx or es).enter_context(nc.sbuf_tensor("t%d" % uid[0], list(shape), dt))
        def ACT(out, in_, func, reads, writes, **kw):
            S.op("act", lambda e: e.activation(out=out, in_=in_, func=func, **kw), reads, writes)
        def TT_(eng, out, in0, in1, op, reads, writes):
            S.op(eng, lambda e: e.tensor_tensor(out=out, in0=in0, in1=in1, op=op), reads, writes)
        def TS(eng, out, in0, s1, s2, op0, op1, reads, writes):
            S.op(eng, lambda e: e.tensor_scalar(out=out, in0=in0, scalar1=s1, scalar2=s2, op0=op0, op1=op1), reads, writes)
        def STT(out, in0, scalar, in1, op0, op1, reads, writes):
            S.op("dve", lambda e: e.scalar_tensor_tensor(out=out, in0=in0, scalar=scalar, in1=in1, op0=op0, op1=op1), reads, writes)
        def CP(eng, out, in_, reads, writes):
            if eng == "act":
                ACT(out, in_, AF.Copy, reads, writes)
            else:
                S.op(eng, lambda e: e.tensor_copy(out=out, in_=in_), reads, writes)
        def RECIP(out, in_, reads, writes):
            S.op("dve", lambda e: e.reciprocal(out=out, in_=in_), reads, writes)
        def MM(out, pairs, reads, writes):
            pairs = list(pairs)
            def fn(e):
                ins = None
                n = len(pairs)
                for i, (l, r) in enumerate(pairs):
                    ins = e.matmul(out, lhsT=l, rhs=r, start=(i == 0), stop=(i == n - 1))
                return ins
            S.op("pe", fn, reads, writes)
        def MMS(items, reads, writes):
            items = list(items)
            def fn(e):
                ins = None
                for (o, l, r) in items:
                    ins = e.matmul(o, lhsT=l, rhs=r, start=True, stop=True)
                return ins
            S.op("pe", fn, reads, writes)
        def MEMSET(eng, out, val, writes):
            S.op(eng, lambda e: e.memset(out, val), (), writes)

        PSB = [es.enter_context(nc.psum_tensor("ps%d" % i, [128, 512], F32)) for i in range(8)]
        prr = [0]
        def newps():
            i = prr[0]
            prr[0] = (i + 1) % 8
            return PSB[i], "ps%d" % i

        Y = sb([128, 8, NT])
        HT = sb([128, 8, NT], BF16)
        OBt = sb([128, 8, NT], BF16)
        OB128 = OBt[:]
        def OBH(h, sl):
            return OBt[(h % 2) * 64:(h % 2) * 64 + 64, h // 2, sl]
        NSLOT = 3
        WR = [sb([128, 4096], BF16) for _ in range(NSLOT)]
        wrr = [0]
        C64 = sb([64, 832])
        C16 = sb([16, 1040])
        C128 = sb([128, 256])
        MODALL = sb([128, 4, 2, 48])
        SC = sb([128, 2, 8], BF16)
        CONDT = sb([128, 2, 8])
        ONESB = sb([128, 128], BF16)
        IDB = sb([64, 64], BF16)
        NG = sb([128, 4, 16])
        ADAB = sb([128, 4, 48])
        LP = sb([128, 64])
        IDENT4 = C64[:, 0:256]
        NEG_LO = C64[:, 256:512]
        NEG_UP = C64[:, 512:768]
        PERMT = C64[:, 768:832]
        def BLK(d, hh):
            o = (d * 2 + hh) * 256
            return C16[:, o:o + 256]
        def SEL(d, hh):
            o = 1024 + (d * 2 + hh) * 4
            return C16[:, o:o + 4]
        MASK_LO = C128[:, 0:128]
        MASK_UP = C128[:, 128:256]

        S.dma("sp", C64[:], c64[:, :], writes=["c64"])
        S.dma("sp", C16[:], c16[:, :], writes=["c16"])
        S.dma("sp", C128[:], c128[:, :], writes=["c128"])
        for g in range(2):
            S.dma("sp", CONDT[:, g, :], cond[g], writes=["condt"])
        for l in range(NL):
            S.dma("sp", NG[:, l, :], ng[l], writes=["ng"])
            S.dma("sp", ADAB[:, l, :], ada_b[l], writes=["adab"])
        ACT(SC[:], CONDT[:], AF.Silu, ["condt"], ["sc"])
        MEMSET("dve", ONESB[:], 1.0, ["onesb"])
        CP("dve", IDB[:], C64[:, 0:64], ["c64"], ["idb"])
        ONES16 = sb([128, 64])
        NEGONES16 = sb([128, 64])
        CD = sb([128, 384])
        CS = sb([128, 520])
        S.dma("sp", CD[:], cd[:, :], writes=["cd"])
        S.dma("sp", CS[:], csel[:, :], writes=["cs"])
        IDENT2 = CD[:, 0:128]
        NEG_LO2 = CD[:, 128:256]
        NEG_UP2 = CD[:, 256:384]
        def BLK2(d, hh):
            o = (d * 2 + hh) * 128
            return CS[:, o:o + 128]
        def SEL2(d, hh):
            o = 512 + (d * 2 + hh) * 2
            return CS[:, o:o + 2]
        IDB2 = sb([128, 64], BF16)
        CP("dve", IDB2[:], CD[:, 0:64], ["cd"], ["idb2"])
        MEMSET("dve", ONES16[:], 1.0, ["ones16"])
        MEMSET("dve", NEGONES16[:], -1.0, ["ones16"])
        CK = ["c64", "c16", "c128", "onesb", "idb", "ones16"]

        def wload(src_ap, parts=128):
            i = wrr[0]
            wrr[0] = (i + 1) % NSLOT
            S.dma("pool", WR[i][0:parts, :], src_ap, writes=["w%d" % i])
            return WR[i], "w%d" % i

        def adaln(l):
            ps, pk = newps()
            for mb in range(12):
                W, wk = wload(ada_w[l, mb])
                Wv = W[:].rearrange("p (k m) -> p k m", k=8)
                items = []
                for mi in range(4):
                    m = mb * 4 + mi
                    for kc in range(8):
                        pass
                def fn(e, Wv=Wv, mb=mb):
                    ins = None
                    for mi in range(4):
                        m = mb * 4 + mi
                        for kc in range(8):
                            ins = e.matmul(ps[:, 2 * m:2 * m + 2], lhsT=Wv[:, kc, mi * 128:(mi + 1) * 128], rhs=SC[:, :, kc],
                                           start=(kc == 0), stop=(kc == 7))
                    return ins
                S.op("pe", fn, [wk, "sc"], [pk])
            for g in range(2):
                TT_("dve", MODALL[:, l, g, :], ps[:, 0:96].rearrange("p (m g) -> p g m", g=2)[:, g, :], ADAB[:, l, :], ALU.add,
                    [pk, "adab"], ["mod%d" % l])

        def mod(l, g, i):
            return MODALL[:, l, g, i * 8:(i + 1) * 8]

        def modulate(l, g, which, ctx):
            GS = LP[:, which * 8:(which + 1) * 8]
            STT(GS, mod(l, g, 1 + 3 * which), 1.0, NG[:, l, which * 8:(which + 1) * 8], ALU.add, ALU.mult,
                ["mod%d" % l, "ng"], ["lp%d" % which])
            SQ = [sb([128, 8, TT], BF16, ctx) for _ in range(2)]
            RS = [sb([128, TT], F32, ctx) for _ in range(2)]
            RSTD = [sb([128, TT], F32, ctx) for _ in range(2)]
            TMP = [[sb([128, TT], F32, ctx) for _ in range(2)] for _ in range(2)]
            sls = [slice(tt * TT, (tt + 1) * TT) for tt in range(2)]
            pss = []
            for tt in range(2):
                for hf in range(2):
                    ACT(SQ[tt][:, hf * 4:(hf + 1) * 4, :], Y[:, hf * 4:(hf + 1) * 4, sls[tt]], AF.Square, ["y"], ["msq%d%d" % (tt, hf)])
            for tt in range(2):
                ps, pk = newps()
                pss.append((ps, pk))
                MM(ps[:], [(ONESB[:], SQ[tt][:, c, :]) for c in range(8)], ["msq%d0" % tt, "msq%d1" % tt, "onesb"], [pk])
            for tt in range(2):
                ACT(RS[tt][:], pss[tt][0][:], AF.Ln, [pss[tt][1]], ["mrs%d" % tt], scale=1.0 / D, bias=EPS)
            for tt in range(2):
                ACT(RSTD[tt][:], RS[tt][:], AF.Exp, ["mrs%d" % tt], ["mrstd%d" % tt], scale=-0.5)
            for c in range(8):
                for tt in range(2):
                    T_ = TMP[tt][c % 2]
                    tk = "mtmp%d%d" % (tt, c % 2)
                    STT(T_[:], Y[:, c, sls[tt]], GS[:, c:c + 1], RSTD[tt][:], ALU.mult, ALU.mult, ["y", "lp%d" % which, "mrstd%d" % tt], [tk])
                    ACT(HT[:, c, sls[tt]], T_[:], AF.Identity, [tk, "mod%d" % l], ["ht"],
                        bias=mod(l, g, 3 * which)[:, c:c + 1], scale=1.0)

        def mlp(l, g, ctx):
            H1 = sb([128, 32, NT], BF16, ctx)
            SQ2 = [sb([128, TT], F32, ctx) for _ in range(2)]
            n = 0
            for mb in range(8):
                W, wk = wload(w1[l, mb])
                Wv = W[:].rearrange("p (k m) -> p k m", k=8)
                for mi in range(4):
                    m = mb * 4 + mi
                    for tt in range(2):
                        sl = slice(tt * TT, (tt + 1) * TT)
                        ps, pk = newps()
                        MM(ps[:], [(Wv[:, kc, mi * 128:(mi + 1) * 128], HT[:, kc, sl]) for kc in range(8)], [wk, "ht"], [pk])
                        q = SQ2[n % 2]; qk = "sq2%d" % (n % 2); n += 1
                        ACT(q[:], ps[:], AF.Square, [pk], [qk])
                        STT(H1[:, m, sl], ps[:], 0.0, q[:], ALU.is_gt, ALU.mult, [pk, qk], [("h1", m)])
            for m in range(8):
                W, wk = wload(w2[l, m])
                Wv = W[:].rearrange("p (k m) -> p k m", k=32)
                for tt in range(2):
                    sl = slice(tt * TT, (tt + 1) * TT)
                    ps, pk = newps()
                    MM(ps[:], [(Wv[:, kc, :], H1[:, kc, sl]) for kc in range(32)], [wk] + [("h1", kc) for kc in range(32)], [pk])
                    STT(Y[:, m, sl], ps[:], mod(l, g, 5)[:, m:m + 1], Y[:, m, sl], ALU.mult, ALU.add, [pk, "y", "mod%d" % l], ["y"])

        def mixer_c(l, g, ctx):
            j = l // 2
            seqs = [(s * 256, 256) for s in range(4)] if g == 0 else [(0, 1024)]
            CPt = sb([128, 8, 11], F32, ctx)
            S.dma("sp", CPt[:], cp[j].rearrange("p (c k) -> p c k", k=11), writes=["cp"])
            CL = sb([128, 2, 8], F32, ctx)
            T1 = sb([128, 2, 8], F32, ctx)
            ACT(T1[:], CPt[:, :, 9:11].rearrange("p c d -> p d c"), AF.Exp, ["cp"], ["ct1"], scale=-1.0)
            ACT(T1[:], T1[:], AF.Ln, ["ct1"], ["ct1"], bias=1.0, scale=1.0)
            TS("dve", CL[:], T1[:], -8.0, None, ALU.mult, ALU.bypass, ["ct1"], ["cl"])
            BD = sb([128, 4, 8, 128], F32, ctx)
            for i in range(4):
                S.dma("sp", BD[:, i], cbd[j, i].rearrange("p (c m) -> p c m", c=8), writes=["bd"])
            H0 = sb([128, 2, 8], F32, ctx)
            if g == 1:
                for d in range(2):
                    S.dma("sp", H0[:, d, :], slru[j, d], writes=["h0"])
            FIN = sb([128, 2, 4, 8], F32, ctx)
            NSL = 2
            CSL = []
            for s_ in range(NSL):
                CSL.append(dict(xr=sb([128, NT], F32, ctx), xc=sb([128, NT], F32, ctx), gt=sb([128, NT], F32, ctx), ga=sb([128, NT], F32, ctx),
                                aa=[sb([128, NT], F32, ctx) for _ in range(2)], ta=[sb([128, NT], F32, ctx) for _ in range(2)],
                                tb=[sb([128, NT], F32, ctx) for _ in range(2)]))
            ns = len(seqs); L = seqs[0][1]
            def chunk_gen(s_, c, ci, Wxv, Wgv, wxk, wgk):
                B_ = CSL[s_]
                xr, xc, gt, ga = B_["xr"], B_["xc"], B_["gt"], B_["ga"]
                K_ = lambda n_: "%s_%d" % (n_, s_)
                kx, kc_, kg, kga = K_("xr"), K_("xc"), K_("gt"), K_("ga")
                banks = [(PSB[4 * s_ + q], "ps%d" % (4 * s_ + q)) for q in range(4)]
                for tt in range(2):
                    sl = slice(tt * TT, (tt + 1) * TT)
                    ps, pk = banks[tt]
                    MM(ps[:], [(Wxv[:, kc, ci * 128:(ci + 1) * 128], HT[:, kc, sl]) for kc in range(8)], [wxk, "ht"], [pk])
                    ps, pk = banks[2 + tt]
                    MM(ps[:], [(Wgv[:, kc, ci * 128:(ci + 1) * 128], HT[:, kc, sl]) for kc in range(8)], [wgk, "ht"], [pk])
                yield
                for tt in range(2):
                    sl = slice(tt * TT, (tt + 1) * TT)
                    CP("act", xr[:, sl], banks[tt][0][:], [banks[tt][1]], [kx])
                    CP("dve", gt[:, sl], banks[2 + tt][0][:], [banks[2 + tt][1]], [kg])
                yield
                TS("dve", xc[:], xr[:], CPt[:, c, 2:3], CPt[:, c, 4:5], ALU.mult, ALU.add, [kx, "cp"], [kc_])
                ACT(ga[:], gt[:], AF.Square, [kg], [kga])
                yield
                xr3 = xr[:].rearrange("p (s t) -> p s t", s=ns)
                xc3 = xc[:].rearrange("p (s t) -> p s t", s=ns)
                for tap, o in ((0, -2), (1, -1), (3, 1)):
                    d0, d1 = max(0, -o), L - max(0, o)
                    STT(xc3[:, :, d0:d1], xr3[:, :, d0 + o:d1 + o], CPt[:, c, tap:tap + 1], xc3[:, :, d0:d1], ALU.mult, ALU.add,
                        [kx, kc_, "cp"], [kc_])
                    yield
                TS("dve", ga[:], ga[:], 0.044715, 1.0, ALU.mult, ALU.add, [kga], [kga])
                yield
                TT_("dve", ga[:], ga[:], gt[:], ALU.mult, [kga, kg], [kga])
                yield
                ACT(ga[:], ga[:], AF.Sigmoid, [kga], [kga], scale=1.5957691216)
                yield
                TT_("dve", ga[:], ga[:], gt[:], ALU.mult, [kga, kg], [kga])
                yield
                for d in range(2):
                    a_, ta, tb = B_["aa"][d], B_["ta"][d], B_["tb"][d]
                    ka, kta, ktb = K_("aa%d" % d), K_("ta%d" % d), K_("tb%d" % d)
                    for tt in range(2):
                        sl = slice(tt * TT, (tt + 1) * TT)
                        ps, pk = banks[tt]
                        MM(ps[:], [(BD[:, d, c, :], xc[:, sl])], ["bd", kc_], [pk])
                        ps, pk = banks[2 + tt]
                        MM(ps[:], [(BD[:, 2 + d, c, :], xc[:, sl])], ["bd", kc_], [pk])
                    yield
                    for tt in range(2):
                        sl = slice(tt * TT, (tt + 1) * TT)
                        ACT(ta[:, sl], banks[tt][0][:], AF.Sigmoid, [banks[tt][1], "cp"], [kta], bias=CPt[:, c, 5 + d:6 + d], scale=1.0)
                        ACT(tb[:, sl], banks[2 + tt][0][:], AF.Sigmoid, [banks[2 + tt][1], "cp"], [ktb], bias=CPt[:, c, 7 + d:8 + d], scale=1.0)
                    yield
                    ACT(a_[:], ta[:], AF.Exp, [kta, "cl"], [ka], scale=CL[:, d, c:c + 1])
                    TT_("dve", tb[:], tb[:], xc[:], ALU.mult, [ktb, kc_], [ktb])
                    yield
                    ACT(ta[:], a_[:], AF.Square, [ka], [kta])
                    yield
                    ACT(ta[:], ta[:], AF.Sqrt, [kta], [kta], scale=-1.0, bias=1.0)
                    yield
                    TT_("dve", tb[:], ta[:], tb[:], ALU.mult, [kta, ktb], [ktb])
                    yield
                    for si, (t0, L_) in enumerate(seqs):
                        init = H0[:, d, c:c + 1] if g == 1 else 0.0
                        if d == 0:
                            o_, a2, u2 = tb[:, t0:t0 + L_], a_[:, t0:t0 + L_], tb[:, t0:t0 + L_]
                        else:
                            o_, a2, u2 = tb[:, t0:t0 + L_][:, ::-1], a_[:, t0:t0 + L_][:, ::-1], tb[:, t0:t0 + L_][:, ::-1]
                        S.op("dve", (lambda e, o_=o_, a2=a2, u2=u2, init=init: e.tensor_tensor_scan(
                            out=o_, data0=a2, data1=u2, initial=init, op0=ALU.mult, op1=ALU.add)), [ka, ktb, "h0"], [ktb])
                        if g == 0:
                            col = t0 + L_ - 1 if d == 0 else t0
                            CP("act", FIN[:, d, si, c:c + 1], tb[:, col:col + 1], [ktb], ["fin"])
                        yield
                tb0, tb1 = B_["tb"][0], B_["tb"][1]
                TT_("dve", tb0[:], tb0[:], tb1[:], ALU.add, [K_("tb0"), K_("tb1")], [K_("tb0")])
                yield
                TT_("dve", OB128[:, c, :], tb0[:], ga[:], ALU.mult, [K_("tb0"), kga], ["ob"])
                yield
            def run_rr2(gens):
                gens = list(gens)
                while gens:
                    nxt = []
                    for g_ in gens:
                        try:
                            next(g_)
                            nxt.append(g_)
                        except StopIteration:
                            pass
                    gens = nxt
            for half in range(2):
                Wx, wxk = wload(cin[j, half])
                Wg, wgk = wload(cin[j, 2 + half])
                Wxv = Wx[:].rearrange("p (k m) -> p k m", k=8)
                Wgv = Wg[:].rearrange("p (k m) -> p k m", k=8)
                for c0_ in range(0, 4, NSL):
                    run_rr2([chunk_gen(s_, half * 4 + c0_ + s_, c0_ + s_, Wxv, Wgv, wxk, wgk) for s_ in range(NSL)])
            if g == 0:
                for d in range(2):
                    for si in range(4):
                        S.dma("sp", o_lru[j, d, si], FIN[:, d, si, :], reads=["fin"], writes=["o_lru"])
            for mb in range(2):
                W, wk = wload(cout[j, mb])
                Wv = W[:].rearrange("p (k m) -> p k m", k=8)
                for mi in range(4):
                    m = mb * 4 + mi
                    for tt in range(2):
                        sl = slice(tt * TT, (tt + 1) * TT)
                        ps, pk = newps()
                        MM(ps[:], [(Wv[:, kc, mi * 128:(mi + 1) * 128], OB128[:, kc, sl]) for kc in range(8)], [wk, "ob"], [pk])
                        STT(Y[:, m, sl], ps[:], mod(l, g, 2)[:, m:m + 1], Y[:, m, sl], ALU.mult, ALU.add, [pk, "y", "mod%d" % l], ["y"])

        DNP = {}
        ONESBD = sb([128, 128], BF16)
        MEMSET("dve", ONESBD[:], 0.0, ["onesbd"])
        MEMSET("dve", ONESBD[0:64, 0:64], 1.0, ["onesbd"])
        MEMSET("dve", ONESBD[64:128, 64:128], 1.0, ["onesbd"])
        def deltanet(l, g, ctx0):
            j = l // 2
            seqs = [(s * 256, 4) for s in range(4)] if g == 0 else [(0, 16)]
            P16 = sb([16, 2], F32, ctx0)
            S.dma("sp", P16[:], abp16[j], writes=["p16"])
            P128 = sb([128, 61], F32, ctx0)
            S.dma("sp", P128[:], abp128[j], writes=["p128"])
            DNP["CW2"] = P128[:, 0:48].rearrange("p (h k) -> p h k", k=4)
            DNP["CB2"] = P128[:, 48:60]
            DNP["DNG2"] = P128[:, 60:61]
            BETA = sb([128, NT], F32, ctx0)
            GCF = sb([128, NT], F32, ctx0)
            GCB = sb([128, NT], F32, ctx0)
            NEA = sb([16, 1], F32, ctx0)
            cs_ = ExitStack()
            G = sb([16, NT], F32, cs_)
            CM = sb([16, 2048], F32, cs_)
            S.dma("sp", CM[:], cmask[:, :], writes=["cm"])
            CMF = CM[:, 0:1024]
            CMB = CM[:, 1024:2048]
            MEMSET("dve", BETA[:], 0.0, ["beta"])
            MEMSET("dve", GCF[:], 0.0, ["gcf"])
            MEMSET("dve", GCB[:], 0.0, ["gcb"])
            ACT(NEA[:], P16[:, 0:1], AF.Exp, ["p16"], ["nea"])
            TS("dve", NEA[:], NEA[:], -1.0, None, ALU.mult, ALU.bypass, ["nea"], ["nea"])
            W5, w5k = wload(abin[j, 5])
            W5v = W5[:].rearrange("p (k m) -> p k m", k=8)
            for tt in range(2):
                sl = slice(tt * TT, (tt + 1) * TT)
                ps, pk = newps()
                MM(ps[0:16, :], [(W5v[:, kc, 0:16], HT[:, kc, sl]) for kc in range(8)], [w5k, "ht"], [pk])
                ACT(G[:, sl], ps[0:16, :], AF.Exp, [pk, "p16"], ["g"], bias=P16[:, 1:2], scale=1.0)
                ACT(G[:, sl], G[:, sl], AF.Ln, ["g"], ["g"], bias=1.0, scale=1.0)
                TS("dve", G[:, sl], G[:, sl], NEA[:, 0:1], None, ALU.mult, ALU.bypass, ["g", "nea"], ["g"])
                ps, pk = newps()
                MM(ps[0:16, :], [(W5v[:, kc, 16:32], HT[:, kc, sl]) for kc in range(8)], [w5k, "ht"], [pk])
                ACT(BETA[0:16, sl], ps[0:16, :], AF.Sigmoid, [pk, "beta"], ["beta"])
            S.op("dve", lambda e: e.tensor_tensor_scan(out=GCF[0:16, :], data0=CMF, data1=G[:], initial=0.0, op0=ALU.mult, op1=ALU.add),
                 ["g", "cm", "gcf"], ["gcf"])
            S.op("dve", lambda e: e.tensor_tensor_scan(out=GCB[0:16, :][:, ::-1], data0=CMB[:, ::-1], data1=G[:, ::-1], initial=0.0,
                                                       op0=ALU.mult, op1=ALU.add), ["g", "cm", "gcb"], ["gcb"])
            CP("act", BETA[64:80, :], BETA[0:16, :], ["beta"], ["beta"])
            CP("act", GCF[64:80, :], GCF[0:16, :], ["gcf"], ["gcf"])
            CP("act", GCB[64:80, :], GCB[0:16, :], ["gcb"], ["gcb"])
            S.barrier()
            cs_.close()
            if DN_STAGE < 2:
                return
            for hh in range(2):
                with ExitStack() as ctx:
                    deltanet_half(l, g, j, hh, seqs, ctx, BETA, GCF, GCB)
                S.barrier()

        def deltanet_half(l, g, j, hh, seqs, ctx, BETA, GCF, GCB):
            CW2, CB2, DNG2 = DNP["CW2"], DNP["CB2"], DNP["DNG2"]
            HW_ = 128
            QT = sb([128, 2, NT], BF16, ctx)
            KT = sb([128, 2, NT], BF16, ctx)
            VT = sb([128, 2, NT], BF16, ctx)
            GATE = sb([128, 2, NT], BF16, ctx)
            OACC = sb([128, 2, NT], F32, ctx)
            SQ = sb([128, TT], BF16, ctx)
            RS = sb([128, TT], F32, ctx)
            c2 = ctx
            SQ2 = [SQ, sb([128, TT], BF16, c2)]
            RS2 = [RS, sb([128, TT], F32, c2)]
            RAW = [sb([128, NT], F32, c2) for _ in range(2)]
            CV = [sb([128, NT], F32, c2) for _ in range(2)]
            nseq = 4 if g == 0 else 1
            L = NT // nseq
            n = 0
            for blk in range(4):
                W, wk = wload(abin[j, blk])
                Wv = W[:].rearrange("p (k m) -> p k m", k=8)
                for pr in range(2):
                    h0 = hh * NH + 2 * pr
                    b = n % 2; n += 1
                    raw, cv = RAW[b], CV[b]
                    kr, kv_ = "raw%d" % b, "cv%d" % b
                    for tt in range(2):
                        sl = slice(tt * TT, (tt + 1) * TT)
                        ps, pk = newps()
                        MM(ps[:], [(Wv[:, kc, h0 * 64:(h0 + 2) * 64], HT[:, kc, sl]) for kc in range(8)], [wk, "ht"], [pk])
                        if blk == 3:
                            ACT(GATE[:, pr, sl], ps[:], AF.Silu, [pk], ["gate"])
                        else:
                            CP("act", raw[:, sl], ps[:], [pk], [kr])
                    if blk == 3:
                        continue
                    pbi = blk * 4 + hh * 2 + pr
                    TS("dve", cv[:], raw[:], CW2[:, pbi, 2:3], CB2[:, pbi:pbi + 1], ALU.mult, ALU.add, [kr, "p128"], [kv_])
                    r3 = raw[:].rearrange("p (s t) -> p s t", s=nseq)
                    c3 = cv[:].rearrange("p (s t) -> p s t", s=nseq)
                    for tap, o in ((0, -2), (1, -1), (3, 1)):
                        d0, d1 = max(0, -o), L - max(0, o)
                        STT(c3[:, :, d0:d1], r3[:, :, d0 + o:d1 + o], CW2[:, pbi, tap:tap + 1], c3[:, :, d0:d1], ALU.mult, ALU.add,
                            [kr, kv_, "p128"], [kv_])
                    if blk == 2:
                        ACT(VT[:, pr, :], cv[:], AF.Silu, [kv_], ["vt"])
                        continue
                    ACT(cv[:], cv[:], AF.Silu, [kv_], [kv_])
                    dst = QT if blk == 0 else KT
                    dk_ = "qt" if blk == 0 else "kt"
                    sls = [slice(tt * TT, (tt + 1) * TT) for tt in range(2)]
                    pss = []
                    for tt in range(2):
                        ACT(SQ2[tt][:], cv[:, sls[tt]], AF.Square, [kv_], ["dsq%d" % tt])
                    for tt in range(2):
                        ps, pk = newps()
                        pss.append((ps, pk))
                        MM(ps[:], [(ONESBD[:], SQ2[tt][:])], ["dsq%d" % tt, "onesbd"], [pk])
                    for tt in range(2):
                        ACT(RS2[tt][:], pss[tt][0][:], AF.Ln, [pss[tt][1]], ["drs%d" % tt], bias=EPS, scale=1.0)
                    for tt in range(2):
                        ACT(RS2[tt][:], RS2[tt][:], AF.Exp, ["drs%d" % tt], ["drs%d" % tt], scale=-0.5)
                    for tt in range(2):
                        STT(dst[:, pr, sls[tt]], cv[:, sls[tt]], (0.125 if blk == 0 else 1.0), RS2[tt][:], ALU.mult, ALU.mult,
                            [kv_, "drs%d" % tt], [dk_])
            if DN_STAGE < 3:
                return
            KS = 4
            def mk(shape, dt=F32):
                return sb(shape, dt, ctx)
            SL = []
            for k in range(KS):
                SL.append(dict(A=mk([128, HW_]), Bm=mk([128, HW_]), EGT=mk([128, HW_]),
                               KBc=mk([128, HW_], BF16), QGc=mk([128, HW_], BF16), INTR=mk([128, HW_], BF16), KDEC=mk([128, HW_], BF16),
                               VTOK=mk([128, HW_], BF16), KTOK=mk([128, HW_], BF16), QX=mk([128, 2 * HW_]), QT=mk([128, HW_]),
                               TKS=mk([128, 10]), SELGL=mk([128, 2]), RT=mk([128, HW_]), VNB=mk([128, HW_], BF16)))
            for k in range(KS):
                MEMSET("dve", SL[k]["SELGL"][:], 0.0, ["selgl_%d" % k])
            SS = {}
            for si in range(min(2, len(seqs))):
                for d in range(2):
                    SS[(si, d)] = (sb([128, HW_], F32, ctx), sb([128, HW_], BF16, ctx))
            def init_state(si):
                for d in range(2):
                    Sf, Sb_ = SS[(si % 2, d)]
                    sk = ("S", si % 2, d)
                    if g == 1:
                        S.dma("sp", Sf[:], sdelta[j, d, hh], writes=[sk])
                    else:
                        MEMSET("dve", Sf[:], 0.0, [sk])
                    CP("act", Sb_[:], Sf[:], [sk], [("Sb", si % 2, d)])
            oacc_written = set()
            def h2(ap):
                return ap.rearrange("p (i x) -> p i x", i=2)
            def bc(ap):
                return ap.unsqueeze(2).to_broadcast([128, 2, 64])
            GP = ((0, slice(0, 64), slice(0, 16)), (1, slice(64, 128), slice(64, 80)))
            def geom(si, d, ck):
                t0, nch = seqs[si]
                chunk = ck if d == 0 else nch - 1 - ck
                c0 = t0 + chunk * 64
                return chunk, c0, slice(c0, c0 + 64)
            def partA(k, si, d, ck):
                sl_ = SL[k]
                X, Xk = PSB[2 * k], "ps%d" % (2 * k)
                Yb, Yk = PSB[2 * k + 1], "ps%d" % (2 * k + 1)
                chunk, c0, cs = geom(si, d, ck)
                GC = GCF if d == 0 else GCB
                gck = "gcf" if d == 0 else "gcb"
                last = 63 if d == 0 else 0
                NEGM = NEG_LO2 if d == 0 else NEG_UP2
                NEGMT = NEG_UP2 if d == 0 else NEG_LO2
                K_ = lambda n_: "%s_%d" % (n_, k)
                A, Bm, EGT = sl_["A"], sl_["Bm"], sl_["EGT"]
                KBc, QGc, INTR, KDEC, VTOK, KTOK = sl_["KBc"], sl_["QGc"], sl_["INTR"], sl_["KDEC"], sl_["VTOK"], sl_["KTOK"]
                QX, QT_, tks, SELGL = sl_["QX"], sl_["QT"], sl_["TKS"], sl_["SELGL"]
                blk = BLK2(d, hh)
                sel = SEL2(d, hh)
                TT_("pool", h2(A[0:80, :]), GC[0:80, cs].unsqueeze(1).to_broadcast([80, 2, 64]), h2(blk[0:80, :]), ALU.mult, [gck, "cs"], [K_("A")])
                TT_("pool", h2(Bm[0:80, :]), BETA[0:80, cs].unsqueeze(1).to_broadcast([80, 2, 64]), h2(blk[0:80, :]), ALU.mult, ["beta", "cs"], [K_("Bm")])
                TS("pool", SELGL[0:80, :], sel[0:80, :], GC[0:80, c0 + last:c0 + last + 1], None, ALU.mult, ALU.bypass, [gck, "cs"], [K_("selgl")])
                MMS([(X[pr_, i * 64:(i + 1) * 64], VT[pr_, i, cs], IDB2[pr_, :]) for (gp, pr_, p16) in GP for i in range(2)], ["vt", "idb2"], [Xk])
                MMS([(Yb[pr_, i * 64:(i + 1) * 64], KT[pr_, i, cs], IDB2[pr_, :]) for (gp, pr_, p16) in GP for i in range(2)], ["kt", "idb2"], [Yk])
                yield
                CP("act", VTOK[:], X[:, 0:HW_], [Xk], [K_("vtok")])
                CP("dve", KTOK[:], Yb[:, 0:HW_], [Yk], [K_("ktok")])
                yield
                def fE(e):
                    ins = None
                    for (gp, pr_, p16) in GP:
                        e.matmul(X[pr_, 0:HW_], lhsT=GC[p16, cs], rhs=blk[p16, :], start=True, stop=False)
                        e.matmul(X[pr_, 0:HW_], lhsT=NEGONES16[p16, :], rhs=A[p16, :], start=False, stop=True)
                        e.matmul(X[pr_, HW_:HW_ + 2], lhsT=BETA[p16, cs], rhs=sel[p16, :], start=True, stop=True)
                        e.matmul(X[pr_, HW_ + 2:HW_ + 4], lhsT=GC[p16, cs], rhs=sel[p16, :], start=True, stop=True)
                        ins = e.matmul(X[pr_, HW_ + 4:HW_ + 6], lhsT=ONES16[p16, :], rhs=SELGL[p16, :], start=True, stop=True)
                    return ins
                S.op("pe", fE, [gck, "beta", "cs", "ones16", K_("A"), K_("selgl")], [Xk])
                MMS([(Yb[pr_, 0:HW_], ONES16[p16, :], Bm[p16, :]) for (gp, pr_, p16) in GP] +
                    [(Yb[pr_, HW_:2 * HW_], ONES16[p16, :], A[p16, :]) for (gp, pr_, p16) in GP], ["ones16", K_("A"), K_("Bm")], [Yk])
                yield
                ACT(EGT[:], Yb[:, HW_:2 * HW_], AF.Exp, [Yk], [K_("EGT")])
                TT_("dve", h2(KBc[:]), KT[:, :, cs], h2(Yb[:, 0:HW_]), ALU.mult, ["kt", Yk], [K_("kbc")])
                ACT(tks[:, 2:6], X[:, HW_ + 2:HW_ + 6], AF.Exp, [Xk], [K_("tks")])
                ACT(tks[:, 6:8], h2(X[:, 0:HW_])[:, :, last], AF.Exp, [Xk], [K_("tks")], scale=-1.0)
                CP("dve", tks[:, 0:2], X[:, HW_:HW_ + 2], [Xk], [K_("tks")])
                yield
                TT_("dve", A[:], X[:, 0:HW_], NEGM, ALU.add, [Xk, "cd"], [K_("A")])
                STT(Bm[:], X[:, 0:HW_], -1.0, NEGMT, ALU.mult, ALU.add, [Xk, "cd"], [K_("Bm")])
                yield
                ACT(A[:], A[:], AF.Exp, [K_("A")], [K_("A")])
                ACT(Bm[:], Bm[:], AF.Exp, [K_("Bm")], [K_("Bm")])
                TT_("pool", h2(QGc[:]), QT[:, :, cs], h2(EGT[:]), ALU.mult, ["qt", K_("EGT")], [K_("qgc")])
                TT_("pool", h2(KDEC[:]), h2(KTOK[:]), bc(tks[:, 6:8]), ALU.mult, [K_("ktok"), K_("tks")], [K_("kdec")])
                yield
                TT_("pool", tks[:, 8:10], tks[:, 0:2], tks[:, 2:4], ALU.mult, [K_("tks")], [K_("tks")])
                kb3 = h2(KBc[:])
                MMS([(Yb[pr_, i * 64:(i + 1) * 64], kb3[pr_, i, :], KT[pr_, i, cs]) for (gp, pr_, p16) in GP for i in range(2)] +
                    [(Yb[pr_, HW_ + i * 64:HW_ + (i + 1) * 64], KT[pr_, i, cs], kb3[pr_, i, :]) for (gp, pr_, p16) in GP for i in range(2)],
                    [K_("kbc"), "kt"], [Yk])
                MMS([(X[pr_, i * 64:(i + 1) * 64], KT[pr_, i, cs], QT[pr_, i, cs]) for (gp, pr_, p16) in GP for i in range(2)], ["kt", "qt"], [Xk])
                yield
                qx4 = QX[:].rearrange("p (i two x) -> p i two x", i=2, two=2)
                qx3 = QX[:].rearrange("p (i y) -> p i y", i=2)
                qt3 = h2(QT_[:])
                STT(qt3, h2(Yb[:, 0:HW_]), -1.0, h2(A[:]), ALU.mult, ALU.mult, [Yk, K_("A")], [K_("qtn")])
                STT(qx4[:, :, 0, :], h2(Yb[:, HW_:2 * HW_]), -1.0, h2(Bm[:]), ALU.mult, ALU.mult, [Yk, K_("Bm")], [K_("qx")])
                yield
                TT_("pool", Bm[:], Bm[:], IDENT2, ALU.add, [K_("Bm"), "cd"], [K_("Bm")])
                TT_("dve", INTR[:], X[:, 0:HW_], Bm[:], ALU.mult, [Xk, K_("Bm")], [K_("intr")])
                TT_("pool", h2(A[:]), h2(VTOK[:]), bc(tks[:, 0:2]), ALU.mult, [K_("vtok"), K_("tks"), K_("A")], [K_("A")])
                yield
                for lev in range(5):
                    if lev == 0:
                        MMS([(X[pr_, i * 64:(i + 1) * 64], qt3[pr_, i, :], qx4[pr_, i, 0, :]) for (gp, pr_, p16) in GP for i in range(2)],
                            [K_("qtn"), K_("qx")], [Xk])
                    elif lev == 4:
                        MMS([(X[pr_, i * 64:(i + 1) * 64], qt3[pr_, i, :], qx4[pr_, i, 1, :]) for (gp, pr_, p16) in GP for i in range(2)],
                            [K_("qtn"), K_("qx")], [Xk])
                    else:
                        MMS([(X[pr_, i * 128:(i + 1) * 128], qt3[pr_, i, :], qx3[pr_, i, :]) for (gp, pr_, p16) in GP for i in range(2)],
                            [K_("qtn"), K_("qx")], [Xk])
                    MMS([(Yb[pr_, i * 64:(i + 1) * 64], qx4[pr_, i, 0, :], qt3[pr_, i, :]) for (gp, pr_, p16) in GP for i in range(2)],
                        [K_("qtn"), K_("qx")], [Yk])
                    yield
                    x4 = X[:, 0:2 * HW_].rearrange("p (i two x) -> p i two x", i=2, two=2)
                    if lev == 0:
                        TT_("pool", qx4[:, :, 1, :], qx4[:, :, 0, :], h2(IDENT2), ALU.add, [K_("qx"), "cd"], [K_("qx")])
                        CP("act", qx4[:, :, 0, :], h2(X[:, 0:HW_]), [Xk], [K_("qx")])
                    elif lev == 4:
                        TT_("dve", qx4[:, :, 1, :], h2(X[:, 0:HW_]), qx4[:, :, 1, :], ALU.add, [Xk, K_("qx")], [K_("qx")])
                    else:
                        CP("act", qx4[:, :, 0, :], x4[:, :, 0, :], [Xk], [K_("qx")])
                        TT_("dve", qx4[:, :, 1, :], x4[:, :, 1, :], qx4[:, :, 1, :], ALU.add, [Xk, K_("qx")], [K_("qx")])
                    CP("act", QT_[:], Yb[:, 0:HW_], [Yk], [K_("qtn")])
                    yield
                MMS([(X[pr_, i * 64:(i + 1) * 64], qt3[pr_, i, :], qx4[pr_, i, 1, :]) for (gp, pr_, p16) in GP for i in range(2)],
                    [K_("qtn"), K_("qx")], [Xk])
                yield
                TT_("dve", h2(EGT[:]), h2(X[:, 0:HW_]), qx4[:, :, 1, :], ALU.add, [Xk, K_("qx"), K_("EGT")], [K_("EGT")])
                yield

            def partB(k, si, d, ck):
                sl_ = SL[k]
                X, Xk = PSB[2 * k], "ps%d" % (2 * k)
                Yb, Yk = PSB[2 * k + 1], "ps%d" % (2 * k + 1)
                chunk, c0, cs = geom(si, d, ck)
                K_ = lambda n_: "%s_%d" % (n_, k)
                BV, QGc, INTR, KDEC = sl_["A"], sl_["QGc"], sl_["INTR"], sl_["KDEC"]
                tks, RT, VNB, TTB, SD = sl_["TKS"], sl_["RT"], sl_["VNB"], sl_["EGT"], sl_["Bm"]
                Sf, Sb_ = SS[(si % 2, d)]
                sk, sbk = ("S", si % 2, d), ("Sb", si % 2, d)
                Sb3, Sf3 = h2(Sb_[:]), h2(Sf[:])
                MMS([(X[pr_, i * 64:(i + 1) * 64], KT[pr_, i, cs], Sb3[pr_, i, :]) for (gp, pr_, p16) in GP for i in range(2)], ["kt", sbk], [Xk])
                TT_("pool", h2(SD[:]), Sf3, bc(tks[:, 4:6]), ALU.mult, [sk, K_("tks"), K_("Bm")], [K_("Bm")])
                yield
                TT_("dve", h2(RT[:]), h2(X[:, 0:HW_]), bc(tks[:, 8:10]), ALU.mult, [Xk, K_("tks")], [K_("rt")])
                yield
                TT_("dve", RT[:], BV[:], RT[:], ALU.subtract, [K_("A"), K_("rt")], [K_("rt")])
                yield
                tt3, r3 = h2(TTB[:]), h2(RT[:])
                MMS([(Yb[pr_, i * 64:(i + 1) * 64], tt3[pr_, i, :], r3[pr_, i, :]) for (gp, pr_, p16) in GP for i in range(2)], [K_("EGT"), K_("rt")], [Yk])
                yield
                CP("act", VNB[:], Yb[:, 0:HW_], [Yk], [K_("vnb")])
                yield
                vn3, qg3, in3, kd3 = h2(VNB[:]), h2(QGc[:]), h2(INTR[:]), h2(KDEC[:])
                MMS([(Yb[pr_, i * 64:(i + 1) * 64], kd3[pr_, i, :], vn3[pr_, i, :]) for (gp, pr_, p16) in GP for i in range(2)], [K_("kdec"), K_("vnb")], [Yk])
                def fo(e):
                    ins = None
                    for (gp, pr_, p16) in GP:
                        for i in range(2):
                            e.matmul(X[pr_, i * 64:(i + 1) * 64], lhsT=Sb3[pr_, i, :], rhs=qg3[pr_, i, :], start=True, stop=False)
                            ins = e.matmul(X[pr_, i * 64:(i + 1) * 64], lhsT=vn3[pr_, i, :], rhs=in3[pr_, i, :], start=False, stop=True)
                    return ins
                S.op("pe", fo, [sbk, K_("qgc"), K_("vnb"), K_("intr")], [Xk])
                yield
                TT_("dve", Sb_[:], Yb[:, 0:HW_], SD[:], ALU.add, [Yk, K_("Bm")], [sbk])
                TT_("dve", Sf[:], Yb[:, 0:HW_], SD[:], ALU.add, [Yk, K_("Bm")], [sk])
                ok_ = ("oacc", si, chunk)
                if ok_ not in oacc_written:
                    oacc_written.add(ok_)
                    CP("act", OACC[:, :, cs], h2(X[:, 0:HW_]), [Xk], [ok_])
                else:
                    TT_("dve", OACC[:, :, cs], h2(X[:, 0:HW_]), OACC[:, :, cs], ALU.add, [Xk, ok_], [ok_])
                yield

            def run_rr(gens):
                gens = list(gens)
                while gens:
                    nxt = []
                    for g_ in gens:
                        try:
                            next(g_)
                            nxt.append(g_)
                        except StopIteration:
                            pass
                    gens = nxt

            for sp_ in range(0, len(seqs), 2):
                sis = [si for si in (sp_, sp_ + 1) if si < len(seqs)]
                for si in sis:
                    init_state(si)
                nck = seqs[sis[0]][1] if DN_STAGE > 3 else 1
                steps = [(si, d, ck) for ck in range(nck) for si in sis for d in range(2)]
                for w0 in range(0, len(steps), KS):
                    win = steps[w0:w0 + KS]
                    run_rr([partA(k, *st) for k, st in enumerate(win)])
                    pending = list(enumerate(win))
                    while pending:
                        seen, rnd, rest = set(), [], []
                        for k, st in pending:
                            ch = (st[0], st[1])
                            if ch in seen:
                                rest.append((k, st))
                            else:
                                seen.add(ch)
                                rnd.append((k, st))
                        run_rr([partB(k, *st) for k, st in rnd])
                        pending = rest
                if g == 0:
                    for si in sis:
                        for d in range(2):
                            S.dma("sp", o_dn[si, j, d, hh], SS[(si % 2, d)][0][:], reads=[("S", si % 2, d)], writes=["o_dn"])
            if DN_STAGE < 9:
                return
            allo = [("oacc", si, c) for si in range(len(seqs)) for c in range(seqs[si][1])]
            TO = sb([128, TT], F32, ctx)
            for i in range(2):
                for tt in range(2):
                    sl = slice(tt * TT, (tt + 1) * TT)
                    ACT(SQ[:], OACC[:, i, sl], AF.Square, allo, ["dsq0"])
                    ps, pk = newps()
                    MM(ps[:], [(ONESBD[:], SQ[:])], ["dsq0", "onesbd"], [pk])
                    ACT(RS[:], ps[:], AF.Ln, [pk], ["drs0"], bias=EPS, scale=1.0 / 64)
                    ACT(RS[:], RS[:], AF.Exp, ["drs0"], ["drs0"], scale=-0.5)
                    STT(TO[:], OACC[:, i, sl], DNG2, RS[:], ALU.mult, ALU.mult, allo + ["drs0", "p128"], ["to"])
                    TT_("dve", OBt[:, hh * 2 + i, sl], TO[:], GATE[:, i, sl], ALU.mult, ["to", "gate"], ["ob"])

        def attention(l, g, ctx):
            j = l // 2
            P64 = sb([64, 24 * 5 + 11], F32, ctx)
            S.dma("sp", P64[:], abp64[j], writes=["p64b"])
            QG, KG = P64[:, 121:122], P64[:, 122:123]
            ESINK = sb([64, 8], F32, ctx)
            ACT(ESINK[:], P64[:, 123:131], AF.Exp, ["p64b"], ["esink"])
            QB = sb([64, 8, NT], BF16, ctx)
            KB = sb([64, 2, NT], BF16, ctx)
            KN = sb([64, 2, NT], F32, ctx)
            VB = sb([128, 8, 128], BF16, ctx)
            VF = sb([128, 8, 128], F32, ctx)
            SQa = [sb([128, TT], BF16, ctx) for _ in range(2)]
            RSa = [sb([128, TT], F32, ctx) for _ in range(2)]
            QN = [sb([128, TT], F32, ctx) for _ in range(2)]
            QNBa = [sb([128, TT], BF16, ctx) for _ in range(2)]
            T1a = [sb([128, TT], F32, ctx) for _ in range(2)]
            T2a = [sb([128, TT], F32, ctx) for _ in range(2)]
            PERMB = sb([128, 128], BF16, ctx)
            G128 = sb([128, 2], F32, ctx)
            if g == 1:
                CSN = sb([128, 2048], F32, ctx)
                for hf in range(2):
                    S.dma("sp", CSN[hf * 64:(hf + 1) * 64, :], cossin[:, :], writes=["csn"])
                COS = CSN[:, 0:1024]
                SIN = CSN[:, 1024:2048]
            MEMSET("dve", PERMB[:], 0.0, ["permb"])
            CP("dve", PERMB[0:64, 0:64], PERMT, ["c64", "permb"], ["permb"])
            CP("dve", PERMB[64:128, 64:128], PERMT, ["c64", "permb"], ["permb"])
            for hf in range(2):
                S.dma("sp", G128[hf * 64:(hf + 1) * 64, :], abp64[j][:, 121:123], writes=["g128"])
            W4, w4k = wload(abin[j, 4])
            W5, w5k = wload(abin[j, 5])
            W4v = W4[:].rearrange("p (k m) -> p k m", k=8)
            W5v = W5[:].rearrange("p (k m) -> p k m", k=8)
            n = 0
            for pp in range(5):
                sls = [slice(tt * TT, (tt + 1) * TT) for tt in range(2)]
                if pp < 4:
                    gn = G128[:, 0:1]
                    dkey = "qb"
                    dsts = [[QB[:, 2 * pp, sls[tt]], QB[:, 2 * pp + 1, sls[tt]]] for tt in range(2)]
                else:
                    gn = G128[:, 1:2]
                    dkey = "kb"
                    dsts = [[KB[:, 0, sls[tt]], KB[:, 1, sls[tt]]] for tt in range(2)]
                pss, pss2, pss3 = [], [], []
                for tt in range(2):
                    ps, pk = newps()
                    pss.append((ps, pk))
                    if pp < 4:
                        MM(ps[:], [(W4v[:, kc, pp * 128:(pp + 1) * 128], HT[:, kc, sls[tt]]) for kc in range(8)], [w4k, "ht"], [pk])
                    else:
                        MM(ps[:], [(W5v[:, kc, 32:160], HT[:, kc, sls[tt]]) for kc in range(8)], [w5k, "ht"], [pk])
                for tt in range(2):
                    ACT(SQa[tt][:], pss[tt][0][:], AF.Square, [pss[tt][1]], ["asq%d" % tt])
                for tt in range(2):
                    ps2, pk2 = newps()
                    pss2.append((ps2, pk2))
                    MM(ps2[:], [(ONESBD[:], SQa[tt][:])], ["asq%d" % tt, "onesbd"], [pk2])
                for tt in range(2):
                    ACT(RSa[tt][:], pss2[tt][0][:], AF.Ln, [pss2[tt][1]], ["ars%d" % tt], bias=EPS, scale=1.0 / 64)
                for tt in range(2):
                    ACT(RSa[tt][:], RSa[tt][:], AF.Exp, ["ars%d" % tt], ["ars%d" % tt], scale=-0.5)
                for tt in range(2):
                    STT(QN[tt][:], pss[tt][0][:], gn, RSa[tt][:], ALU.mult, ALU.mult, [pss[tt][1], "ars%d" % tt, "g128"], ["qn%d" % tt])
                if g == 0:
                    for tt in range(2):
                        for hf in range(2):
                            CP("act", dsts[tt][hf], QN[tt][hf * 64:(hf + 1) * 64, :], ["qn%d" % tt], [dkey])
                            if pp == 4:
                                CP("act", KN[:, hf, sls[tt]], QN[tt][hf * 64:(hf + 1) * 64, :], ["qn%d" % tt], ["kn"])
                else:
                    for tt in range(2):
                        CP("act", QNBa[tt][:], QN[tt][:], ["qn%d" % tt], ["qnb%d" % tt])
                    for tt in range(2):
                        ps3, pk3 = newps()
                        pss3.append((ps3, pk3))
                        MM(ps3[:], [(PERMB[:], QNBa[tt][:])], ["permb", "qnb%d" % tt], [pk3])
                    for tt in range(2):
                        TT_("dve", T1a[tt][:], QN[tt][:], COS[:, sls[tt]], ALU.mult, ["qn%d" % tt, "csn"], ["at1%d" % tt])
                    for tt in range(2):
                        TT_("dve", T2a[tt][:], pss3[tt][0][:], SIN[:, sls[tt]], ALU.mult, [pss3[tt][1], "csn"], ["at2%d" % tt])
                    for tt in range(2):
                        for hf in range(2):
                            TT_("dve", dsts[tt][hf], T1a[tt][hf * 64:(hf + 1) * 64, :], T2a[tt][hf * 64:(hf + 1) * 64, :], ALU.add,
                                ["at1%d" % tt, "at2%d" % tt], [dkey])
            if g == 0:
                S.dma("sp", o_k[j], KN[:], reads=["kn"], writes=["o_k"])
            for tb in range(8):
                ps, pk = newps()
                MM(ps[:, 0:128], [(HT[:, kc, tb * 128:(tb + 1) * 128], W5v[:, kc, 160:288]) for kc in range(8)], [w5k, "ht"], [pk])
                CP("act", VB[:, tb, :], ps[:, 0:128], [pk], ["vb"])
                if g == 0:
                    CP("dve", VF[:, tb, :], ps[:, 0:128], [pk], ["vf"])
            if g == 0:
                S.dma("sp", o_v[j], VF[:], reads=["vf"], writes=["o_v"])
            if g == 1:
                KCF = sb([64, 2, 256], F32, ctx)
                KC = sb([64, 2, 256], BF16, ctx)
                VCF = sb([128, 2, 128], F32, ctx)
                VC = sb([128, 2, 128], BF16, ctx)
                S.dma("sp", KCF[:], kcT[j], writes=["kcf"])
                S.dma("sp", VCF[:], vc[j], writes=["vcf"])
                CP("dve", KC[:], KCF[:], ["kcf"], ["kc"])
                CP("dve", VC[:], VCF[:], ["vcf"], ["vcb"])
            PT = [sb([128, 5, 512], BF16, ctx) for _ in range(2)]
            DENs = [sb([64, 512], F32, ctx) for _ in range(2)]
            items = [(qb, kv) for qb in range(8) for kv in range(2)]
            kbls = {}
            def head(it):
                qb, kv = items[it]
                qs = slice(qb * 128, (qb + 1) * 128)
                if g == 0:
                    s_ = qb // 2
                    kbl = [("lat", 2 * s_, None), ("lat", 2 * s_ + 1, None)]
                else:
                    kbl = []
                    if qb > 0:
                        kbl.append(("lat", qb - 1, MASK_LO))
                    kbl.append(("lat", qb, None))
                    if qb < 7:
                        kbl.append(("lat", qb + 1, MASK_UP))
                    kbl += [("ctx", 0, None), ("ctx", 1, None)]
                kbls[it] = kbl
                pt = PT[it % 2]; ptk = "pt%d" % (it % 2)
                for bi, (kind, kb_, msk) in enumerate(kbl):
                    ps, pk = newps()
                    if kind == "lat":
                        lh, lk = KB[:, kv, kb_ * 128:(kb_ + 1) * 128], "kb"
                    else:
                        lh, lk = KC[:, kv, kb_ * 128:(kb_ + 1) * 128], "kc"
                    MM(ps[:].rearrange("p (h q) -> p h q", h=4), [(lh, QB[:, 4 * kv:4 * kv + 4, qs])], [lk, "qb"], [pk])
                    ACT(pt[:, bi, :], ps[:], AF.Exp, [pk], [(ptk, bi)], scale=0.125)
                    if msk is not None:
                        TT_("dve", pt[:, bi, :].rearrange("p (h q) -> p h q", h=4), pt[:, bi, :].rearrange("p (h q) -> p h q", h=4),
                            msk.unsqueeze(1).to_broadcast([128, 4, 128]), ALU.mult, [(ptk, bi), "c128"], [(ptk, bi)])
            def tail(it):
                qb, kv = items[it]
                qs = slice(qb * 128, (qb + 1) * 128)
                kbl = kbls[it]
                pt = PT[it % 2]; ptk = "pt%d" % (it % 2)
                DEN = DENs[it % 2]; dnk = "den%d" % (it % 2)
                nb = len(kbl)
                pv, pvk = newps()
                prs = []
                for bi, (kind, kb_, msk) in enumerate(kbl):
                    if kind == "lat":
                        prs.append((VB[:, kb_, kv * 64:(kv + 1) * 64], pt[:, bi, :]))
                    else:
                        prs.append((VC[:, kb_, kv * 64:(kv + 1) * 64], pt[:, bi, :]))
                MM(pv[0:64, :], prs, ["vb", "vcb"] + [(ptk, bi) for bi in range(nb)], [pvk])
                pd, pdk = newps()
                MM(pd[0:64, :], [(ONESB[:, 0:64], pt[:, bi, :]) for bi in range(nb)], ["onesb"] + [(ptk, bi) for bi in range(nb)], [pdk])
                TT_("dve", DEN[:].rearrange("p (h q) -> p h q", h=4), pd[0:64, :].rearrange("p (h q) -> p h q", h=4),
                    ESINK[:, 4 * kv:4 * kv + 4].unsqueeze(2).to_broadcast([64, 4, 128]), ALU.add, [pdk, "esink"], [dnk])
                ACT(DEN[:], DEN[:], AF.Ln, [dnk], [dnk])
                ACT(DEN[:], DEN[:], AF.Exp, [dnk], [dnk], scale=-1.0)
                for hi in range(4):
                    TT_("dve", OBH(8 + 4 * kv + hi, qs), pv[0:64, hi * 128:(hi + 1) * 128], DEN[:, hi * 128:(hi + 1) * 128], ALU.mult,
                        [pvk, dnk], ["ob"])
            head(0)
            for it in range(len(items)):
                if it + 1 < len(items):
                    head(it + 1)
                tail(it)

        def out_proj_ab(l, g):
            j = l // 2
            for mb in range(2):
                W, wk = wload(about[j, mb])
                Wv = W[:].rearrange("p (k m) -> p k m", k=8)
                for mi in range(4):
                    m = mb * 4 + mi
                    for tt in range(2):
                        sl = slice(tt * TT, (tt + 1) * TT)
                        ps, pk = newps()
                        MM(ps[:], [(Wv[:, kc, mi * 128:(mi + 1) * 128], OB128[:, kc, sl]) for kc in range(8)], [wk, "ob"], [pk])
                        STT(Y[:, m, sl], ps[:], mod(l, g, 2)[:, m:m + 1], Y[:, m, sl], ALU.mult, ALU.add, [pk, "y", "mod%d" % l], ["y"])

        ada_done = set()
        for g in GROUPS:
            for c in range(8):
                S.dma("sp", Y[:, c, :], xT[g, c * 128:(c + 1) * 128, :], writes=["y"])
            for l in range(DEPTH):
                if l not in ada_done:
                    adaln(l)
                    ada_done.add(l)
                with ExitStack() as ctx:
                    modulate(l, g, 0, ctx)
                S.barrier()
                if l % 2 == 0:
                    if not (DO_A and DO_B):
                        MEMSET("dve", OBt[:], 0.0, ["ob"])
                    if DO_A:
                        with ExitStack() as ctx:
                            deltanet(l, g, ctx)
                        S.barrier()
                    if DO_B:
                        with ExitStack() as ctx:
                            attention(l, g, ctx)
                        S.barrier()
                    out_proj_ab(l, g)
                else:
                    if DO_C:
                        with ExitStack() as ctx:
                            mixer_c(l, g, ctx)
                        S.barrier()
                with ExitStack() as ctx:
                    modulate(l, g, 1, ctx)
                S.barrier()
                if DO_MLP:
                    with ExitStack() as ctx:
                        mlp(l, g, ctx)
                    S.barrier()
            for c in range(8):
                S.dma("sp", yT[g, c * 128:(c + 1) * 128, :], Y[:, c, :], reads=["y"], writes=["yT"])
        S.barrier()
        with nc.Block() as block:
            S.emit(block)
    return nc


_CACHE = {}


def _consts():
    c64 = np.zeros((64, 832), np.float32)
    eye = np.eye(64, dtype=np.float32)
    r = np.arange(64)[:, None]; x = np.arange(64)[None, :]
    lo = np.where(r > x, 0.0, -30000.0).astype(np.float32)
    up = np.where(r < x, 0.0, -30000.0).astype(np.float32)
    for i in range(4):
        c64[:, i * 64:(i + 1) * 64] = eye
        c64[:, 256 + i * 64:256 + (i + 1) * 64] = lo
        c64[:, 512 + i * 64:512 + (i + 1) * 64] = up
    P = np.zeros((64, 64), np.float32)
    for m in range(64):
        if m % 32 < 16:
            P[m, m + 16] = -1.0
        else:
            P[m, m - 16] = 1.0
    c64[:, 768:832] = P.T
    t = np.arange(1024, dtype=np.float32)
    inv = (10000.0 ** (-np.arange(0, 32, 2, dtype=np.float32) / 32)).astype(np.float32)
    ang_r = (np.floor(t / 64)[:, None] * inv).astype(np.float32)
    ang_c = ((t % 64)[:, None] * inv).astype(np.float32)
    ang = np.zeros((64, 1024), np.float32)
    for p in range(64):
        ang[p] = (ang_r if p < 32 else ang_c)[:, p % 16]
    cossin = np.concatenate([np.cos(ang), np.sin(ang)], axis=1).astype(np.float32)
    c16 = np.zeros((16, 1040), np.float32)
    cmask = np.zeros((16, 2048), np.float32)
    for d in range(2):
        for hh in range(2):
            for i in range(4):
                k = d * 8 + hh * 4 + i
                o = (d * 2 + hh) * 256
                c16[k, o + i * 64:o + (i + 1) * 64] = 1.0
                c16[k, 1024 + (d * 2 + hh) * 4 + i] = 1.0
    cm = np.ones(1024, np.float32); cm[0::64] = 0.0
    cmask[:, 0:1024] = cm
    cm = np.ones(1024, np.float32); cm[63::64] = 0.0
    cmask[:, 1024:2048] = cm
    c128 = np.zeros((128, 256), np.float32)
    k = np.arange(128)[:, None]; q = np.arange(128)[None, :]
    c128[:, 0:128] = (q <= k)
    c128[:, 128:256] = (k <= q)
    return c64, c16, c128, cossin, cmask


def _fm(v):
    return np.ascontiguousarray(np.swapaxes(v.reshape(v.shape[:-1] + (8, 128)), -1, -2))


def _wt(w, kc, mcols):
    K, M = w.shape
    a = w.reshape(kc, 128, M // mcols, mcols)
    return np.ascontiguousarray(a.transpose(2, 1, 0, 3)).reshape(M // mcols, 128, kc * mcols)


def kernel(x_prompt, x_sample, state_delta, cache_k, cache_v, state_lru, c, c_ctx,
           ada_w, ada_b, norm1_g, norm2_g, ff_w1, ff_w2,
           ab_w_in, ab_conv_w, ab_conv_b, dn_a_log, dn_dt_bias, dn_norm_g,
           attn_q_norm_g, attn_k_norm_g, attn_sink, ab_w_out,
           c_w_in, c_conv_w, c_conv_b, lru_w_a, lru_b_a, lru_w_x, lru_b_x, lru_lambda, c_w_out):
    f = lambda a: np.asarray(a, dtype=np.float32)
    x_prompt, x_sample = f(x_prompt), f(x_sample)
    NE = (DEPTH + 1) // 2
    NO = max(1, DEPTH // 2)
    NL = DEPTH
    c64, c16, c128, cossin, cmask = _consts()
    shared = {"c64": c64, "c16": c16, "c128": c128, "cossin": cossin, "cmask": cmask}
    shared["ada_w"] = np.stack([_wt(f(ada_w[l]), 8, 512) for l in range(NL)])
    shared["ada_b"] = np.stack([np.ascontiguousarray(f(ada_b[l]).reshape(48, 128).T) for l in range(NL)])
    shared["ng"] = np.stack([np.concatenate([_fm(f(norm1_g[l])), _fm(f(norm2_g[l]))], axis=1) for l in range(NL)])
    shared["w1"] = np.stack([_wt(f(ff_w1[l]), 8, 512) for l in range(NL)])
    shared["w2"] = np.stack([_wt(f(ff_w2[l]), 32, 128) for l in range(NL)])
    abin = []
    for j in range(NE):
        w = f(ab_w_in[j])
        pad = np.zeros((1024, 6 * 512), np.float32)
        pad[:, 0:1536] = w[:, 0:1536]
        pad[:, 1536:2048] = w[:, 1536:2048]
        pad[:, 2048:2560] = w[:, 2080:2592]
        pad[:, 2560:2592] = w[:, 2048:2080]
        pad[:, 2592:2720] = w[:, 2592:2720]
        pad[:, 2720:2848] = w[:, 2720:2848]
        abin.append(_wt(pad, 8, 512))
    shared["abin"] = np.stack(abin)
    shared["about"] = np.stack([_wt(f(ab_w_out[j]), 8, 512) for j in range(NE)])
    p64 = np.zeros((NE, 64, 24 * 5 + 11), np.float32)
    for j in range(NE):
        cw = f(ab_conv_w[j]).reshape(4, 24, 64)
        p64[j, :, 0:96] = cw.transpose(2, 1, 0).reshape(64, 96)
        p64[j, :, 96:120] = f(ab_conv_b[j]).reshape(24, 64).T
        p64[j, :, 120] = f(dn_norm_g[j])
        p64[j, :, 121] = f(attn_q_norm_g[j])
        p64[j, :, 122] = f(attn_k_norm_g[j])
        p64[j, :, 123:131] = f(attn_sink[j])[None, :]
    shared["abp64"] = p64
    p128 = np.zeros((NE, 128, 61), np.float32)
    for j in range(NE):
        p128[j, :, 0:48] = f(ab_conv_w[j]).reshape(4, 12, 128).transpose(2, 1, 0).reshape(128, 48)
        p128[j, :, 48:60] = f(ab_conv_b[j]).reshape(12, 128).T
        p128[j, :, 60] = np.tile(f(dn_norm_g[j]), 2)
    shared["abp128"] = p128
    cdm = np.zeros((128, 384), np.float32)
    r_ = (np.arange(128) % 64)[:, None]; x_ = np.arange(64)[None, :]
    for i in range(2):
        cdm[:, i * 64:(i + 1) * 64] = (r_ == x_)
        cdm[:, 128 + i * 64:128 + (i + 1) * 64] = np.where(r_ > x_, 0.0, -30000.0)
        cdm[:, 256 + i * 64:256 + (i + 1) * 64] = np.where(r_ < x_, 0.0, -30000.0)
    shared["cd"] = cdm
    csm = np.zeros((128, 520), np.float32)
    for d in range(2):
        for hh_ in range(2):
            for gp in range(2):
                for ii in range(2):
                    kk = d * 8 + hh_ * 4 + 2 * ii + gp
                    o = (d * 2 + hh_) * 128
                    csm[gp * 64 + kk, o + ii * 64:o + (ii + 1) * 64] = 1.0
                    csm[gp * 64 + kk, 512 + (d * 2 + hh_) * 2 + ii] = 1.0
    shared["csel"] = csm
    p16 = np.zeros((NE, 16, 2), np.float32)
    for j in range(NE):
        p16[j, :, 0] = f(dn_a_log[j]).reshape(16)
        p16[j, :, 1] = f(dn_dt_bias[j]).reshape(16)
    shared["abp16"] = p16
    shared["cin"] = np.stack([_wt(f(c_w_in[j]), 8, 512) for j in range(NO)])
    shared["cout"] = np.stack([_wt(f(c_w_out[j]), 8, 512) for j in range(NO)])
    cbd = np.zeros((NO, 4, 128, 8, 128), np.float32)
    for j in range(NO):
        for gi, wsrc in enumerate((lru_w_a, lru_w_x)):
            for d in range(2):
                w = f(wsrc[j][d])
                for cc in range(8):
                    for s_ in range(2):
                        cbd[j, gi * 2 + d, s_ * 64:(s_ + 1) * 64, cc, s_ * 64:(s_ + 1) * 64] = w[cc * 2 + s_]
    shared["cbd"] = cbd.reshape(NO, 4, 128, 8 * 128)
    cpp = np.zeros((NO, 128, 8, 11), np.float32)
    for j in range(NO):
        cpp[j, :, :, 0:4] = np.stack([_fm(f(c_conv_w[j][t])) for t in range(4)], axis=-1)
        cpp[j, :, :, 4] = _fm(f(c_conv_b[j]))
        for d in range(2):
            cpp[j, :, :, 5 + d] = _fm(f(lru_b_a[j][d]))
            cpp[j, :, :, 7 + d] = _fm(f(lru_b_x[j][d]))
            cpp[j, :, :, 9 + d] = _fm(f(lru_lambda[j][d]))
    shared["cp"] = cpp.reshape(NO, 128, 88)
    in_maps = []
    for i in range(8):
        b = i % 4
        m = dict(shared)
        xp = x_prompt[4 * i:4 * i + 4].reshape(1024, 1024)
        m["xT"] = np.ascontiguousarray(np.stack([xp.T, x_sample[b].T]))
        m["cond"] = np.stack([_fm(f(c_ctx)), _fm(f(c[b]))])
        sd = f(state_delta[b])[:NE]
        m["sdelta"] = np.ascontiguousarray(sd.reshape(NE, 2, 2, 2, 2, 64, 64).transpose(0, 1, 2, 4, 5, 3, 6)).reshape(NE, 2, 2, 128, 128)
        ck = f(cache_k[b])[:NE]
        m["kcT"] = np.ascontiguousarray(ck.transpose(0, 3, 2, 1))
        cv = f(cache_v[b])[:NE].reshape(NE, 2, 128, 128)
        m["vc"] = np.ascontiguousarray(cv.transpose(0, 2, 1, 3))
        m["slru"] = _fm(f(state_lru[b])[:NO])
        in_maps.append(m)
    if "nc" not in _CACHE:
        _CACHE["nc"] = build_program()
    res = run_bass_kernel_spmd(_CACHE["nc"], in_maps, core_ids=list(range(8)))
    R = res.results
    y_prompt = np.zeros((32, 256, 1024), np.float32)
    y_sample = np.zeros((4, 1024, 1024), np.float32)
    new_dn = np.zeros((32, NE, 2, 8, 64, 64), np.float32)
    new_k = np.zeros((32, NE, 256, 2, 64), np.float32)
    new_v = np.zeros((32, NE, 256, 2, 64), np.float32)
    new_lru = np.zeros((32, NO, 2, 1024), np.float32)
    for i in range(8):
        r = R[i]
        y_prompt[4 * i:4 * i + 4] = r["yT"][0].T.reshape(4, 256, 1024)
        if i < 4:
            y_sample[i] = r["yT"][1].T
        dn = r["o_dn"].reshape(4, NE, 2, 2, 2, 64, 2, 64)
        new_dn[4 * i:4 * i + 4] = dn.transpose(0, 1, 2, 3, 6, 4, 5, 7).reshape(4, NE, 2, 8, 64, 64)
        ok = r["o_k"].reshape(NE, 64, 2, 4, 256)
        new_k[4 * i:4 * i + 4] = ok.transpose(3, 0, 4, 2, 1)
        ov = r["o_v"].reshape(NE, 128, 4, 2, 2, 64)
        new_v[4 * i:4 * i + 4] = ov.transpose(2, 0, 3, 1, 4, 5).reshape(4, NE, 256, 2, 64)
        ol = r["o_lru"].reshape(NO, 2, 4, 128, 8)
        new_lru[4 * i:4 * i + 4] = ol.transpose(2, 0, 1, 4, 3).reshape(4, NO, 2, 1024)
    return (y_prompt, y_sample, new_dn, new_k, new_v, new_lru)
```

```python
import os
import math
import numpy as np
from contextlib import ExitStack
import concourse.bass as bass
import concourse.mybir as mybir
from concourse.bass_utils import run_bass_kernel_spmd

F32 = mybir.dt.float32
BF16 = mybir.dt.bfloat16
F32R = mybir.dt.float32r
USE_F32R = os.environ.get("K_F32R", "0") == "1"
AF = mybir.ActivationFunctionType
ALU = mybir.AluOpType

D = 1024
NT = 1024
TT = 512
DEPTH = int(os.environ.get("K_DEPTH", "4"))
DO_A = os.environ.get("K_A", "1") == "1"
DO_B = os.environ.get("K_B", "1") == "1"
DO_C = os.environ.get("K_C", "1") == "1"
DO_MLP = os.environ.get("K_MLP", "1") == "1"
GROUPS = [int(x) for x in os.environ.get("K_GROUPS", "01")]
EPS = 1e-6
DN_STAGE = int(os.environ.get("K_DN_STAGE", "9"))
DN_CUT = float(os.environ.get("K_DN_CUT", "9"))
NH = 4
WH = NH * 64
EPOCH = 2000


class Sched:
    ENG = ("pe", "act", "dve", "pool", "sp")

    def __init__(self, nc, es, n_dma_sems=4):
        self.nc = nc
        self.es = es
        self.ops = {e: [] for e in self.ENG}
        self.cnt = {e: 0 for e in self.ENG}
        self.sems = {}
        self.dma_sems = {}
        self.dma_cnt = {}
        self.dma_rr = {}
        self.n_dma_sems = n_dma_sems
        for q in ("sp", "pool"):
            for s in range(n_dma_sems):
                self.dma_sems[(q, s)] = es.enter_context(nc.semaphore("d_%s%d" % (q, s)))
                self.dma_cnt[(q, s)] = 0
            self.dma_rr[q] = 0
        self.waited = {e: {} for e in self.ENG}
        self.last_w = {}
        self.readers = {}

    def _deps(self, eng, reads, writes):
        need = {}
        def add(ev):
            sk, v = ev
            if need.get(sk, 0) < v:
                need[sk] = v
        for k in reads:
            if k in self.last_w:
                add(self.last_w[k])
        for k in writes:
            if k in self.last_w:
                add(self.last_w[k])
            for ev in self.readers.get(k, ()):
                add(ev)
        waits = []
        w = self.waited[eng]
        for sk, v in need.items():
            if w.get(sk, 0) < v:
                w[sk] = v
                waits.append((sk, v))
        return waits

    def _commit(self, ev, reads, writes):
        for k in reads:
            r = self.readers.setdefault(k, [])
            r.append(ev)
            if len(r) > 24:
                best = {}
                for sk, v in r:
                    if best.get(sk, 0) < v:
                        best[sk] = v
                self.readers[k] = list(best.items())
        for k in writes:
            self.last_w[k] = ev
            self.readers[k] = []

    def op(self, eng, fn, reads=(), writes=()):
        psr = [k for k in reads if isinstance(k, str) and k.startswith("ps")]
        if psr:
            writes = list(writes) + [k for k in psr if k not in writes]
        waits = self._deps(eng, reads, writes)
        self.cnt[eng] += 1
        ev = (eng, self.cnt[eng])
        self.ops[eng].append((waits, fn, ("c", eng, self.cnt[eng])))
        self._commit(ev, reads, writes)

    def dma(self, q, out, in_, reads=(), writes=()):
        s = self.dma_rr[q]
        self.dma_rr[q] = (s + 1) % self.n_dma_sems
        sk = ("dma", q, s)
        waits = self._deps(q, reads, writes)
        prev = self.dma_cnt[(q, s)]
        if prev > 0 and self.waited[q].get(sk, 0) < prev:
            self.waited[q][sk] = prev
            waits.append((sk, prev))
        self.dma_cnt[(q, s)] = prev + 16
        ev = (sk, prev + 16)
        self.ops[q].append((waits, (lambda e: e.dma_start(out=out, in_=in_)), ("d", (q, s))))
        self._commit(ev, reads, writes)

    def barrier(self):
        for e in self.ENG:
            waits = []
            lazy = (e == "pool")
            for o in self.ENG:
                if o != "sp" and self.cnt[o] > 0 and self.waited[e].get(o, 0) < self.cnt[o]:
                    if not lazy:
                        self.waited[e][o] = self.cnt[o]
                    waits.append((o, self.cnt[o]))
            for (q, s), c in self.dma_cnt.items():
                sk = ("dma", q, s)
                if c > 0 and self.waited[e].get(sk, 0) < c:
                    if not lazy:
                        self.waited[e][sk] = c
                    waits.append((sk, c))
            if waits:
                self.ops[e].append((waits, None, ("bar",) if lazy else None))

    def _get_sem(self, eng, ep):
        k = (eng, ep)
        if k not in self.sems:
            self.sems[k] = self.es.enter_context(self.nc.semaphore("s_%s%d" % (eng, ep)))
        return self.sems[k]

    def _wait(self, e, sk, v):
        if isinstance(sk, tuple):
            e.wait_ge(self.dma_sems[(sk[1], sk[2])], v)
        else:
            ep, r = divmod(v - 1, EPOCH)
            e.wait_ge(self._get_sem(sk, ep), r + 1)

    def emit(self, block):
        for eng in self.ENG:
            for ep in range((self.cnt[eng] + EPOCH - 1) // EPOCH + 1):
                if eng != "sp":
                    self._get_sem(eng, ep)
        pl = self.ops["pool"]
        keep = []
        for i, (waits, fn, kind) in enumerate(pl):
            if kind == ("bar",):
                has_c = False
                for (w2, f2, k2) in pl[i + 1:]:
                    if k2 == ("bar",):
                        break
                    if k2 is not None and k2[0] == "c":
                        has_c = True
                        break
                if has_c:
                    keep.append((waits, None, None))
            else:
                keep.append((waits, fn, kind))
        self.ops["pool"] = keep
        def mk(ename):
            def body(e):
                for waits, fn, kind in self.ops[ename]:
                    for sk, v in waits:
                        self._wait(e, sk, v)
                    if fn is None:
                        continue
                    ins = fn(e)
                    if kind[0] == "c":
                        ep = (kind[2] - 1) // EPOCH
                        ins.then_inc(self._get_sem(kind[1], ep), 1)
                    else:
                        ins.then_inc(self.dma_sems[kind[1]], 16)
            return body
        block.tensor(mk("pe"))
        block.scalar(mk("act"))
        block.vector(mk("dve"))
        block.gpsimd(mk("pool"))
        block.sync(mk("sp"))


def build_program():
    nc = bass.Bass("TRN2", target_bir_lowering=False)
    NE = (DEPTH + 1) // 2
    NO = max(1, DEPTH // 2)
    NL = DEPTH
    def din(name, shape):
        return nc.dram_tensor(name, list(shape), F32, kind="ExternalInput").ap()
    def dout(name, shape):
        return nc.dram_tensor(name, list(shape), F32, kind="ExternalOutput").ap()
    xT = din("xT", [2, D, NT])
    cond = din("cond", [2, 128, 8])
    sdelta = din("sdelta", [NE, 2, 2, 128, 128])
    kcT = din("kcT", [NE, 64, 2, 256])
    vc = din("vc", [NE, 128, 2, 128])
    slru = din("slru", [NO, 2, 128, 8])
    ada_w = din("ada_w", [NL, 12, 128, 8 * 512])
    ada_b = din("ada_b", [NL, 128, 48])
    ng = din("ng", [NL, 128, 16])
    w1 = din("w1", [NL, 8, 128, 8 * 512])
    w2 = din("w2", [NL, 8, 128, 32 * 128])
    abin = din("abin", [NE, 6, 128, 8 * 512])
    about = din("about", [NE, 2, 128, 8 * 512])
    abp64 = din("abp64", [NE, 64, 24 * 5 + 3 + 8])
    abp16 = din("abp16", [NE, 16, 2])
    abp128 = din("abp128", [NE, 128, 61])
    cd = din("cd", [128, 384])
    csel = din("csel", [128, 520])
    cin = din("cin", [NO, 4, 128, 8 * 512])
    cout = din("cout", [NO, 2, 128, 8 * 512])
    cbd = din("cbd", [NO, 4, 128, 8 * 128])
    cp = din("cp", [NO, 128, 8 * 11])
    c64 = din("c64", [64, 832])
    cossin = din("cossin", [64, 2048])
    c16 = din("c16", [16, 1040])
    cmask = din("cmask", [16, 2048])
    c128 = din("c128", [128, 256])
    yT = dout("yT", [2, D, NT])
    o_dn = dout("o_dn", [4, NE, 2, 2, 128, 128])
    o_k = dout("o_k", [NE, 64, 2, NT])
    o_v = dout("o_v", [NE, 128, 8, 128])
    o_lru = dout("o_lru", [NO, 2, 4, 128, 8])

    with ExitStack() as es:
        S = Sched(nc, es)
        uid = [0]
        def sb(shape, dt=F32, ctx=None):
            uid[0] += 1
            return (ctx or es).enter_context(nc.sbuf_tensor("t%d" % uid[0], list(shape), dt))
        def ACT(out, in_, func, reads, writes, **kw):
            S.op("act", lambda e: e.activation(out=out, in_=in_, func=func, **kw), reads, writes)
        def TT_(eng, out, in0, in1, op, reads, writes):
            S.op(eng, lambda e: e.tensor_tensor(out=out, in0=in0, in1=in1, op=op), reads, writes)
        def TS(eng, out, in0, s1, s2, op0, op1, reads, writes):
            S.op(eng, lambda e: e.tensor_scalar(out=out, in0=in0, scalar1=s1, scalar2=s2, op0=op0, op1=op1), reads, writes)
        def STT(out, in0, scalar, in1, op0, op1, reads, writes):
            S.op("dve", lambda e: e.scalar_tensor_tensor(out=out, in0=in0, scalar=scalar, in1=in1, op0=op0, op1=op1), reads, writes)
        def CP(eng, out, in_, reads, writes):
            if eng == "act":
                ACT(out, in_, AF.Copy, reads, writes)
            else:
                S.op(eng, lambda e: e.tensor_copy(out=out, in_=in_), reads, writes)
        def RECIP(out, in_, reads, writes):
            S.op("dve", lambda e: e.reciprocal(out=out, in_=in_), reads, writes)
        def MM(out, pairs, reads, writes):
            pairs = list(pairs)
            def fn(e):
                ins = None
                n = len(pairs)
                for i, (l, r) in enumerate(pairs):
                    ins = e.matmul(out, lhsT=l, rhs=r, start=(i == 0), stop=(i == n - 1))
                return ins
            S.op("pe", fn, reads, writes)
        def MMS(items, reads, writes):
            items = list(items)
            def fn(e):
                ins = None
                for (o, l, r) in items:
                    ins = e.matmul(o, lhsT=l, rhs=r, start=True, stop=True)
                return ins
            S.op("pe", fn, reads, writes)
        def MEMSET(eng, out, val, writes):
            S.op(eng, lambda e: e.memset(out, val), (), writes)

        PSB = [es.enter_context(nc.psum_tensor("ps%d" % i, [128, 512], F32)) for i in range(8)]
        prr = [0]
        def newps():
            i = prr[0]
            prr[0] = (i + 1) % 8
            return PSB[i], "ps%d" % i

        Y = sb([128, 8, NT])
        HT = sb([128, 8, NT], BF16)
        OBt = sb([128, 8, NT], BF16)
        OB128 = OBt[:]
        def OBH(h, sl):
            return OBt[(h % 2) * 64:(h % 2) * 64 + 64, h // 2, sl]
        NSLOT = 3
        WR = [sb([128, 4096], BF16) for _ in range(NSLOT)]
        wrr = [0]
        C64 = sb([64, 832])
        C16 = sb([16, 1040])
        C128 = sb([128, 256])
        MODALL = sb([128, 4, 2, 48])
        SC = sb([128, 2, 8], BF16)
        CONDT = sb([128, 2, 8])
        ONESB = sb([128, 128], BF16)
        IDB = sb([64, 64], BF16)
        NG = sb([128, 4, 16])
        ADAB = sb([128, 4, 48])
        LP = sb([128, 64])
        IDENT4 = C64[:, 0:256]
        NEG_LO = C64[:, 256:512]
        NEG_UP = C64[:, 512:768]
        PERMT = C64[:, 768:832]
        def BLK(d, hh):
            o = (d * 2 + hh) * 256
            return C16[:, o:o + 256]
        def SEL(d, hh):
            o = 1024 + (d * 2 + hh) * 4
            return C16[:, o:o + 4]
        MASK_LO = C128[:, 0:128]
        MASK_UP = C128[:, 128:256]

        S.dma("sp", C64[:], c64[:, :], writes=["c64"])
        S.dma("sp", C16[:], c16[:, :], writes=["c16"])
        S.dma("sp", C128[:], c128[:, :], writes=["c128"])
        for g in range(2):
            S.dma("sp", CONDT[:, g, :], cond[g], writes=["condt"])
        for l in range(NL):
            S.dma("sp", NG[:, l, :], ng[l], writes=["ng"])
            S.dma("sp", ADAB[:, l, :], ada_b[l], writes=["adab"])
        ACT(SC[:], CONDT[:], AF.Silu, ["condt"], ["sc"])
        MEMSET("dve", ONESB[:], 1.0, ["onesb"])
        CP("dve", IDB[:], C64[:, 0:64], ["c64"], ["idb"])
        ONES16 = sb([128, 64])
        NEGONES16 = sb([128, 64])
        CD = sb([128, 384])
        CS = sb([128, 520])
        S.dma("sp", CD[:], cd[:, :], writes=["cd"])
        S.dma("sp", CS[:], csel[:, :], writes=["cs"])
        IDENT2 = CD[:, 0:128]
        NEG_LO2 = CD[:, 128:256]
        NEG_UP2 = CD[:, 256:384]
        def BLK2(d, hh):
            o = (d * 2 + hh) * 128
            return CS[:, o:o + 128]
        def SEL2(d, hh):
            o = 512 + (d * 2 + hh) * 2
            return CS[:, o:o + 2]
        IDB2 = sb([128, 64], BF16)
        CP("dve", IDB2[:], CD[:, 0:64], ["cd"], ["idb2"])
        MEMSET("dve", ONES16[:], 1.0, ["ones16"])
        MEMSET("dve", NEGONES16[:], -1.0, ["ones16"])
        CK = ["c64", "c16", "c128", "onesb", "idb", "ones16"]

        def wload(src_ap, parts=128):
            i = wrr[0]
            wrr[0] = (i + 1) % NSLOT
            S.dma("pool", WR[i][0:parts, :], src_ap, writes=["w%d" % i])
            return WR[i], "w%d" % i

        def adaln(l):
            ps, pk = newps()
            for mb in range(12):
                W, wk = wload(ada_w[l, mb])
                Wv = W[:].rearrange("p (k m) -> p k m", k=8)
                items = []
                for mi in range(4):
                    m = mb * 4 + mi
                    for kc in range(8):
                        pass
                def fn(e, Wv=Wv, mb=mb):
                    ins = None
                    for mi in range(4):
                        m = mb * 4 + mi
                        for kc in range(8):
                            ins = e.matmul(ps[:, 2 * m:2 * m + 2], lhsT=Wv[:, kc, mi * 128:(mi + 1) * 128], rhs=SC[:, :, kc],
                                           start=(kc == 0), stop=(kc == 7))
                    return ins
                S.op("pe", fn, [wk, "sc"], [pk])
            for g in range(2):
                TT_("dve", MODALL[:, l, g, :], ps[:, 0:96].rearrange("p (m g) -> p g m", g=2)[:, g, :], ADAB[:, l, :], ALU.add,
                    [pk, "adab"], ["mod%d" % l])

        def mod(l, g, i):
            return MODALL[:, l, g, i * 8:(i + 1) * 8]

        def modulate(l, g, which, ctx):
            GS = LP[:, which * 8:(which + 1) * 8]
            STT(GS, mod(l, g, 1 + 3 * which), 1.0, NG[:, l, which * 8:(which + 1) * 8], ALU.add, ALU.mult,
                ["mod%d" % l, "ng"], ["lp%d" % which])
            SQ = [sb([128, 8, TT], BF16, ctx) for _ in range(2)]
            RS = [sb([128, TT], F32, ctx) for _ in range(2)]
            RSTD = [sb([128, TT], F32, ctx) for _ in range(2)]
            TMP = [[sb([128, TT], F32, ctx) for _ in range(2)] for _ in range(2)]
            sls = [slice(tt * TT, (tt + 1) * TT) for tt in range(2)]
            pss = []
            for tt in range(2):
                for hf in range(2):
                    ACT(SQ[tt][:, hf * 4:(hf + 1) * 4, :], Y[:, hf * 4:(hf + 1) * 4, sls[tt]], AF.Square, ["y"], ["msq%d%d" % (tt, hf)])
            for tt in range(2):
                ps, pk = newps()
                pss.append((ps, pk))
                MM(ps[:], [(ONESB[:], SQ[tt][:, c, :]) for c in range(8)], ["msq%d0" % tt, "msq%d1" % tt, "onesb"], [pk])
            for tt in range(2):
                ACT(RS[tt][:], pss[tt][0][:], AF.Ln, [pss[tt][1]], ["mrs%d" % tt], scale=1.0 / D, bias=EPS)
            for tt in range(2):
                ACT(RSTD[tt][:], RS[tt][:], AF.Exp, ["mrs%d" % tt], ["mrstd%d" % tt], scale=-0.5)
            for c in range(8):
                for tt in range(2):
                    T_ = TMP[tt][c % 2]
                    tk = "mtmp%d%d" % (tt, c % 2)
                    STT(T_[:], Y[:, c, sls[tt]], GS[:, c:c + 1], RSTD[tt][:], ALU.mult, ALU.mult, ["y", "lp%d" % which, "mrstd%d" % tt], [tk])
                    ACT(HT[:, c, sls[tt]], T_[:], AF.Identity, [tk, "mod%d" % l], ["ht"],
                        bias=mod(l, g, 3 * which)[:, c:c + 1], scale=1.0)

        def mlp(l, g, ctx):
            H1 = sb([128, 32, NT], BF16, ctx)
            SQ2 = [sb([128, TT], F32, ctx) for _ in range(2)]
            n = 0
            for mb in range(8):
                W, wk = wload(w1[l, mb])
                Wv = W[:].rearrange("p (k m) -> p k m", k=8)
                for mi in range(4):
                    m = mb * 4 + mi
                    for tt in range(2):
                        sl = slice(tt * TT, (tt + 1) * TT)
                        ps, pk = newps()
                        MM(ps[:], [(Wv[:, kc, mi * 128:(mi + 1) * 128], HT[:, kc, sl]) for kc in range(8)], [wk, "ht"], [pk])
                        q = SQ2[n % 2]; qk = "sq2%d" % (n % 2); n += 1
                        ACT(q[:], ps[:], AF.Square, [pk], [qk])
                        STT(H1[:, m, sl], ps[:], 0.0, q[:], ALU.is_gt, ALU.mult, [pk, qk], [("h1", m)])
            for m in range(8):
                W, wk = wload(w2[l, m])
                Wv = W[:].rearrange("p (k m) -> p k m", k=32)
                for tt in range(2):
                    sl = slice(tt * TT, (tt + 1) * TT)
                    ps, pk = newps()
                    MM(ps[:], [(Wv[:, kc, :], H1[:, kc, sl]) for kc in range(32)], [wk] + [("h1", kc) for kc in range(32)], [pk])
                    STT(Y[:, m, sl], ps[:], mod(l, g, 5)[:, m:m + 1], Y[:, m, sl], ALU.mult, ALU.add, [pk, "y", "mod%d" % l], ["y"])

        def mixer_c(l, g, ctx):
            j = l // 2
            seqs = [(s * 256, 256) for s in range(4)] if g == 0 else [(0, 1024)]
            CPt = sb([128, 8, 11], F32, ctx)
            S.dma("sp", CPt[:], cp[j].rearrange("p (c k) -> p c k", k=11), writes=["cp"])
            CL = sb([128, 2, 8], F32, ctx)
            T1 = sb([128, 2, 8], F32, ctx)
            ACT(T1[:], CPt[:, :, 9:11].rearrange("p c d -> p d c"), AF.Exp, ["cp"], ["ct1"], scale=-1.0)
            ACT(T1[:], T1[:], AF.Ln, ["ct1"], ["ct1"], bias=1.0, scale=1.0)
            TS("dve", CL[:], T1[:], -8.0, None, ALU.mult, ALU.bypass, ["ct1"], ["cl"])
            BD = sb([128, 4, 8, 128], F32, ctx)
            for i in range(4):
                S.dma("sp", BD[:, i], cbd[j, i].rearrange("p (c m) -> p c m", c=8), writes=["bd"])
            H0 = sb([128, 2, 8], F32, ctx)
            if g == 1:
                for d in range(2):
                    S.dma("sp", H0[:, d, :], slru[j, d], writes=["h0"])
            FIN = sb([128, 2, 4, 8], F32, ctx)
            NSL = 2
            CSL = []
            for s_ in range(NSL):
                CSL.append(dict(xr=sb([128, NT], F32, ctx), xc=sb([128, NT], F32, ctx), gt=sb([128, NT], F32, ctx), ga=sb([128, NT], F32, ctx),
                                aa=[sb([128, NT], F32, ctx) for _ in range(2)], ta=[sb([128, NT], F32, ctx) for _ in range(2)],
                                tb=[sb([128, NT], F32, ctx) for _ in range(2)]))
            ns = len(seqs); L = seqs[0][1]
            def chunk_gen(s_, c, ci, Wxv, Wgv, wxk, wgk):
                B_ = CSL[s_]
                xr, xc, gt, ga = B_["xr"], B_["xc"], B_["gt"], B_["ga"]
                K_ = lambda n_: "%s_%d" % (n_, s_)
                kx, kc_, kg, kga = K_("xr"), K_("xc"), K_("gt"), K_("ga")
                banks = [(PSB[4 * s_ + q], "ps%d" % (4 * s_ + q)) for q in range(4)]
                for tt in range(2):
                    sl = slice(tt * TT, (tt + 1) * TT)
                    ps, pk = banks[tt]
                    MM(ps[:], [(Wxv[:, kc, ci * 128:(ci + 1) * 128], HT[:, kc, sl]) for kc in range(8)], [wxk, "ht"], [pk])
                    ps, pk = banks[2 + tt]
                    MM(ps[:], [(Wgv[:, kc, ci * 128:(ci + 1) * 128], HT[:, kc, sl]) for kc in range(8)], [wgk, "ht"], [pk])
                yield
                for tt in range(2):
                    sl = slice(tt * TT, (tt + 1) * TT)
                    CP("act", xr[:, sl], banks[tt][0][:], [banks[tt][1]], [kx])
                    CP("dve", gt[:, sl], banks[2 + tt][0][:], [banks[2 + tt][1]], [kg])
                yield
                TS("dve", xc[:], xr[:], CPt[:, c, 2:3], CPt[:, c, 4:5], ALU.mult, ALU.add, [kx, "cp"], [kc_])
                ACT(ga[:], gt[:], AF.Square, [kg], [kga])
                yield
                xr3 = xr[:].rearrange("p (s t) -> p s t", s=ns)
                xc3 = xc[:].rearrange("p (s t) -> p s t", s=ns)
                for tap, o in ((0, -2), (1, -1), (3, 1)):
                    d0, d1 = max(0, -o), L - max(0, o)
                    STT(xc3[:, :, d0:d1], xr3[:, :, d0 + o:d1 + o], CPt[:, c, tap:tap + 1], xc3[:, :, d0:d1], ALU.mult, ALU.add,
                        [kx, kc_, "cp"], [kc_])
                    yield
                TS("dve", ga[:], ga[:], 0.044715, 1.0, ALU.mult, ALU.add, [kga], [kga])
                yield
                TT_("dve", ga[:], ga[:], gt[:], ALU.mult, [kga, kg], [kga])
                yield
                ACT(ga[:], ga[:], AF.Sigmoid, [kga], [kga], scale=1.5957691216)
                yield
                TT_("dve", ga[:], ga[:], gt[:], ALU.mult, [kga, kg], [kga])
                yield
                for d in range(2):
                    a_, ta, tb = B_["aa"][d], B_["ta"][d], B_["tb"][d]
                    ka, kta, ktb = K_("aa%d" % d), K_("ta%d" % d), K_("tb%d" % d)
                    for tt in range(2):
                        sl = slice(tt * TT, (tt + 1) * TT)
                        ps, pk = banks[tt]
                        MM(ps[:], [(BD[:, d, c, :], xc[:, sl])], ["bd", kc_], [pk])
                        ps, pk = banks[2 + tt]
                        MM(ps[:], [(BD[:, 2 + d, c, :], xc[:, sl])], ["bd", kc_], [pk])
                    yield
                    for tt in range(2):
                        sl = slice(tt * TT, (tt + 1) * TT)
                        ACT(ta[:, sl], banks[tt][0][:], AF.Sigmoid, [banks[tt][1], "cp"], [kta], bias=CPt[:, c, 5 + d:6 + d], scale=1.0)
                        ACT(tb[:, sl], banks[2 + tt][0][:], AF.Sigmoid, [banks[2 + tt][1], "cp"], [ktb], bias=CPt[:, c, 7 + d:8 + d], scale=1.0)
                    yield
                    ACT(a_[:], ta[:], AF.Exp, [kta, "cl"], [ka], scale=CL[:, d, c:c + 1])
                    TT_("dve", tb[:], tb[:], xc[:], ALU.mult, [ktb, kc_], [ktb])
                    yield
                    ACT(ta[:], a_[:], AF.Square, [ka], [kta])
                    yield
                    ACT(ta[:], ta[:], AF.Sqrt, [kta], [kta], scale=-1.0, bias=1.0)
                    yield
                    TT_("dve", tb[:], ta[:], tb[:], ALU.mult, [kta, ktb], [ktb])
                    yield
                    for si, (t0, L_) in enumerate(seqs):
                        init = H0[:, d, c:c + 1] if g == 1 else 0.0
                        if d == 0:
                            o_, a2, u2 = tb[:, t0:t0 + L_], a_[:, t0:t0 + L_], tb[:, t0:t0 + L_]
                        else:
                            o_, a2, u2 = tb[:, t0:t0 + L_][:, ::-1], a_[:, t0:t0 + L_][:, ::-1], tb[:, t0:t0 + L_][:, ::-1]
                        S.op("dve", (lambda e, o_=o_, a2=a2, u2=u2, init=init: e.tensor_tensor_scan(
                            out=o_, data0=a2, data1=u2, initial=init, op0=ALU.mult, op1=ALU.add)), [ka, ktb, "h0"], [ktb])
                        if g == 0:
                            col = t0 + L_ - 1 if d == 0 else t0
                            CP("act", FIN[:, d, si, c:c + 1], tb[:, col:col + 1], [ktb], ["fin"])
                        yield
                tb0, tb1 = B_["tb"][0], B_["tb"][1]
                TT_("dve", tb0[:], tb0[:], tb1[:], ALU.add, [K_("tb0"), K_("tb1")], [K_("tb0")])
                yield
                TT_("dve", OB128[:, c, :], tb0[:], ga[:], ALU.mult, [K_("tb0"), kga], ["ob"])
                yield
            def run_rr2(gens):
                gens = list(gens)
                while gens:
                    nxt = []
                    for g_ in gens:
                        try:
                            next(g_)
                            nxt.append(g_)
                        except StopIteration:
                            pass
                    gens = nxt
            for half in range(2):
                Wx, wxk = wload(cin[j, half])
                Wg, wgk = wload(cin[j, 2 + half])
                Wxv = Wx[:].rearrange("p (k m) -> p k m", k=8)
                Wgv = Wg[:].rearrange("p (k m) -> p k m", k=8)
                for c0_ in range(0, 4, NSL):
                    run_rr2([chunk_gen(s_, half * 4 + c0_ + s_, c0_ + s_, Wxv, Wgv, wxk, wgk) for s_ in range(NSL)])
            if g == 0:
                for d in range(2):
                    for si in range(4):
                        S.dma("sp", o_lru[j, d, si], FIN[:, d, si, :], reads=["fin"], writes=["o_lru"])
            for mb in range(2):
                W, wk = wload(cout[j, mb])
                Wv = W[:].rearrange("p (k m) -> p k m", k=8)
                for mi in range(4):
                    m = mb * 4 + mi
                    for tt in range(2):
                        sl = slice(tt * TT, (tt + 1) * TT)
                        ps, pk = newps()
                        MM(ps[:], [(Wv[:, kc, mi * 128:(mi + 1) * 128], OB128[:, kc, sl]) for kc in range(8)], [wk, "ob"], [pk])
                        STT(Y[:, m, sl], ps[:], mod(l, g, 2)[:, m:m + 1], Y[:, m, sl], ALU.mult, ALU.add, [pk, "y", "mod%d" % l], ["y"])

        DNP = {}
        ONESBD = sb([128, 128], BF16)
        MEMSET("dve", ONESBD[:], 0.0, ["onesbd"])
        MEMSET("dve", ONESBD[0:64, 0:64], 1.0, ["onesbd"])
        MEMSET("dve", ONESBD[64:128, 64:128], 1.0, ["onesbd"])
        def deltanet(l, g, ctx0):
            j = l // 2
            seqs = [(s * 256, 4) for s in range(4)] if g == 0 else [(0, 16)]
            P16 = sb([16, 2], F32, ctx0)
            S.dma("sp", P16[:], abp16[j], writes=["p16"])
            P128 = sb([128, 61], F32, ctx0)
            S.dma("sp", P128[:], abp128[j], writes=["p128"])
            DNP["CW2"] = P128[:, 0:48].rearrange("p (h k) -> p h k", k=4)
            DNP["CB2"] = P128[:, 48:60]
            DNP["DNG2"] = P128[:, 60:61]
            BETA = sb([128, NT], F32, ctx0)
            GCF = sb([128, NT], F32, ctx0)
            GCB = sb([128, NT], F32, ctx0)
            NEA = sb([16, 1], F32, ctx0)
            cs_ = ctx0
            G = sb([16, NT], F32, cs_)
            CM = sb([16, 2048], F32, cs_)
            S.dma("sp", CM[:], cmask[:, :], writes=["cm"])
            CMF = CM[:, 0:1024]
            CMB = CM[:, 1024:2048]
            MEMSET("dve", BETA[:], 0.0, ["beta"])
            MEMSET("dve", GCF[:], 0.0, ["gcf"])
            MEMSET("dve", GCB[:], 0.0, ["gcb"])
            ACT(NEA[:], P16[:, 0:1], AF.Exp, ["p16"], ["nea"])
            TS("dve", NEA[:], NEA[:], -1.0, None, ALU.mult, ALU.bypass, ["nea"], ["nea"])
            W5, w5k = wload(abin[j, 5])
            W5v = W5[:].rearrange("p (k m) -> p k m", k=8)
            for tt in range(2):
                sl = slice(tt * TT, (tt + 1) * TT)
                ps, pk = newps()
                MM(ps[0:16, :], [(W5v[:, kc, 0:16], HT[:, kc, sl]) for kc in range(8)], [w5k, "ht"], [pk])
                ACT(G[:, sl], ps[0:16, :], AF.Exp, [pk, "p16"], ["g"], bias=P16[:, 1:2], scale=1.0)
                ACT(G[:, sl], G[:, sl], AF.Ln, ["g"], ["g"], bias=1.0, scale=1.0)
                TS("dve", G[:, sl], G[:, sl], NEA[:, 0:1], None, ALU.mult, ALU.bypass, ["g", "nea"], ["g"])
                ps, pk = newps()
                MM(ps[0:16, :], [(W5v[:, kc, 16:32], HT[:, kc, sl]) for kc in range(8)], [w5k, "ht"], [pk])
                ACT(BETA[0:16, sl], ps[0:16, :], AF.Sigmoid, [pk, "beta"], ["beta"])
            S.op("dve", lambda e: e.tensor_tensor_scan(out=GCF[0:16, :], data0=CMF, data1=G[:], initial=0.0, op0=ALU.mult, op1=ALU.add),
                 ["g", "cm", "gcf"], ["gcf"])
            S.op("dve", lambda e: e.tensor_tensor_scan(out=GCB[0:16, :][:, ::-1], data0=CMB[:, ::-1], data1=G[:, ::-1], initial=0.0,
                                                       op0=ALU.mult, op1=ALU.add), ["g", "cm", "gcb"], ["gcb"])
            CP("act", BETA[64:80, :], BETA[0:16, :], ["beta"], ["beta"])
            CP("act", GCF[64:80, :], GCF[0:16, :], ["gcf"], ["gcf"])
            CP("act", GCB[64:80, :], GCB[0:16, :], ["gcb"], ["gcb"])
            if DN_STAGE < 2:
                return
            for hh in range(2):
                with ExitStack() as ctx:
                    deltanet_half(l, g, j, hh, seqs, ctx, BETA, GCF, GCB)
                S.barrier()

        def deltanet_half(l, g, j, hh, seqs, ctx, BETA, GCF, GCB):
            CW2, CB2, DNG2 = DNP["CW2"], DNP["CB2"], DNP["DNG2"]
            HW_ = 128
            QT = sb([128, 2, NT], BF16, ctx)
            KT = sb([128, 2, NT], BF16, ctx)
            VT = sb([128, 2, NT], BF16, ctx)
            GATE = sb([128, 2, NT], BF16, ctx)
            OACC = sb([128, 2, NT], F32, ctx)
            SQ = sb([128, TT], BF16, ctx)
            RS = sb([128, TT], F32, ctx)
            c2 = ctx
            SQ2 = [SQ, sb([128, TT], BF16, c2)]
            RS2 = [RS, sb([128, TT], F32, c2)]
            RAW = [sb([128, NT], F32, c2) for _ in range(2)]
            CV = [sb([128, NT], F32, c2) for _ in range(2)]
            nseq = 4 if g == 0 else 1
            L = NT // nseq
            n = 0
            for blk in range(4):
                W, wk = wload(abin[j, blk])
                Wv = W[:].rearrange("p (k m) -> p k m", k=8)
                for pr in range(2):
                    h0 = hh * NH + 2 * pr
                    b = n % 2; n += 1
                    raw, cv = RAW[b], CV[b]
                    kr, kv_ = "raw%d" % b, "cv%d" % b
                    for tt in range(2):
                        sl = slice(tt * TT, (tt + 1) * TT)
                        ps, pk = newps()
                        MM(ps[:], [(Wv[:, kc, h0 * 64:(h0 + 2) * 64], HT[:, kc, sl]) for kc in range(8)], [wk, "ht"], [pk])
                        if blk == 3:
                            ACT(GATE[:, pr, sl], ps[:], AF.Silu, [pk], ["gate"])
                        else:
                            CP("act", raw[:, sl], ps[:], [pk], [kr])
                    if blk == 3:
                        continue
                    pbi = blk * 4 + hh * 2 + pr
                    TS("dve", cv[:], raw[:], CW2[:, pbi, 2:3], CB2[:, pbi:pbi + 1], ALU.mult, ALU.add, [kr, "p128"], [kv_])
                    r3 = raw[:].rearrange("p (s t) -> p s t", s=nseq)
                    c3 = cv[:].rearrange("p (s t) -> p s t", s=nseq)
                    for tap, o in ((0, -2), (1, -1), (3, 1)):
                        d0, d1 = max(0, -o), L - max(0, o)
                        STT(c3[:, :, d0:d1], r3[:, :, d0 + o:d1 + o], CW2[:, pbi, tap:tap + 1], c3[:, :, d0:d1], ALU.mult, ALU.add,
                            [kr, kv_, "p128"], [kv_])
                    if blk == 2:
                        ACT(VT[:, pr, :], cv[:], AF.Silu, [kv_], ["vt"])
                        continue
                    ACT(cv[:], cv[:], AF.Silu, [kv_], [kv_])
                    dst = QT if blk == 0 else KT
                    dk_ = "qt" if blk == 0 else "kt"
                    sls = [slice(tt * TT, (tt + 1) * TT) for tt in range(2)]
                    pss = []
                    for tt in range(2):
                        ACT(SQ2[tt][:], cv[:, sls[tt]], AF.Square, [kv_], ["dsq%d" % tt])
                    for tt in range(2):
                        ps, pk = newps()
                        pss.append((ps, pk))
                        MM(ps[:], [(ONESBD[:], SQ2[tt][:])], ["dsq%d" % tt, "onesbd"], [pk])
                    for tt in range(2):
                        ACT(RS2[tt][:], pss[tt][0][:], AF.Ln, [pss[tt][1]], ["drs%d" % tt], bias=EPS, scale=1.0)
                    for tt in range(2):
                        ACT(RS2[tt][:], RS2[tt][:], AF.Exp, ["drs%d" % tt], ["drs%d" % tt], scale=-0.5)
                    for tt in range(2):
                        STT(dst[:, pr, sls[tt]], cv[:, sls[tt]], (0.125 if blk == 0 else 1.0), RS2[tt][:], ALU.mult, ALU.mult,
                            [kv_, "drs%d" % tt], [dk_])
            if DN_STAGE < 3:
                return
            KS = 4
            def mk(shape, dt=F32):
                return sb(shape, dt, ctx)
            SL = []
            for k in range(KS):
                SL.append(dict(A=mk([128, HW_]), Bm=mk([128, HW_]), EGT=mk([128, HW_]),
                               KBc=mk([128, HW_], BF16), QGc=mk([128, HW_], BF16), INTR=mk([128, HW_], BF16), KDEC=mk([128, HW_], BF16),
                               VTOK=mk([128, HW_], BF16), KTOK=mk([128, HW_], BF16), QX=mk([128, 2 * HW_]), QT=mk([128, HW_]),
                               TKS=mk([128, 10]), SELGL=mk([128, 2]), RT=mk([128, HW_]), VNB=mk([128, HW_], BF16)))
            for k in range(KS):
                MEMSET("dve", SL[k]["SELGL"][:], 0.0, ["selgl_%d" % k])
            SS = {}
            for si in range(min(2, len(seqs))):
                for d in range(2):
                    SS[(si, d)] = (sb([128, HW_], F32, ctx), sb([128, HW_], BF16, ctx))
            def init_state(si):
                for d in range(2):
                    Sf, Sb_ = SS[(si % 2, d)]
                    sk = ("S", si % 2, d)
                    if g == 1:
                        S.dma("sp", Sf[:], sdelta[j, d, hh], writes=[sk])
                    else:
                        MEMSET("dve", Sf[:], 0.0, [sk])
                    CP("act", Sb_[:], Sf[:], [sk], [("Sb", si % 2, d)])
            oacc_written = set()
            def h2(ap):
                return ap.rearrange("p (i x) -> p i x", i=2)
            def bc(ap):
                return ap.unsqueeze(2).to_broadcast([128, 2, 64])
            GP = ((0, slice(0, 64), slice(0, 16)), (1, slice(64, 128), slice(64, 80)))
            def geom(si, d, ck):
                t0, nch = seqs[si]
                chunk = ck if d == 0 else nch - 1 - ck
                c0 = t0 + chunk * 64
                return chunk, c0, slice(c0, c0 + 64)
            def partA(k, si, d, ck):
                sl_ = SL[k]
                X, Xk = PSB[2 * k], "ps%d" % (2 * k)
                Yb, Yk = PSB[2 * k + 1], "ps%d" % (2 * k + 1)
                chunk, c0, cs = geom(si, d, ck)
                GC = GCF if d == 0 else GCB
                gck = "gcf" if d == 0 else "gcb"
                last = 63 if d == 0 else 0
                NEGM = NEG_LO2 if d == 0 else NEG_UP2
                NEGMT = NEG_UP2 if d == 0 else NEG_LO2
                K_ = lambda n_: "%s_%d" % (n_, k)
                A, Bm, EGT = sl_["A"], sl_["Bm"], sl_["EGT"]
                KBc, QGc, INTR, KDEC, VTOK, KTOK = sl_["KBc"], sl_["QGc"], sl_["INTR"], sl_["KDEC"], sl_["VTOK"], sl_["KTOK"]
                QX, QT_, tks, SELGL = sl_["QX"], sl_["QT"], sl_["TKS"], sl_["SELGL"]
                blk = BLK2(d, hh)
                sel = SEL2(d, hh)
                TT_("pool", h2(A[0:80, :]), GC[0:80, cs].unsqueeze(1).to_broadcast([80, 2, 64]), h2(blk[0:80, :]), ALU.mult, [gck, "cs"], [K_("A")])
                TT_("pool", h2(Bm[0:80, :]), BETA[0:80, cs].unsqueeze(1).to_broadcast([80, 2, 64]), h2(blk[0:80, :]), ALU.mult, ["beta", "cs"], [K_("Bm")])
                TS("pool", SELGL[0:80, :], sel[0:80, :], GC[0:80, c0 + last:c0 + last + 1], None, ALU.mult, ALU.bypass, [gck, "cs"], [K_("selgl")])
                MMS([(X[pr_, i * 64:(i + 1) * 64], VT[pr_, i, cs], IDB2[pr_, :]) for (gp, pr_, p16) in GP for i in range(2)], ["vt", "idb2"], [Xk])
                MMS([(Yb[pr_, i * 64:(i + 1) * 64], KT[pr_, i, cs], IDB2[pr_, :]) for (gp, pr_, p16) in GP for i in range(2)], ["kt", "idb2"], [Yk])
                yield
                CP("act", VTOK[:], X[:, 0:HW_], [Xk], [K_("vtok")])
                CP("dve", KTOK[:], Yb[:, 0:HW_], [Yk], [K_("ktok")])
                yield
                def fE(e):
                    ins = None
                    for (gp, pr_, p16) in GP:
                        e.matmul(X[pr_, 0:HW_], lhsT=GC[p16, cs], rhs=blk[p16, :], start=True, stop=False)
                        e.matmul(X[pr_, 0:HW_], lhsT=NEGONES16[p16, :], rhs=A[p16, :], start=False, stop=True)
                        e.matmul(X[pr_, HW_:HW_ + 2], lhsT=BETA[p16, cs], rhs=sel[p16, :], start=True, stop=True)
                        e.matmul(X[pr_, HW_ + 2:HW_ + 4], lhsT=GC[p16, cs], rhs=sel[p16, :], start=True, stop=True)
                        ins = e.matmul(X[pr_, HW_ + 4:HW_ + 6], lhsT=ONES16[p16, :], rhs=SELGL[p16, :], start=True, stop=True)
                    return ins
                S.op("pe", fE, [gck, "beta", "cs", "ones16", K_("A"), K_("selgl")], [Xk])
                MMS([(Yb[pr_, 0:HW_], ONES16[p16, :], Bm[p16, :]) for (gp, pr_, p16) in GP] +
                    [(Yb[pr_, HW_:2 * HW_], ONES16[p16, :], A[p16, :]) for (gp, pr_, p16) in GP], ["ones16", K_("A"), K_("Bm")], [Yk])
                yield
                ACT(EGT[:], Yb[:, HW_:2 * HW_], AF.Exp, [Yk], [K_("EGT")])
                TT_("dve", h2(KBc[:]), KT[:, :, cs], h2(Yb[:, 0:HW_]), ALU.mult, ["kt", Yk], [K_("kbc")])
                ACT(tks[:, 2:6], X[:, HW_ + 2:HW_ + 6], AF.Exp, [Xk], [K_("tks")])
                ACT(tks[:, 6:8], h2(X[:, 0:HW_])[:, :, last], AF.Exp, [Xk], [K_("tks")], scale=-1.0)
                CP("dve", tks[:, 0:2], X[:, HW_:HW_ + 2], [Xk], [K_("tks")])
                yield
                TT_("dve", A[:], X[:, 0:HW_], NEGM, ALU.add, [Xk, "cd"], [K_("A")])
                STT(Bm[:], X[:, 0:HW_], -1.0, NEGMT, ALU.mult, ALU.add, [Xk, "cd"], [K_("Bm")])
                yield
                ACT(A[:], A[:], AF.Exp, [K_("A")], [K_("A")])
                ACT(Bm[:], Bm[:], AF.Exp, [K_("Bm")], [K_("Bm")])
                TT_("pool", h2(QGc[:]), QT[:, :, cs], h2(EGT[:]), ALU.mult, ["qt", K_("EGT")], [K_("qgc")])
                TT_("pool", h2(KDEC[:]), h2(KTOK[:]), bc(tks[:, 6:8]), ALU.mult, [K_("ktok"), K_("tks")], [K_("kdec")])
                yield
                TT_("pool", tks[:, 8:10], tks[:, 0:2], tks[:, 2:4], ALU.mult, [K_("tks")], [K_("tks")])
                kb3 = h2(KBc[:])
                MMS([(Yb[pr_, i * 64:(i + 1) * 64], kb3[pr_, i, :], KT[pr_, i, cs]) for (gp, pr_, p16) in GP for i in range(2)] +
                    [(Yb[pr_, HW_ + i * 64:HW_ + (i + 1) * 64], KT[pr_, i, cs], kb3[pr_, i, :]) for (gp, pr_, p16) in GP for i in range(2)],
                    [K_("kbc"), "kt"], [Yk])
                MMS([(X[pr_, i * 64:(i + 1) * 64], KT[pr_, i, cs], QT[pr_, i, cs]) for (gp, pr_, p16) in GP for i in range(2)], ["kt", "qt"], [Xk])
                yield
                qx4 = QX[:].rearrange("p (i two x) -> p i two x", i=2, two=2)
                qx3 = QX[:].rearrange("p (i y) -> p i y", i=2)
                qt3 = h2(QT_[:])
                STT(qt3, h2(Yb[:, 0:HW_]), -1.0, h2(A[:]), ALU.mult, ALU.mult, [Yk, K_("A")], [K_("qtn")])
                STT(qx4[:, :, 0, :], h2(Yb[:, HW_:2 * HW_]), -1.0, h2(Bm[:]), ALU.mult, ALU.mult, [Yk, K_("Bm")], [K_("qx")])
                yield
                TT_("pool", Bm[:], Bm[:], IDENT2, ALU.add, [K_("Bm"), "cd"], [K_("Bm")])
                TT_("dve", INTR[:], X[:, 0:HW_], Bm[:], ALU.mult, [Xk, K_("Bm")], [K_("intr")])
                TT_("pool", h2(A[:]), h2(VTOK[:]), bc(tks[:, 0:2]), ALU.mult, [K_("vtok"), K_("tks"), K_("A")], [K_("A")])
                yield
                for lev in range(5):
                    if lev == 0:
                        MMS([(X[pr_, i * 64:(i + 1) * 64], qt3[pr_, i, :], qx4[pr_, i, 0, :]) for (gp, pr_, p16) in GP for i in range(2)],
                            [K_("qtn"), K_("qx")], [Xk])
                    elif lev == 4:
                        MMS([(X[pr_, i * 64:(i + 1) * 64], qt3[pr_, i, :], qx4[pr_, i, 1, :]) for (gp, pr_, p16) in GP for i in range(2)],
                            [K_("qtn"), K_("qx")], [Xk])
                    else:
                        MMS([(X[pr_, i * 128:(i + 1) * 128], qt3[pr_, i, :], qx3[pr_, i, :]) for (gp, pr_, p16) in GP for i in range(2)],
                            [K_("qtn"), K_("qx")], [Xk])
                    MMS([(Yb[pr_, i * 64:(i + 1) * 64], qx4[pr_, i, 0, :], qt3[pr_, i, :]) for (gp, pr_, p16) in GP for i in range(2)],
                        [K_("qtn"), K_("qx")], [Yk])
                    yield
                    x4 = X[:, 0:2 * HW_].rearrange("p (i two x) -> p i two x", i=2, two=2)
                    if lev == 0:
                        TT_("pool", qx4[:, :, 1, :], qx4[:, :, 0, :], h2(IDENT2), ALU.add, [K_("qx"), "cd"], [K_("qx")])
                        CP("act", qx4[:, :, 0, :], h2(X[:, 0:HW_]), [Xk], [K_("qx")])
                    elif lev == 4:
                        TT_("dve", qx4[:, :, 1, :], h2(X[:, 0:HW_]), qx4[:, :, 1, :], ALU.add, [Xk, K_("qx")], [K_("qx")])
                    else:
                        CP("act", qx4[:, :, 0, :], x4[:, :, 0, :], [Xk], [K_("qx")])
                        TT_("dve", qx4[:, :, 1, :], x4[:, :, 1, :], qx4[:, :, 1, :], ALU.add, [Xk, K_("qx")], [K_("qx")])
                    CP("act", QT_[:], Yb[:, 0:HW_], [Yk], [K_("qtn")])
                    yield
                MMS([(X[pr_, i * 64:(i + 1) * 64], qt3[pr_, i, :], qx4[pr_, i, 1, :]) for (gp, pr_, p16) in GP for i in range(2)],
                    [K_("qtn"), K_("qx")], [Xk])
                yield
                TT_("dve", h2(EGT[:]), h2(X[:, 0:HW_]), qx4[:, :, 1, :], ALU.add, [Xk, K_("qx"), K_("EGT")], [K_("EGT")])
                yield

            def partB(k, si, d, ck):
                sl_ = SL[k]
                X, Xk = PSB[2 * k], "ps%d" % (2 * k)
                Yb, Yk = PSB[2 * k + 1], "ps%d" % (2 * k + 1)
                chunk, c0, cs = geom(si, d, ck)
                K_ = lambda n_: "%s_%d" % (n_, k)
                BV, QGc, INTR, KDEC = sl_["A"], sl_["QGc"], sl_["INTR"], sl_["KDEC"]
                tks, RT, VNB, TTB, SD = sl_["TKS"], sl_["RT"], sl_["VNB"], sl_["EGT"], sl_["Bm"]
                Sf, Sb_ = SS[(si % 2, d)]
                sk, sbk = ("S", si % 2, d), ("Sb", si % 2, d)
                Sb3, Sf3 = h2(Sb_[:]), h2(Sf[:])
                MMS([(X[pr_, i * 64:(i + 1) * 64], KT[pr_, i, cs], Sb3[pr_, i, :]) for (gp, pr_, p16) in GP for i in range(2)], ["kt", sbk], [Xk])
                TT_("pool", h2(SD[:]), Sf3, bc(tks[:, 4:6]), ALU.mult, [sk, K_("tks"), K_("Bm")], [K_("Bm")])
                yield
                TT_("dve", h2(RT[:]), h2(X[:, 0:HW_]), bc(tks[:, 8:10]), ALU.mult, [Xk, K_("tks")], [K_("rt")])
                yield
                TT_("dve", RT[:], BV[:], RT[:], ALU.subtract, [K_("A"), K_("rt")], [K_("rt")])
                yield
                tt3, r3 = h2(TTB[:]), h2(RT[:])
                MMS([(Yb[pr_, i * 64:(i + 1) * 64], tt3[pr_, i, :], r3[pr_, i, :]) for (gp, pr_, p16) in GP for i in range(2)], [K_("EGT"), K_("rt")], [Yk])
                yield
                CP("act", VNB[:], Yb[:, 0:HW_], [Yk], [K_("vnb")])
                yield
                vn3, qg3, in3, kd3 = h2(VNB[:]), h2(QGc[:]), h2(INTR[:]), h2(KDEC[:])
                MMS([(Yb[pr_, i * 64:(i + 1) * 64], kd3[pr_, i, :], vn3[pr_, i, :]) for (gp, pr_, p16) in GP for i in range(2)], [K_("kdec"), K_("vnb")], [Yk])
                def fo(e):
                    ins = None
                    for (gp, pr_, p16) in GP:
                        for i in range(2):
                            e.matmul(X[pr_, i * 64:(i + 1) * 64], lhsT=Sb3[pr_, i, :], rhs=qg3[pr_, i, :], start=True, stop=False)
                            ins = e.matmul(X[pr_, i * 64:(i + 1) * 64], lhsT=vn3[pr_, i, :], rhs=in3[pr_, i, :], start=False, stop=True)
                    return ins
                S.op("pe", fo, [sbk, K_("qgc"), K_("vnb"), K_("intr")], [Xk])
                yield
                TT_("dve", Sb_[:], Yb[:, 0:HW_], SD[:], ALU.add, [Yk, K_("Bm")], [sbk])
                TT_("dve", Sf[:], Yb[:, 0:HW_], SD[:], ALU.add, [Yk, K_("Bm")], [sk])
                ok_ = ("oacc", si, chunk)
                if ok_ not in oacc_written:
                    oacc_written.add(ok_)
                    CP("act", OACC[:, :, cs], h2(X[:, 0:HW_]), [Xk], [ok_])
                else:
                    TT_("dve", OACC[:, :, cs], h2(X[:, 0:HW_]), OACC[:, :, cs], ALU.add, [Xk, ok_], [ok_])
                yield

            def run_rr(gens):
                gens = list(gens)
                while gens:
                    nxt = []
                    for g_ in gens:
                        try:
                            next(g_)
                            nxt.append(g_)
                        except StopIteration:
                            pass
                    gens = nxt

            for sp_ in range(0, len(seqs), 2):
                sis = [si for si in (sp_, sp_ + 1) if si < len(seqs)]
                for si in sis:
                    init_state(si)
                nck = seqs[sis[0]][1] if DN_STAGE > 3 else 1
                steps = [(si, d, ck) for ck in range(nck) for si in sis for d in range(2)]
                for w0 in range(0, len(steps), KS):
                    win = steps[w0:w0 + KS]
                    run_rr([partA(k, *st) for k, st in enumerate(win)])
                    pending = list(enumerate(win))
                    while pending:
                        seen, rnd, rest = set(), [], []
                        for k, st in pending:
                            ch = (st[0], st[1])
                            if ch in seen:
                                rest.append((k, st))
                            else:
                                seen.add(ch)
                                rnd.append((k, st))
                        run_rr([partB(k, *st) for k, st in rnd])
                        pending = rest
                if g == 0:
                    for si in sis:
                        for d in range(2):
                            S.dma("sp", o_dn[si, j, d, hh], SS[(si % 2, d)][0][:], reads=[("S", si % 2, d)], writes=["o_dn"])
            if DN_STAGE < 9:
                return
            allo = [("oacc", si, c) for si in range(len(seqs)) for c in range(seqs[si][1])]
            TO = sb([128, TT], F32, ctx)
            for i in range(2):
                for tt in range(2):
                    sl = slice(tt * TT, (tt + 1) * TT)
                    ACT(SQ[:], OACC[:, i, sl], AF.Square, allo, ["dsq0"])
                    ps, pk = newps()
                    MM(ps[:], [(ONESBD[:], SQ[:])], ["dsq0", "onesbd"], [pk])
                    ACT(RS[:], ps[:], AF.Ln, [pk], ["drs0"], bias=EPS, scale=1.0 / 64)
                    ACT(RS[:], RS[:], AF.Exp, ["drs0"], ["drs0"], scale=-0.5)
                    STT(TO[:], OACC[:, i, sl], DNG2, RS[:], ALU.mult, ALU.mult, allo + ["drs0", "p128"], ["to"])
                    TT_("dve", OBt[:, hh * 2 + i, sl], TO[:], GATE[:, i, sl], ALU.mult, ["to", "gate"], ["ob"])

        def attention(l, g, ctx):
            j = l // 2
            P64 = sb([64, 24 * 5 + 11], F32, ctx)
            S.dma("sp", P64[:], abp64[j], writes=["p64b"])
            QG, KG = P64[:, 121:122], P64[:, 122:123]
            ESINK = sb([64, 8], F32, ctx)
            ACT(ESINK[:], P64[:, 123:131], AF.Exp, ["p64b"], ["esink"])
            QB = sb([64, 8, NT], BF16, ctx)
            KB = sb([64, 2, NT], BF16, ctx)
            KN = sb([64, 2, NT], F32, ctx)
            VB = sb([128, 8, 128], BF16, ctx)
            VF = sb([128, 8, 128], F32, ctx)
            SQa = [sb([128, TT], BF16, ctx) for _ in range(2)]
            RSa = [sb([128, TT], F32, ctx) for _ in range(2)]
            QN = [sb([128, TT], F32, ctx) for _ in range(2)]
            QNBa = [sb([128, TT], BF16, ctx) for _ in range(2)]
            T1a = [sb([128, TT], F32, ctx) for _ in range(2)]
            T2a = [sb([128, TT], F32, ctx) for _ in range(2)]
            PERMB = sb([128, 128], BF16, ctx)
            G128 = sb([128, 2], F32, ctx)
            if g == 1:
                CSN = sb([128, 2048], F32, ctx)
                for hf in range(2):
                    S.dma("sp", CSN[hf * 64:(hf + 1) * 64, :], cossin[:, :], writes=["csn"])
                COS = CSN[:, 0:1024]
                SIN = CSN[:, 1024:2048]
            MEMSET("dve", PERMB[:], 0.0, ["permb"])
            CP("dve", PERMB[0:64, 0:64], PERMT, ["c64", "permb"], ["permb"])
            CP("dve", PERMB[64:128, 64:128], PERMT, ["c64", "permb"], ["permb"])
            for hf in range(2):
                S.dma("sp", G128[hf * 64:(hf + 1) * 64, :], abp64[j][:, 121:123], writes=["g128"])
            W4, w4k = wload(abin[j, 4])
            W5, w5k = wload(abin[j, 5])
            W4v = W4[:].rearrange("p (k m) -> p k m", k=8)
            W5v = W5[:].rearrange("p (k m) -> p k m", k=8)
            n = 0
            for pp in range(5):
                sls = [slice(tt * TT, (tt + 1) * TT) for tt in range(2)]
                if pp < 4:
                    gn = G128[:, 0:1]
                    dkey = "qb"
                    dsts = [[QB[:, 2 * pp, sls[tt]], QB[:, 2 * pp + 1, sls[tt]]] for tt in range(2)]
                else:
                    gn = G128[:, 1:2]
                    dkey = "kb"
                    dsts = [[KB[:, 0, sls[tt]], KB[:, 1, sls[tt]]] for tt in range(2)]
                pss, pss2, pss3 = [], [], []
                for tt in range(2):
                    ps, pk = newps()
                    pss.append((ps, pk))
                    if pp < 4:
                        MM(ps[:], [(W4v[:, kc, pp * 128:(pp + 1) * 128], HT[:, kc, sls[tt]]) for kc in range(8)], [w4k, "ht"], [pk])
                    else:
                        MM(ps[:], [(W5v[:, kc, 32:160], HT[:, kc, sls[tt]]) for kc in range(8)], [w5k, "ht"], [pk])
                for tt in range(2):
                    ACT(SQa[tt][:], pss[tt][0][:], AF.Square, [pss[tt][1]], ["asq%d" % tt])
                for tt in range(2):
                    ps2, pk2 = newps()
                    pss2.append((ps2, pk2))
                    MM(ps2[:], [(ONESBD[:], SQa[tt][:])], ["asq%d" % tt, "onesbd"], [pk2])
                for tt in range(2):
                    ACT(RSa[tt][:], pss2[tt][0][:], AF.Ln, [pss2[tt][1]], ["ars%d" % tt], bias=EPS, scale=1.0 / 64)
                for tt in range(2):
                    ACT(RSa[tt][:], RSa[tt][:], AF.Exp, ["ars%d" % tt], ["ars%d" % tt], scale=-0.5)
                for tt in range(2):
                    STT(QN[tt][:], pss[tt][0][:], gn, RSa[tt][:], ALU.mult, ALU.mult, [pss[tt][1], "ars%d" % tt, "g128"], ["qn%d" % tt])
                if g == 0:
                    for tt in range(2):
                        for hf in range(2):
                            CP("act", dsts[tt][hf], QN[tt][hf * 64:(hf + 1) * 64, :], ["qn%d" % tt], [dkey])
                            if pp == 4:
                                CP("act", KN[:, hf, sls[tt]], QN[tt][hf * 64:(hf + 1) * 64, :], ["qn%d" % tt], ["kn"])
                else:
                    for tt in range(2):
                        CP("act", QNBa[tt][:], QN[tt][:], ["qn%d" % tt], ["qnb%d" % tt])
                    for tt in range(2):
                        ps3, pk3 = newps()
                        pss3.append((ps3, pk3))
                        MM(ps3[:], [(PERMB[:], QNBa[tt][:])], ["permb", "qnb%d" % tt], [pk3])
                    for tt in range(2):
                        TT_("dve", T1a[tt][:], QN[tt][:], COS[:, sls[tt]], ALU.mult, ["qn%d" % tt, "csn"], ["at1%d" % tt])
                    for tt in range(2):
                        TT_("dve", T2a[tt][:], pss3[tt][0][:], SIN[:, sls[tt]], ALU.mult, [pss3[tt][1], "csn"], ["at2%d" % tt])
                    for tt in range(2):
                        for hf in range(2):
                            TT_("dve", dsts[tt][hf], T1a[tt][hf * 64:(hf + 1) * 64, :], T2a[tt][hf * 64:(hf + 1) * 64, :], ALU.add,
                                ["at1%d" % tt, "at2%d" % tt], [dkey])
            if g == 0:
                S.dma("sp", o_k[j], KN[:], reads=["kn"], writes=["o_k"])
            for tb in range(8):
                ps, pk = newps()
                MM(ps[:, 0:128], [(HT[:, kc, tb * 128:(tb + 1) * 128], W5v[:, kc, 160:288]) for kc in range(8)], [w5k, "ht"], [pk])
                CP("act", VB[:, tb, :], ps[:, 0:128], [pk], ["vb"])
                if g == 0:
                    CP("dve", VF[:, tb, :], ps[:, 0:128], [pk], ["vf"])
            if g == 0:
                S.dma("sp", o_v[j], VF[:], reads=["vf"], writes=["o_v"])
            if g == 1:
                KCF = sb([64, 2, 256], F32, ctx)
                KC = sb([64, 2, 256], BF16, ctx)
                VCF = sb([128, 2, 128], F32, ctx)
                VC = sb([128, 2, 128], BF16, ctx)
                S.dma("sp", KCF[:], kcT[j], writes=["kcf"])
                S.dma("sp", VCF[:], vc[j], writes=["vcf"])
                CP("dve", KC[:], KCF[:], ["kcf"], ["kc"])
                CP("dve", VC[:], VCF[:], ["vcf"], ["vcb"])
            PT = [sb([128, 5, 512], BF16, ctx) for _ in range(2)]
            DENs = [sb([64, 512], F32, ctx) for _ in range(2)]
            items = [(qb, kv) for qb in range(8) for kv in range(2)]
            kbls = {}
            def head(it):
                qb, kv = items[it]
                qs = slice(qb * 128, (qb + 1) * 128)
                if g == 0:
                    s_ = qb // 2
                    kbl = [("lat", 2 * s_, None), ("lat", 2 * s_ + 1, None)]
                else:
                    kbl = []
                    if qb > 0:
                        kbl.append(("lat", qb - 1, MASK_LO))
                    kbl.append(("lat", qb, None))
                    if qb < 7:
                        kbl.append(("lat", qb + 1, MASK_UP))
                    kbl += [("ctx", 0, None), ("ctx", 1, None)]
                kbls[it] = kbl
                pt = PT[it % 2]; ptk = "pt%d" % (it % 2)
                for bi, (kind, kb_, msk) in enumerate(kbl):
                    ps, pk = newps()
                    if kind == "lat":
                        lh, lk = KB[:, kv, kb_ * 128:(kb_ + 1) * 128], "kb"
                    else:
                        lh, lk = KC[:, kv, kb_ * 128:(kb_ + 1) * 128], "kc"
                    MM(ps[:].rearrange("p (h q) -> p h q", h=4), [(lh, QB[:, 4 * kv:4 * kv + 4, qs])], [lk, "qb"], [pk])
                    ACT(pt[:, bi, :], ps[:], AF.Exp, [pk], [(ptk, bi)], scale=0.125)
                    if msk is not None:
                        TT_("dve", pt[:, bi, :].rearrange("p (h q) -> p h q", h=4), pt[:, bi, :].rearrange("p (h q) -> p h q", h=4),
                            msk.unsqueeze(1).to_broadcast([128, 4, 128]), ALU.mult, [(ptk, bi), "c128"], [(ptk, bi)])
            def tail(it):
                qb, kv = items[it]
                qs = slice(qb * 128, (qb + 1) * 128)
                kbl = kbls[it]
                pt = PT[it % 2]; ptk = "pt%d" % (it % 2)
                DEN = DENs[it % 2]; dnk = "den%d" % (it % 2)
                nb = len(kbl)
                pv, pvk = newps()
                prs = []
                for bi, (kind, kb_, msk) in enumerate(kbl):
                    if kind == "lat":
                        prs.append((VB[:, kb_, kv * 64:(kv + 1) * 64], pt[:, bi, :]))
                    else:
                        prs.append((VC[:, kb_, kv * 64:(kv + 1) * 64], pt[:, bi, :]))
                MM(pv[0:64, :], prs, ["vb", "vcb"] + [(ptk, bi) for bi in range(nb)], [pvk])
                pd, pdk = newps()
                MM(pd[0:64, :], [(ONESB[:, 0:64], pt[:, bi, :]) for bi in range(nb)], ["onesb"] + [(ptk, bi) for bi in range(nb)], [pdk])
                TT_("dve", DEN[:].rearrange("p (h q) -> p h q", h=4), pd[0:64, :].rearrange("p (h q) -> p h q", h=4),
                    ESINK[:, 4 * kv:4 * kv + 4].unsqueeze(2).to_broadcast([64, 4, 128]), ALU.add, [pdk, "esink"], [dnk])
                ACT(DEN[:], DEN[:], AF.Ln, [dnk], [dnk])
                ACT(DEN[:], DEN[:], AF.Exp, [dnk], [dnk], scale=-1.0)
                for hi in range(4):
                    TT_("dve", OBH(8 + 4 * kv + hi, qs), pv[0:64, hi * 128:(hi + 1) * 128], DEN[:, hi * 128:(hi + 1) * 128], ALU.mult,
                        [pvk, dnk], ["ob"])
            head(0)
            for it in range(len(items)):
                if it + 1 < len(items):
                    head(it + 1)
                tail(it)

        def out_proj_ab(l, g):
            j = l // 2
            for mb in range(2):
                W, wk = wload(about[j, mb])
                Wv = W[:].rearrange("p (k m) -> p k m", k=8)
                for mi in range(4):
                    m = mb * 4 + mi
                    for tt in range(2):
                        sl = slice(tt * TT, (tt + 1) * TT)
                        ps, pk = newps()
                        MM(ps[:], [(Wv[:, kc, mi * 128:(mi + 1) * 128], OB128[:, kc, sl]) for kc in range(8)], [wk, "ob"], [pk])
                        STT(Y[:, m, sl], ps[:], mod(l, g, 2)[:, m:m + 1], Y[:, m, sl], ALU.mult, ALU.add, [pk, "y", "mod%d" % l], ["y"])

        ada_done = set()
        for g in GROUPS:
            for c in range(8):
                S.dma("sp", Y[:, c, :], xT[g, c * 128:(c + 1) * 128, :], writes=["y"])
            for l in range(DEPTH):
                if l not in ada_done:
                    adaln(l)
                    ada_done.add(l)
                with ExitStack() as ctx:
                    modulate(l, g, 0, ctx)
                S.barrier()
                if l % 2 == 0:
                    if not (DO_A and DO_B):
                        MEMSET("dve", OBt[:], 0.0, ["ob"])
                    if DO_A:
                        with ExitStack() as ctx:
                            deltanet(l, g, ctx)
                        S.barrier()
                    if DO_B:
                        with ExitStack() as ctx:
                            attention(l, g, ctx)
                        S.barrier()
                    out_proj_ab(l, g)
                else:
                    if DO_C:
                        with ExitStack() as ctx:
                            mixer_c(l, g, ctx)
                        S.barrier()
                with ExitStack() as ctx:
                    modulate(l, g, 1, ctx)
                S.barrier()
                if DO_MLP:
                    with ExitStack() as ctx:
                        mlp(l, g, ctx)
                    S.barrier()
            for c in range(8):
                S.dma("sp", yT[g, c * 128:(c + 1) * 128, :], Y[:, c, :], reads=["y"], writes=["yT"])
        S.barrier()
        with nc.Block() as block:
            S.emit(block)
    return nc


_CACHE = {}


def _consts():
    c64 = np.zeros((64, 832), np.float32)
    eye = np.eye(64, dtype=np.float32)
    r = np.arange(64)[:, None]; x = np.arange(64)[None, :]
    lo = np.where(r > x, 0.0, -30000.0).astype(np.float32)
    up = np.where(r < x, 0.0, -30000.0).astype(np.float32)
    for i in range(4):
        c64[:, i * 64:(i + 1) * 64] = eye
        c64[:, 256 + i * 64:256 + (i + 1) * 64] = lo
        c64[:, 512 + i * 64:512 + (i + 1) * 64] = up
    P = np.zeros((64, 64), np.float32)
    for m in range(64):
        if m % 32 < 16:
            P[m, m + 16] = -1.0
        else:
            P[m, m - 16] = 1.0
    c64[:, 768:832] = P.T
    t = np.arange(1024, dtype=np.float32)
    inv = (10000.0 ** (-np.arange(0, 32, 2, dtype=np.float32) / 32)).astype(np.float32)
    ang_r = (np.floor(t / 64)[:, None] * inv).astype(np.float32)
    ang_c = ((t % 64)[:, None] * inv).astype(np.float32)
    ang = np.zeros((64, 1024), np.float32)
    for p in range(64):
        ang[p] = (ang_r if p < 32 else ang_c)[:, p % 16]
    cossin = np.concatenate([np.cos(ang), np.sin(ang)], axis=1).astype(np.float32)
    c16 = np.zeros((16, 1040), np.float32)
    cmask = np.zeros((16, 2048), np.float32)
    for d in range(2):
        for hh in range(2):
            for i in range(4):
                k = d * 8 + hh * 4 + i
                o = (d * 2 + hh) * 256
                c16[k, o + i * 64:o + (i + 1) * 64] = 1.0
                c16[k, 1024 + (d * 2 + hh) * 4 + i] = 1.0
    cm = np.ones(1024, np.float32); cm[0::64] = 0.0
    cmask[:, 0:1024] = cm
    cm = np.ones(1024, np.float32); cm[63::64] = 0.0
    cmask[:, 1024:2048] = cm
    c128 = np.zeros((128, 256), np.float32)
    k = np.arange(128)[:, None]; q = np.arange(128)[None, :]
    c128[:, 0:128] = (q <= k)
    c128[:, 128:256] = (k <= q)
    return c64, c16, c128, cossin, cmask


def _fm(v):
    return np.ascontiguousarray(np.swapaxes(v.reshape(v.shape[:-1] + (8, 128)), -1, -2))


def _wt(w, kc, mcols):
    K, M = w.shape
    a = w.reshape(kc, 128, M // mcols, mcols)
    return np.ascontiguousarray(a.transpose(2, 1, 0, 3)).reshape(M // mcols, 128, kc * mcols)


def kernel(x_prompt, x_sample, state_delta, cache_k, cache_v, state_lru, c, c_ctx,
           ada_w, ada_b, norm1_g, norm2_g, ff_w1, ff_w2,
           ab_w_in, ab_conv_w, ab_conv_b, dn_a_log, dn_dt_bias, dn_norm_g,
           attn_q_norm_g, attn_k_norm_g, attn_sink, ab_w_out,
           c_w_in, c_conv_w, c_conv_b, lru_w_a, lru_b_a, lru_w_x, lru_b_x, lru_lambda, c_w_out):
    f = lambda a: np.asarray(a, dtype=np.float32)
    x_prompt, x_sample = f(x_prompt), f(x_sample)
    NE = (DEPTH + 1) // 2
    NO = max(1, DEPTH // 2)
    NL = DEPTH
    c64, c16, c128, cossin, cmask = _consts()
    shared = {"c64": c64, "c16": c16, "c128": c128, "cossin": cossin, "cmask": cmask}
    shared["ada_w"] = np.stack([_wt(f(ada_w[l]), 8, 512) for l in range(NL)])
    shared["ada_b"] = np.stack([np.ascontiguousarray(f(ada_b[l]).reshape(48, 128).T) for l in range(NL)])
    shared["ng"] = np.stack([np.concatenate([_fm(f(norm1_g[l])), _fm(f(norm2_g[l]))], axis=1) for l in range(NL)])
    shared["w1"] = np.stack([_wt(f(ff_w1[l]), 8, 512) for l in range(NL)])
    shared["w2"] = np.stack([_wt(f(ff_w2[l]), 32, 128) for l in range(NL)])
    abin = []
    for j in range(NE):
        w = f(ab_w_in[j])
        pad = np.zeros((1024, 6 * 512), np.float32)
        pad[:, 0:1536] = w[:, 0:1536]
        pad[:, 1536:2048] = w[:, 1536:2048]
        pad[:, 2048:2560] = w[:, 2080:2592]
        pad[:, 2560:2592] = w[:, 2048:2080]
        pad[:, 2592:2720] = w[:, 2592:2720]
        pad[:, 2720:2848] = w[:, 2720:2848]
        abin.append(_wt(pad, 8, 512))
    shared["abin"] = np.stack(abin)
    shared["about"] = np.stack([_wt(f(ab_w_out[j]), 8, 512) for j in range(NE)])
    p64 = np.zeros((NE, 64, 24 * 5 + 11), np.float32)
    for j in range(NE):
        cw = f(ab_conv_w[j]).reshape(4, 24, 64)
        p64[j, :, 0:96] = cw.transpose(2, 1, 0).reshape(64, 96)
        p64[j, :, 96:120] = f(ab_conv_b[j]).reshape(24, 64).T
        p64[j, :, 120] = f(dn_norm_g[j])
        p64[j, :, 121] = f(attn_q_norm_g[j])
        p64[j, :, 122] = f(attn_k_norm_g[j])
        p64[j, :, 123:131] = f(attn_sink[j])[None, :]
    shared["abp64"] = p64
    p128 = np.zeros((NE, 128, 61), np.float32)
    for j in range(NE):
        p128[j, :, 0:48] = f(ab_conv_w[j]).reshape(4, 12, 128).transpose(2, 1, 0).reshape(128, 48)
        p128[j, :, 48:60] = f(ab_conv_b[j]).reshape(12, 128).T
        p128[j, :, 60] = np.tile(f(dn_norm_g[j]), 2)
    shared["abp128"] = p128
    cdm = np.zeros((128, 384), np.float32)
    r_ = (np.arange(128) % 64)[:, None]; x_ = np.arange(64)[None, :]
    for i in range(2):
        cdm[:, i * 64:(i + 1) * 64] = (r_ == x_)
        cdm[:, 128 + i * 64:128 + (i + 1) * 64] = np.where(r_ > x_, 0.0, -30000.0)
        cdm[:, 256 + i * 64:256 + (i + 1) * 64] = np.where(r_ < x_, 0.0, -30000.0)
    shared["cd"] = cdm
    csm = np.zeros((128, 520), np.float32)
    for d in range(2):
        for hh_ in range(2):
            for gp in range(2):
                for ii in range(2):
                    kk = d * 8 + hh_ * 4 + 2 * ii + gp
                    o = (d * 2 + hh_) * 128
                    csm[gp * 64 + kk, o + ii * 64:o + (ii + 1) * 64] = 1.0
                    csm[gp * 64 + kk, 512 + (d * 2 + hh_) * 2 + ii] = 1.0
    shared["csel"] = csm
    p16 = np.zeros((NE, 16, 2), np.float32)
    for j in range(NE):
        p16[j, :, 0] = f(dn_a_log[j]).reshape(16)
        p16[j, :, 1] = f(dn_dt_bias[j]).reshape(16)
    shared["abp16"] = p16
    shared["cin"] = np.stack([_wt(f(c_w_in[j]), 8, 512) for j in range(NO)])
    shared["cout"] = np.stack([_wt(f(c_w_out[j]), 8, 512) for j in range(NO)])
    cbd = np.zeros((NO, 4, 128, 8, 128), np.float32)
    for j in range(NO):
        for gi, wsrc in enumerate((lru_w_a, lru_w_x)):
            for d in range(2):
                w = f(wsrc[j][d])
                for cc in range(8):
                    for s_ in range(2):
                        cbd[j, gi * 2 + d, s_ * 64:(s_ + 1) * 64, cc, s_ * 64:(s_ + 1) * 64] = w[cc * 2 + s_]
    shared["cbd"] = cbd.reshape(NO, 4, 128, 8 * 128)
    cpp = np.zeros((NO, 128, 8, 11), np.float32)
    for j in range(NO):
        cpp[j, :, :, 0:4] = np.stack([_fm(f(c_conv_w[j][t])) for t in range(4)], axis=-1)
        cpp[j, :, :, 4] = _fm(f(c_conv_b[j]))
        for d in range(2):
            cpp[j, :, :, 5 + d] = _fm(f(lru_b_a[j][d]))
            cpp[j, :, :, 7 + d] = _fm(f(lru_b_x[j][d]))
            cpp[j, :, :, 9 + d] = _fm(f(lru_lambda[j][d]))
    shared["cp"] = cpp.reshape(NO, 128, 88)
    in_maps = []
    for i in range(8):
        b = i % 4
        m = dict(shared)
        xp = x_prompt[4 * i:4 * i + 4].reshape(1024, 1024)
        m["xT"] = np.ascontiguousarray(np.stack([xp.T, x_sample[b].T]))
        m["cond"] = np.stack([_fm(f(c_ctx)), _fm(f(c[b]))])
        sd = f(state_delta[b])[:NE]
        m["sdelta"] = np.ascontiguousarray(sd.reshape(NE, 2, 2, 2, 2, 64, 64).transpose(0, 1, 2, 4, 5, 3, 6)).reshape(NE, 2, 2, 128, 128)
        ck = f(cache_k[b])[:NE]
        m["kcT"] = np.ascontiguousarray(ck.transpose(0, 3, 2, 1))
        cv = f(cache_v[b])[:NE].reshape(NE, 2, 128, 128)
        m["vc"] = np.ascontiguousarray(cv.transpose(0, 2, 1, 3))
        m["slru"] = _fm(f(state_lru[b])[:NO])
        in_maps.append(m)
    if "nc" not in _CACHE:
        _CACHE["nc"] = build_program()
    res = run_bass_kernel_spmd(_CACHE["nc"], in_maps, core_ids=list(range(8)))
    R = res.results
    y_prompt = np.zeros((32, 256, 1024), np.float32)
    y_sample = np.zeros((4, 1024, 1024), np.float32)
    new_dn = np.zeros((32, NE, 2, 8, 64, 64), np.float32)
    new_k = np.zeros((32, NE, 256, 2, 64), np.float32)
    new_v = np.zeros((32, NE, 256, 2, 64), np.float32)
    new_lru = np.zeros((32, NO, 2, 1024), np.float32)
    for i in range(8):
        r = R[i]
        y_prompt[4 * i:4 * i + 4] = r["yT"][0].T.reshape(4, 256, 1024)
        if i < 4:
            y_sample[i] = r["yT"][1].T
        dn = r["o_dn"].reshape(4, NE, 2, 2, 2, 64, 2, 64)
        new_dn[4 * i:4 * i + 4] = dn.transpose(0, 1, 2, 3, 6, 4, 5, 7).reshape(4, NE, 2, 8, 64, 64)
        ok = r["o_k"].reshape(NE, 64, 2, 4, 256)
        new_k[4 * i:4 * i + 4] = ok.transpose(3, 0, 4, 2, 1)
        ov = r["o_v"].reshape(NE, 128, 4, 2, 2, 64)
        new_v[4 * i:4 * i + 4] = ov.transpose(2, 0, 3, 1, 4, 5).reshape(4, NE, 256, 2, 64)
        ol = r["o_lru"].reshape(NO, 2, 4, 128, 8)
        new_lru[4 * i:4 * i + 4] = ol.transpose(2, 0, 1, 4, 3).reshape(4, NO, 2, 1024)
    return (y_prompt, y_sample, new_dn, new_k, new_v, new_lru)
```
